# Optimizing a Trainium2 kernel written in Bass

```python
import jax, jax.numpy as jnp
from jax import lax
import numpy as np

D_MODEL = 1024
BATCH = 16
SEQ = 2048
DEPTH = 1

D_MIX = D_MODEL
GLA_WIDTH = D_MIX // 2
DIFF_WIDTH = D_MIX - GLA_WIDTH
GLA_HEADS = 4
GLA_DV = GLA_WIDTH // GLA_HEADS
GLA_DK = GLA_DV // 2
GLA_RANK = 16
GLA_GATE_NORM = 16.0
GLA_CHUNK = 64
DIFF_HEADS = 4
DIFF_DV = DIFF_WIDTH // DIFF_HEADS
DIFF_DH = DIFF_DV // 2
Q_BLOCK = 128
EPS = 1e-6

SPLIT_SIZES = (
    GLA_HEADS * GLA_DK,
    GLA_HEADS * GLA_DK,
    GLA_WIDTH,
    GLA_WIDTH,
    GLA_RANK,
    DIFF_WIDTH,
    DIFF_WIDTH,
    DIFF_WIDTH,
    DIFF_WIDTH,
)
D_IN = sum(SPLIT_SIZES)

kernel_name = "hybrid_gla_diffattn_alibi_adaln"


def _rmsnorm(x, gain):
    xf = x.astype(jnp.float32)
    y = xf * lax.rsqrt(jnp.mean(xf * xf, axis=-1, keepdims=True) + EPS)
    return (y * gain.astype(jnp.float32)).astype(x.dtype)


def _gla_chunked(q, k, v, log_a):
    dtype = v.dtype
    B, S, H, DK = q.shape
    DV = v.shape[-1]
    N = S // GLA_CHUNK

    def chunks(t):
        return t.astype(jnp.float32).reshape(B, N, GLA_CHUNK, H, t.shape[-1]).transpose(1, 0, 3, 2, 4)

    q = chunks(q) * (DK ** -0.5)
    k, v, log_a = chunks(k), chunks(v), chunks(log_a)
    b = jnp.cumsum(log_a, axis=3)
    b_last = b[:, :, :, -1:, :]
    q_in = q * jnp.exp(b)
    k_in = k * jnp.exp(-b)
    k_st = k * jnp.exp(b_last - b)
    causal = jnp.tril(jnp.ones((GLA_CHUNK, GLA_CHUNK), dtype=bool))
    scores = jnp.einsum('nbhid,nbhjd->nbhij', q_in, k_in)
    intra = jnp.einsum('nbhij,nbhjv->nbhiv', jnp.where(causal, scores, 0.0), v)

    def step(state, inp):
        q_c, k_c, v_c, dec_c = inp
        out = jnp.einsum('bhik,bhkv->bhiv', q_c, state)
        state = dec_c[:, :, 0, :, None] * state + jnp.einsum('bhjk,bhjv->bhkv', k_c, v_c)
        return state, out

    state0 = jnp.zeros((B, H, DK, DV), jnp.float32)
    _, inter = lax.scan(step, state0, (q_in, k_st, v, jnp.exp(b_last)))
    o = (intra + inter).transpose(1, 0, 3, 2, 4).reshape(B, S, H, DV)
    return o.astype(dtype)


def _diff_attention(q, k, v, lam):
    B, S, H, _, DH = q.shape
    DV = v.shape[-1]
    nb = S // Q_BLOCK
    scale = DH ** -0.5
    slopes = 2.0 ** (-8.0 * jnp.arange(1, H + 1, dtype=jnp.float32) / H)
    kf = k.transpose(0, 2, 3, 1, 4)
    vf = v.transpose(0, 2, 1, 3)
    qb = q.reshape(B, nb, Q_BLOCK, H, 2, DH).transpose(1, 0, 3, 4, 2, 5)
    key_pos = jnp.arange(S)

    def block(args):
        q_blk, blk = args
        q_pos = blk * Q_BLOCK + jnp.arange(Q_BLOCK)
        dist = (q_pos[:, None] - key_pos[None, :]).astype(jnp.float32)
        bias = -slopes[:, None, None] * dist
        s = jnp.einsum('bhiqd,bhikd->bhiqk', q_blk, kf).astype(jnp.float32) * scale + bias[None, :, None]
        s = jnp.where(dist >= 0, s, -jnp.inf)
        p = jax.nn.softmax(s, axis=-1)
        w = p[:, :, 0] - lam * p[:, :, 1]
        return jnp.einsum('bhqk,bhkv->bhqv', w.astype(v.dtype), vf)

    out = lax.map(block, (qb, jnp.arange(nb)))
    return out.transpose(1, 0, 3, 2, 4).reshape(B, S, H, DV)


def setup_inputs(seed: int = 0) -> dict:
    key = jax.random.key(seed)
    ks = jax.random.split(key, 18)

    def nrm(k, shape, s):
        return jax.random.normal(k, shape, jnp.float32) * s

    return {
        "x": nrm(ks[0], (BATCH, SEQ, D_MODEL), 1.0),
        "c": nrm(ks[1], (BATCH, D_MODEL), 1.0),
        "w_ada": nrm(ks[2], (DEPTH, D_MODEL, 3 * D_MODEL), 0.5 * D_MODEL ** -0.5),
        "b_ada": nrm(ks[3], (DEPTH, 3 * D_MODEL), 0.02),
        "norm_gain": 1.0 + nrm(ks[4], (DEPTH, D_MODEL), 0.01),
        "w_in": nrm(ks[5], (DEPTH, D_MODEL, D_IN), D_MODEL ** -0.5),
        "w_gla_gate_up": nrm(ks[6], (DEPTH, GLA_RANK, GLA_HEADS * GLA_DK), GLA_RANK ** -0.5),
        "b_gla_gate": nrm(ks[7], (DEPTH, GLA_HEADS * GLA_DK), 0.1),
        "gla_out_gain": 1.0 + nrm(ks[8], (DEPTH, GLA_WIDTH), 0.01),
        "lambda_q1": nrm(ks[9], (DEPTH, DIFF_DH), 0.1),
        "lambda_k1": nrm(ks[10], (DEPTH, DIFF_DH), 0.1),
        "lambda_q2": nrm(ks[11], (DEPTH, DIFF_DH), 0.1),
        "lambda_k2": nrm(ks[12], (DEPTH, DIFF_DH), 0.1),
        "diff_out_gain": 1.0 + nrm(ks[13], (DEPTH, DIFF_WIDTH), 0.01),
        "w_out": nrm(ks[14], (DEPTH, D_MIX, D_MODEL), D_MIX ** -0.5),
        "final_gain": 1.0 + nrm(ks[15], (D_MODEL,), 0.01),
    }


def reference(x, c, w_ada, b_ada, norm_gain, w_in, w_gla_gate_up, b_gla_gate, gla_out_gain,
              lambda_q1, lambda_k1, lambda_q2, lambda_k2, diff_out_gain, w_out, final_gain):
    B, S, _ = x.shape
    split_points = [int(p) for p in np.cumsum(SPLIT_SIZES)[:-1]]
    for l in range(DEPTH):
        mod = jax.nn.silu(c) @ w_ada[l] + b_ada[l]
        shift, scale, gate = jnp.split(mod, 3, axis=-1)
        h = _rmsnorm(x, norm_gain[l]) * (1.0 + scale[:, None, :]) + shift[:, None, :]

        proj = h @ w_in[l]
        gq, gk, gv, gz, gr, dq, dk, dv, dz = jnp.split(proj, split_points, axis=-1)

        gate_logit = (gr @ w_gla_gate_up[l] + b_gla_gate[l]).astype(jnp.float32)
        log_a = jax.nn.log_sigmoid(gate_logit) / GLA_GATE_NORM
        o_gla = _gla_chunked(gq.reshape(B, S, GLA_HEADS, GLA_DK),
                             gk.reshape(B, S, GLA_HEADS, GLA_DK),
                             gv.reshape(B, S, GLA_HEADS, GLA_DV),
                             log_a.reshape(B, S, GLA_HEADS, GLA_DK))
        o_gla = _rmsnorm(o_gla, gla_out_gain[l].reshape(GLA_HEADS, GLA_DV)).reshape(B, S, GLA_WIDTH)
        o_gla = o_gla * jax.nn.silu(gz)

        lam_init = 0.8 - 0.6 * np.exp(-0.3 * l)
        lam = (jnp.exp(jnp.sum(lambda_q1[l] * lambda_k1[l])) - jnp.exp(jnp.sum(lambda_q2[l] * lambda_k2[l]))
               + lam_init).astype(jnp.float32)
        o_diff = _diff_attention(dq.reshape(B, S, DIFF_HEADS, 2, DIFF_DH),
                                 dk.reshape(B, S, DIFF_HEADS, 2, DIFF_DH),
                                 dv.reshape(B, S, DIFF_HEADS, DIFF_DV), lam)
        o_diff = _rmsnorm(o_diff, diff_out_gain[l].reshape(DIFF_HEADS, DIFF_DV)) * (1.0 - lam_init)
        o_diff = o_diff.reshape(B, S, DIFF_WIDTH) * jax.nn.silu(dz)

        mix = jnp.concatenate([o_gla, o_diff], axis=-1)
        x = x + gate[:, None, :] * (mix @ w_out[l])
    return _rmsnorm(x, final_gain)
```

```python
import math
from contextlib import ExitStack

import numpy as np
import ml_dtypes

import concourse.bass as bass
import concourse.mybir as mybir
from concourse.bass_utils import run_bass_kernel_spmd

F32 = mybir.dt.float32
BF16 = mybir.dt.bfloat16
AF = mybir.ActivationFunctionType
ALU = mybir.AluOpType
AX = mybir.AxisListType

NCORES = 8
D = 1024
S = 2048
NSEQ = 2
NT = S // 128
D_IN = 3600
EPS = 1e-6
LAM_INIT = 0.8 - 0.6 * math.exp(-0.3 * 0)
SLOPES = [2.0 ** (-8.0 * (h + 1) / 4) for h in range(4)]
C_GQ, C_GK, C_GV, C_GZ, C_GR = 0, 256, 512, 1024, 1536
C_DQ, C_DK, C_DV, C_DZ = 1552, 2064, 2576, 3088

EPOCH = 12000


class Op:
    __slots__ = ("eng", "emit", "idx", "deps", "dma_key", "signal", "cnt", "waits")

    def __init__(self, eng, emit, idx, dma_key):
        self.eng = eng
        self.emit = emit
        self.idx = idx
        self.dma_key = dma_key
        self.deps = {}
        self.signal = False
        self.cnt = 0
        self.waits = []


class Prog:
    ENGS = ("pe", "act", "dve", "pool", "sp")

    def __init__(self):
        self.ops = []
        self.last_w = {}
        self.readers = {}
        self.xacc = {}
        self.dma_count = {}
        self.bulk = set()

    def _chan(self, op):
        return ("dma", op.dma_key) if op.dma_key is not None else op.eng

    def add(self, eng, emit, reads=(), writes=(), excl=(), dma_key=None, bulk=False):
        op = Op(eng, emit, len(self.ops), dma_key)
        self.ops.append(op)
        if dma_key is not None:
            self.dma_count[dma_key] = self.dma_count.get(dma_key, 0) + 1
            op.cnt = self.dma_count[dma_key]
            if bulk:
                self.bulk.add(dma_key)
        me = self._chan(op)
        deps = {}
        if any(isinstance(k, tuple) and isinstance(k[0], str) and k[0].startswith("A:") for k in list(reads) + list(writes)):
            reads = list(reads) + ["AR"]

        def dep(i):
            if i is None:
                return
            p = self.ops[i]
            c = self._chan(p)
            if c == "pe" and me == "pe":
                return
            if c == me and op.dma_key is not None and op.dma_key in self.bulk:
                return
            if c not in deps or deps[c] < i:
                deps[c] = i

        for k in reads:
            dep(self.last_w.get(k))
        for k in writes:
            dep(self.last_w.get(k))
            for c, i in self.readers.get(k, {}).items():
                dep(i)
        for k in excl:
            for c, i in self.xacc.get(k, {}).items():
                if c != me:
                    dep(i)
        for k in reads:
            self.readers.setdefault(k, {})[me] = op.idx
        for k in writes:
            self.last_w[k] = op.idx
            self.readers[k] = {}
        for k in excl:
            self.xacc[k] = {me: op.idx}
        op.deps = deps
        for c, i in deps.items():
            self.ops[i].signal = True
        return op

    def finalize(self, nc, stack):
        counters = {}
        for op in self.ops:
            if op.dma_key is None and op.signal:
                counters[op.eng] = counters.get(op.eng, 0) + 1
                op.cnt = counters[op.eng]
        self.sems = {}

        def sem_for(name):
            if name not in self.sems:
                self.sems[name] = stack.enter_context(nc.semaphore("s%d" % len(self.sems)))
            return self.sems[name]

        def target(p):
            if p.dma_key is not None:
                n = self.dma_count[p.dma_key] if p.dma_key in self.bulk else p.cnt
                return ("dma", p.dma_key), 0, 16 * n
            e = (p.cnt - 1) // EPOCH
            return p.eng, e, (p.cnt - 1) % EPOCH + 1

        waited = {e: {} for e in self.ENGS}
        for op in self.ops:
            w = waited[op.eng]
            for c, i in op.deps.items():
                chan, ep, val = target(self.ops[i])
                if w.get(chan, (-1, 0)) >= (ep, val):
                    continue
                w[chan] = (ep, val)
                op.waits.append((sem_for((chan, ep)), val))
        for op in self.ops:
            if op.dma_key is not None:
                op.signal = (sem_for((("dma", op.dma_key), 0)), 16)
            elif op.signal:
                _, ep, _ = target(op)
                op.signal = (sem_for((op.eng, ep)), 1)
            else:
                op.signal = None

    def emit_all(self, nc):
        by_eng = {e: [o for o in self.ops if o.eng == e] for e in self.ENGS}

        def run(engine, ops):
            for op in ops:
                for sem, val in op.waits:
                    engine.wait_ge(sem, val)
                ins = op.emit(engine)
                if op.signal is not None:
                    ins.then_inc(op.signal[0], op.signal[1])

        with nc.Block() as block:
            @block.sync
            def _(e):
                run(e, by_eng["sp"])

            @block.tensor
            def _(e):
                run(e, by_eng["pe"])

            @block.scalar
            def _(e):
                run(e, by_eng["act"])

            @block.vector
            def _(e):
                run(e, by_eng["dve"])

            @block.gpsimd
            def _(e):
                run(e, by_eng["pool"])


def _consts():
    j = np.arange(128)
    c = {}
    c["ident_bf"] = np.eye(128, dtype=np.float32).astype(ml_dtypes.bfloat16)
    c["ident_f"] = np.eye(128, dtype=np.float32)
    c["ones_f"] = np.ones((128, 128), np.float32)
    c["maskT_bf"] = (j[None, :] >= j[:, None]).astype(np.float32).astype(ml_dtypes.bfloat16)
    c["triI_f"] = np.where(j[:, None] <= j[None, :], -1.0 / 16.0, 0.0).astype(np.float32)
    c["triU_f"] = np.where(j[:, None] > j[None, :], -1.0 / 16.0, 0.0).astype(np.float32)
    bt = np.zeros((128, 4, 16), np.float32)
    for h in range(4):
        for d in range(16):
            bt[:, h, d] = SLOPES[h] * (j - 127 - 128 * d)
    c["bias_tab"] = bt.reshape(128, 64)
    return c


CONST_SPECS = [
    ("ident_bf", [128, 128], BF16), ("ident_f", [128, 128], F32), ("ones_f", [128, 128], F32),
    ("maskT_bf", [128, 128], BF16), ("triI_f", [128, 128], F32), ("triU_f", [128, 128], F32),
    ("bias_tab", [128, 64], F32),
]


def build_program(upto="all", dumps=()):
    nc = bass.Bass("TRN2", target_bir_lowering=False)
    pg = Prog()
    stack = ExitStack()
    dram = {}

    def din(name, shape, dt=F32):
        dram[name] = nc.dram_tensor(name, list(shape), dt, kind="ExternalInput").ap()
        return dram[name]

    x_d = din("x", [NSEQ * S, D])
    cT_d = din("cT", [128, 16])
    w_ada_d = din("w_ada", [D, 3 * D])
    b_adaT_d = din("b_adaT", [128, 24])
    ngain_d = din("ngainT", [128, 8])
    w_in_d = din("w_in", [D, D_IN])
    wup_d = din("wup_aug", [17, 256])
    ggain_d = din("ggainT", [128, 4])
    dgain_d = din("dgainT", [128, 4])
    lam_d = din("lam_bc", [128, 256])
    w_out_d = din("w_out", [D, D])
    fgain_d = din("fgain_bc", [128, D])
    cd = {n: din(n, shp, dt) for n, shp, dt in CONST_SPECS}
    y_d = nc.dram_tensor("y", [NSEQ * S, D], F32, kind="ExternalOutput").ap()
    dump_d = {}
    for name, shape in dumps:
        dump_d[name] = nc.dram_tensor("dbg_" + name, list(shape), F32, kind="ExternalOutput").ap()

    def sb(name, shape, dt):
        return stack.enter_context(nc.sbuf_tensor("sb_" + name, list(shape), dt))

    stack.enter_context(nc.allow_low_precision("bf16 matmul operands, fp32 accumulation"))

    w_in_sb = sb("w_in_sb", [128, 8, D_IN], BF16)
    Rbuf = sb("Rbuf", [128, max(8 * S, 16 * D)], BF16)
    hT = Rbuf[:, 0:8 * S].rearrange("p (k t) -> p k t", k=8)
    w_out_sb = Rbuf[:, 0:8 * D].rearrange("p (k n) -> p k n", k=8)
    mixbuf = sb("mixbuf", [128, max(8 * S, 16384)], BF16)
    mixT = mixbuf[:, 0:8 * S].rearrange("p (k t) -> p k t", k=8)
    NXT = 2
    xt = [sb("xt%d" % i, [128, D], F32) for i in range(NXT)]
    xnb_all = sb("xnb_all", [128, 2 * D], BF16)
    xnb = [xnb_all[:, 0:D], xnb_all[:, D:2 * D]]
    ybuf3 = xnb_all[:, :].bitcast(F32)
    ybuf = [sb("ybuf%d" % i, [128, D], F32) for i in range(2)]
    junk = ybuf[1]
    gate_bc = sb("gate_bc", [128, D], F32)
    fgain_sb = sb("fgain_sb", [128, D], F32)
    csb = {n: sb(n, shp, dt) for n, shp, dt in CONST_SPECS}
    cT_sb = sb("cT_sb", [128, 16], F32)
    b_adaT_sb = sb("b_adaT_sb", [128, 24], F32)
    ngain_sb = sb("ngain_sb", [128, 8], F32)
    ggain_sb = sb("ggain_sb", [128, 4], F32)
    dgain_sb = sb("dgain_sb", [128, 4], F32)
    lam_sb = sb("lam_sb", [128, 256], F32)
    wup_f = sb("wup_f", [32, 256], F32)
    wup_bf = sb("wup_bf", [32, 256], BF16)
    small = sb("small", [128, 256], F32)
    modT = sb("modT", [128, 48], F32)
    acol = sb("acol", [128, 16], F32)
    scol = sb("scol", [128, 16], F32)
    gcol = sb("gcol", [128, 16], F32)
    diagg = sb("diagg", [128, 128], F32)
    arena = sb("arena", [128, 24576], BF16)
    ps = stack.enter_context(nc.psum_tensor("ps", [128, 4096], F32))

    def bank(b, n=512, off=0):
        return ps[:, b * 512 + off: b * 512 + off + n]

    def bank_bf(b):
        return ps[:, b * 512:(b + 1) * 512].bitcast(BF16)

    SM = {}
    _sm_next = [0]

    def smcol(name, n=1):
        SM[name] = small[:, _sm_next[0]:_sm_next[0] + n]
        _sm_next[0] += n
        return SM[name]

    eps_col = smcol("eps_col", 1)

    def dma(queue, out, in_, key, reads=(), writes=(), bulk=False, **kw):
        pg.add(queue, lambda e: e.dma_start(out=out, in_=in_, **kw), reads=reads, writes=writes,
               dma_key=key, bulk=bulk)

    def dump(name, src_ap, reads):
        if name in dump_d:
            dma("pool", dump_d[name], src_ap, ("dump", name), reads=reads, max_dma_last_dim=4096)

    def act(out, in_, func, reads, writes, excl=(), bias=0.0, scale=1.0, accum=None):
        def f(e):
            kw = {}
            if accum is not None:
                kw["accum_out"] = accum
            return e.activation(out=out, in_=in_, func=func, bias=bias, scale=scale, **kw)
        pg.add("act", f, reads=reads, writes=writes, excl=excl)

    def tt(eng, out, in0, in1, op, reads, writes, excl=()):
        pg.add(eng, lambda e: e.tensor_tensor(out=out, in0=in0, in1=in1, op=op),
               reads=reads, writes=writes, excl=excl)

    def ts(eng, out, in0, s1, s2, op0, op1, reads, writes, excl=()):
        if s2 is None:
            pg.add(eng, lambda e: e.tensor_scalar(out=out, in0=in0, scalar1=s1, scalar2=None, op0=op0),
                   reads=reads, writes=writes, excl=excl)
        else:
            pg.add(eng, lambda e: e.tensor_scalar(out=out, in0=in0, scalar1=s1, scalar2=s2, op0=op0, op1=op1),
                   reads=reads, writes=writes, excl=excl)

    def stt(out, in0, scalar, in1, op0, op1, reads, writes, excl=()):
        pg.add("dve", lambda e: e.scalar_tensor_tensor(out=out, in0=in0, scalar=scalar, in1=in1, op0=op0, op1=op1),
               reads=reads, writes=writes, excl=excl)

    def copy(eng, out, in_, reads, writes, excl=()):
        pg.add(eng, lambda e: e.tensor_copy(out=out, in_=in_), reads=reads, writes=writes, excl=excl)

    def mm_group(out, pairs, reads, excl, writes=(), first_start=True, last_stop=True):
        def f(e):
            ins = None
            n = len(pairs)
            for i, (l, r) in enumerate(pairs):
                ins = e.matmul(out, l, r, start=(first_start and i == 0), stop=(last_stop and i == n - 1),
                               skip_group_check=not (first_start and last_stop))
            return ins
        pg.add("pe", f, reads=reads, writes=writes, excl=excl)

    def rstd_from_ss(ss, rstd, n, key_ss, key_rstd):
        act(ss, ss, AF.Ln, reads=[key_ss, "eps_col"], writes=[key_ss], scale=1.0 / n, bias=eps_col)
        act(rstd, ss, AF.Exp, reads=[key_ss], writes=[key_rstd], scale=-0.5)

    pg.add("pool", lambda e: e.memset(eps_col, EPS), reads=[], writes=["eps_col"])
    for n, shp, dt in CONST_SPECS:
        dma("sp", csb[n][:, :], cd[n], "const", writes=[n], bulk=True)
    dma("sp", cT_sb[:, :], cT_d, "const", writes=["cT"], bulk=True)
    dma("sp", b_adaT_sb[:, :], b_adaT_d, "const", writes=["b_adaT"], bulk=True)
    dma("sp", ngain_sb[:, :], ngain_d, "const", writes=["ngain"], bulk=True)
    dma("sp", ggain_sb[:, :], ggain_d, "const", writes=["ggain"], bulk=True)
    dma("sp", dgain_sb[:, :], dgain_d, "const", writes=["dgain"], bulk=True)
    dma("sp", lam_sb[:, :], lam_d, "const", writes=["lam_in"], bulk=True)
    dma("sp", wup_f[0:17, :], wup_d, "const", writes=["wup_f"], bulk=True)
    dma("sp", fgain_sb[:, :], fgain_d, "const", writes=["fgain"], bulk=True)
    copy("dve", wup_bf[0:17, :], wup_f[0:17, :], reads=["wup_f"], writes=["wup_bf"])

    sc = smcol("sc", 16)
    tmp16 = smcol("tmp16", 16)
    act(tmp16, cT_sb[:, :], AF.Exp, reads=["cT"], writes=["tmp16"], scale=-1.0)
    ts("dve", tmp16, tmp16, 1.0, None, ALU.add, None, reads=["tmp16"], writes=["tmp16"])
    pg.add("dve", lambda e: e.reciprocal(out=tmp16, in_=tmp16), reads=["tmp16"], writes=["tmp16"])
    tt("dve", sc, cT_sb[:, :], tmp16, ALU.mult, reads=["cT", "tmp16"], writes=["sc"])

    sc_bf = sb("sc_bf", [128, 16], BF16)
    copy("dve", sc_bf[:, :], sc, reads=["sc"], writes=["sc_bf"])
    slab = [mixbuf[:, 0:8192].bitcast(F32).rearrange("p (k n) -> p k n", k=8),
            mixbuf[:, 8192:16384].bitcast(F32).rearrange("p (k n) -> p k n", k=8)]
    slab_bf = [Rbuf[:, 0:4096].rearrange("p (k n) -> p k n", k=8), Rbuf[:, 4096:8192].rearrange("p (k n) -> p k n", k=8)]
    w_ada_v = w_ada_d.rearrange("(k p) n -> p k n", p=128)
    for sl in range(6):
        i2 = sl % 2
        dma("sp", slab[i2], w_ada_v[:, :, sl * 512:(sl + 1) * 512], ("slab", i2), writes=[("slab", i2)])
        copy("dve", slab_bf[i2], slab[i2], reads=[("slab", i2)], writes=[("slabbf", i2)])
        for jj in range(4):
            j = sl * 4 + jj
            pairs = [(slab_bf[i2][:, kc, jj * 128:(jj + 1) * 128], sc_bf[:, kc * 2:(kc + 1) * 2]) for kc in range(8)]
            mm_group(bank(0, 2, j * 2), pairs, reads=[("slabbf", i2), "sc_bf"], excl=[("ps", 0)])
    tt("dve", modT[:, :].rearrange("p (j b) -> p j b", b=2), bank(0, 48).rearrange("p (j b) -> p j b", b=2),
       b_adaT_sb[:, :].unsqueeze(2).to_broadcast([128, 24, 2]), ALU.add,
       reads=["b_adaT"], writes=["modT", ("slab", 0), ("slab", 1), ("slabbf", 0), ("slabbf", 1)], excl=[("ps", 0)])
    ts("dve", acol[:, :], modT[:, 16:32], 1.0, None, ALU.add, None, reads=["modT"], writes=["acol"])
    tt("dve", acol[:, :].rearrange("p (k b) -> p k b", b=2), acol[:, :].rearrange("p (k b) -> p k b", b=2),
       ngain_sb[:, :].unsqueeze(2).to_broadcast([128, 8, 2]), ALU.mult, reads=["acol", "ngain"], writes=["acol"])
    copy("dve", scol[:, :], modT[:, 0:16], reads=["modT"], writes=["scol"])
    copy("dve", gcol[:, :], modT[:, 32:48], reads=["modT"], writes=["gcol"])
    dump("acol", acol[:, :], ["acol"])
    dump("scol", scol[:, :], ["scol"])
    dump("gcol", gcol[:, :], ["gcol"])

    lam_t = smcol("lam_t", 2)
    neg_lam = smcol("neg_lam", 1)
    lamprod = smcol("lamprod", 128)
    tt("dve", lamprod.rearrange("p (a d) -> p a d", a=2), lam_sb[:, :].rearrange("p (a t d) -> p a t d", a=2, t=2)[:, :, 0, :],
       lam_sb[:, :].rearrange("p (a t d) -> p a t d", a=2, t=2)[:, :, 1, :], ALU.mult,
       reads=["lam_in"], writes=["lamprod"])
    pg.add("dve", lambda e: e.tensor_reduce(out=lam_t, in_=lamprod.rearrange("p (a d) -> p a d", a=2), axis=AX.X, op=ALU.add),
           reads=["lamprod"], writes=["lam_t"])
    act(lam_t, lam_t, AF.Exp, reads=["lam_t"], writes=["lam_t"])
    stt(neg_lam, lam_t[:, 1:2], -LAM_INIT, lam_t[:, 0:1], ALU.add, ALU.subtract, reads=["lam_t"], writes=["neg_lam"])
    dump("neg_lam", neg_lam, ["neg_lam"])

    wst = [arena[:, i * 7200:(i + 1) * 7200].bitcast(F32) for i in range(3)]
    for kc in range(8):
        sl = kc % 3
        dma("sp", wst[sl], w_in_d[kc * 128:(kc + 1) * 128, :], ("wst", sl), writes=[("A:wst", sl)])
        copy("dve", w_in_sb[:, kc, :], wst[sl], reads=[("A:wst", sl)], writes=[("win", kc)])

    xslot = [0]

    def load_x(tok0):
        s = xslot[0] % NXT
        xslot[0] += 1
        dma("sp", xt[s][:, :], x_d[tok0:tok0 + 128, :], ("xt", s), writes=[("xt", s)])
        return s

    ss_c = smcol("ss_c", 1)
    rstd_c = smcol("rstd_c", 1)
    ss_p = [smcol("ss_p%d" % i, 1) for i in range(3)]
    rstd_p = [smcol("rstd_p%d" % i, 1) for i in range(3)]
    ss4_p = [smcol("ss4_p%d" % i, 4) for i in range(2)]
    rstd4_p = [smcol("rstd4_p%d" % i, 4) for i in range(2)]


    NG = S // 512
    arena_f = arena[:, :].bitcast(F32)
    dummy = smcol("dummy", 1)
    ipb = [0]

    def phase_barrier():
        pg.add("dve", lambda e: e.memset(dummy, 0.0), reads=[], writes=["AR", "dummy"])

    def win_keys(col):
        return [("win", kc) for kc in range(8)]

    evq = [0]

    def evac_copy(out, in_, reads, writes, excl):
        evq[0] += 1
        if evq[0] % 2 == 0:
            act(out, in_, AF.Copy, reads=reads, writes=writes, excl=excl)
        else:
            copy("dve", out, in_, reads=reads, writes=writes, excl=excl)

    ipbanks = [2]
    ipbl = [[0, 1]]

    def inproj_fm_thunks(col0, M, evac):
        def mk(g):
            def f():
                bk = ipbl[0][ipb[0] % min(ipbanks[0], len(ipbl[0]))]
                ipb[0] += 1
                pairs = [(w_in_sb[:, kc, col0:col0 + M], hT[:, kc, g * 512:(g + 1) * 512]) for kc in range(8)]
                mm_group(bank(bk)[0:M, :], pairs, reads=win_keys(col0) + [("hT", g, "d"), ("hT", g, "a")], excl=[("ps", bk)])
                evac(g, bank(bk)[0:M, :], bk)
            return f
        return [mk(g) for g in range(NG)]

    def inproj_fm(col0, M, evac):
        for f in inproj_fm_thunks(col0, M, evac):
            f()

    def inproj_fm_pairs(col0, M, evac):
        def mk(g):
            def mm(bk):
                pairs = [(w_in_sb[:, kc, col0:col0 + M], hT[:, kc, g * 512:(g + 1) * 512]) for kc in range(8)]
                mm_group(bank(bk)[0:M, :], pairs, reads=win_keys(col0) + [("hT", g, "d"), ("hT", g, "a")], excl=[("ps", bk)])
            def ev(bk):
                evac(g, bank(bk)[0:M, :], bk)
            return (mm, ev)
        return [mk(g) for g in range(NG)]

    def inproj_fm_pieces(col0, M, evac, npieces=4):
        out_list = []
        per = 8 // npieces
        for g in range(NG):
            for pc in range(npieces):
                def f(g=g, pc=pc):
                    bk = 0
                    pairs = [(w_in_sb[:, kc, col0:col0 + M], hT[:, kc, g * 512:(g + 1) * 512])
                             for kc in range(pc * per, (pc + 1) * per)]
                    mm_group(bank(bk)[0:M, :], pairs, reads=win_keys(col0) + [("hT", g, "d"), ("hT", g, "a")], excl=[("ps", bk)],
                             first_start=(pc == 0), last_stop=(pc == npieces - 1))
                    if pc == npieces - 1:
                        evac(g, bank(bk)[0:M, :], bk)
                out_list.append(f)
        return out_list

    def inproj_tm_pairs(col0, evac):
        def mk(t):
            def mm(bk):
                pairs = [(hT[:, kc, t * 128:(t + 1) * 128], w_in_sb[:, kc, col0:col0 + 512]) for kc in range(8)]
                mm_group(bank(bk), pairs, reads=win_keys(col0) + [("hT", t // 4, "d"), ("hT", t // 4, "a")], excl=[("ps", bk)])
            def ev(bk):
                evac(t, bank(bk), bk)
            return (mm, ev)
        return [mk(t) for t in range(NT)]

    def inproj_tm_thunks(col0, evac):
        def mk(t):
            def f():
                bk = ipbl[0][ipb[0] % min(ipbanks[0], len(ipbl[0]))]
                ipb[0] += 1
                pairs = [(hT[:, kc, t * 128:(t + 1) * 128], w_in_sb[:, kc, col0:col0 + 512]) for kc in range(8)]
                mm_group(bank(bk), pairs, reads=win_keys(col0) + [("hT", t // 4, "d"), ("hT", t // 4, "a")], excl=[("ps", bk)])
                evac(t, bank(bk), bk)
            return f
        return [mk(t) for t in range(NT)]

    def inproj_tm(col0, evac):
        for f in inproj_tm_thunks(col0, evac):
            f()

    gqT = arena[:, 0:2 * S].rearrange("p (c t) -> p c t", c=2)
    gkT = arena[:, 2 * S:4 * S].rearrange("p (c t) -> p c t", c=2)
    gv = arena[:, 4 * S:8 * S].rearrange("p (t n) -> p t n", n=512)
    grT = arena[0:32, 8 * S:9 * S]
    tb = 9 * S
    tf = tb // 2
    laA = [arena_f[:, tf + i * 512:tf + (i + 1) * 512] for i in range(2)]
    ebA = [arena_f[:, tf + 1024 + i * 512:tf + 1024 + (i + 1) * 512] for i in range(2)]
    enbA = [arena_f[:, tf + 2048 + i * 512:tf + 2048 + (i + 1) * 512] for i in range(2)]
    assert 2 * (tf + 3072) <= 24576
    kinT = [arena[:, tb + i * 256:tb + (i + 1) * 256] for i in range(2)]
    AT = [arena[:, tb + 512 + i * 512:tb + 512 + (i + 1) * 512].rearrange("p (h t) -> p h t", h=4) for i in range(2)]
    on = [arena[:, tb + 1536 + i * 512:tb + 1536 + (i + 1) * 512].rearrange("p (h t) -> p h t", h=4) for i in range(2)]
    S_bf = arena[:, tb + 2560:tb + 2816].rearrange("p (c t) -> p c t", c=2)
    fB = (tb + 2816) // 2
    S32 = arena_f[:, fB:fB + 256]
    Sd = arena_f[:, fB + 256:fB + 512]
    osq = arena_f[:, fB + 512:fB + 1024]
    assert 2 * (fB + 1024) <= 24576
    dec_all = smcol("dec_all", 2 * NT)
    ss4 = smcol("ss4", 4)
    rstd4 = smcol("rstd4", 4)
    QORD = [0, 2, 1, 3]

    def dec_col(t, c):
        i = (t // 2) * 4 + c * 2 + (t % 2)
        return dec_all[:, i:i + 1]

    def gla_begin(b):
        phase_barrier()
        pg.add("pool", lambda e: e.memset(grT, 1.0), reads=[], writes=[("A:gr",)])
        per_g = {g: [] for g in range(NG)}
        def add_fm(col0, M, evac):
            for g, f in enumerate(inproj_fm_pairs(col0, M, evac)):
                per_g[g].append(f)
        add_fm(C_GR, 16, lambda g, p, bk: copy("dve", grT[0:16, g * 512:(g + 1) * 512], p, reads=[], writes=[("A:gr",)], excl=[("ps", bk)]))
        for c in range(2):
            add_fm(C_GQ + c * 128, 128, lambda g, p, bk, c=c: evac_copy(
                gqT[:, c, g * 512:(g + 1) * 512], p, reads=[], writes=[("A:gq", c, g)], excl=[("ps", bk)]))
            add_fm(C_GK + c * 128, 128, lambda g, p, bk, c=c: evac_copy(
                gkT[:, c, g * 512:(g + 1) * 512], p, reads=[], writes=[("A:gk", c, g)], excl=[("ps", bk)]))
        for hc in range(4):
            add_fm(C_GZ + hc * 128, 128, lambda g, p, bk, hc=hc: evac_copy(
                mixT[:, hc, g * 512:(g + 1) * 512], p, reads=[], writes=[("mix", hc, g)], excl=[("ps", bk)]))
        for t, f in enumerate(inproj_tm_pairs(C_GV, lambda t, p, bk: evac_copy(gv[:, t, :], p, reads=[], writes=[("A:gv", t)], excl=[("ps", bk)]))):
            per_g[t // 4].append(f)
        return per_g

    def gla_gate():
        for hc in range(4):
            mk = [("mix", hc, g) for g in range(NG)]
            dst = mixT[:, hc, :]
            act(dst, dst, AF.Silu, reads=mk, writes=mk)
            ts("dve", dst, dst, ggain_sb[:, hc:hc + 1], None, ALU.mult, None, reads=mk + ["ggain"], writes=mk)

    def run_gla(b, iptail):
        ipbanks[0] = 2
        ipbl[0] = [0, 1]
        ip_step, need_group, ipq, pend_ev = iptail
        gv_thunks = []

        def zstage(p):
            pp = p % 2
            zb = 2 + 2 * pp
            def fz(e):
                ins = None
                for j in range(2):
                    tok = slice(p * 256 + j * 128, p * 256 + (j + 1) * 128)
                    ins = e.matmul(bank(zb, 256, j * 256), grT[0:17, tok], wup_bf[0:17, :], start=True, stop=True)
                return ins
            pg.add("pe", fz, reads=[("A:gr",), "wup_bf"], excl=[("ps", zb)])
            act(laA[pp], bank(zb), AF.Exp, reads=[], writes=[("A:la", pp)], excl=[("ps", zb)], scale=-1.0)
            act(laA[pp], laA[pp], AF.Ln, reads=[("A:la", pp)], writes=[("A:la", pp)], bias=1.0)

        def cstage(p):
            pp = p % 2
            g = p // 2
            bb = 3 + 2 * pp
            tk = slice(p * 256, (p + 1) * 256)
            def fcs(e):
                ins = None
                for c in range(2):
                    for j in range(2):
                        ins = e.matmul(bank(bb, 128, (c * 2 + j) * 128), laA[pp][:, j * 256 + c * 128:j * 256 + (c + 1) * 128],
                                       csb["triI_f"][:, :], start=True, stop=True)
                return ins
            pg.add("pe", fcs, reads=[("A:la", pp), "triI_f"], excl=[("ps", bb)])
            act(ebA[pp], bank(bb), AF.Exp, reads=[], writes=[("A:eb", pp)], excl=[("ps", bb)], bias=math.log(0.125))
            act(enbA[pp], bank(bb), AF.Exp, reads=[], writes=[("A:enb", pp)], excl=[("ps", bb)], scale=-1.0)
            act(dec_all[:, p * 4:(p + 1) * 4].unsqueeze(2), bank(bb).rearrange("p (i t) -> p i t", i=4)[:, :, 127:128], AF.Exp,
                reads=[], writes=[("dec", p)], excl=[("ps", bb)])
            tt("pool", gqT[:, :, tk], gqT[:, :, tk], ebA[pp].rearrange("p (c t) -> p c t", c=2), ALU.mult,
               reads=[("A:gq", 0, g), ("A:gq", 1, g), ("A:eb", pp)], writes=[("A:qin", p)])
            tt("pool", gkT[:, :, tk], gkT[:, :, tk], enbA[pp].rearrange("p (c t) -> p c t", c=2), ALU.mult,
               reads=[("A:gk", 0, g), ("A:gk", 1, g), ("A:enb", pp)], writes=[("A:kin", p)])

        NP = NT // 2
        need_group(0)
        zstage(0)
        for p in range(NP):
            if p + 1 < NP:
                need_group((p + 1) // 2)
                zstage(p + 1)
            if ipq or pend_ev:
                ip_step(2)
                ip_step(2)
            cstage(p)
        while ipq or pend_ev:
            ip_step(2)
        gla_gate()
        pg.add("dve", lambda e: e.memset(S32, 0.0),
               reads=[("A:la", 0), ("A:la", 1), ("A:eb", 0), ("A:eb", 1), ("A:enb", 0), ("A:enb", 1)],
               writes=[("A:S32",), ("A:la", 0), ("A:la", 1), ("A:eb", 0), ("A:eb", 1), ("A:enb", 0), ("A:enb", 1)])
        pg.add("pool", lambda e: e.memset(Sd, 0.0), reads=[("A:S32",)], writes=[("A:Sd",)])
        bkey = [("A:la", 0)]

        def pre(t):
            tok = slice(t * 128, (t + 1) * 128)
            p = t // 2
            par = t % 2
            ktp = bank_bf(1)[:, 0:256]
            def fkt(e):
                e.transpose(ktp[:, 0:128], gkT[:, 0, tok], csb["ident_bf"][:, :])
                return e.transpose(ktp[:, 128:256], gkT[:, 1, tok], csb["ident_bf"][:, :])
            pg.add("pe", fkt, reads=[("A:kin", p), "ident_bf"], excl=[("ps", 1)])
            act(kinT[par], ktp, AF.Copy, reads=bkey, writes=[("A:kinT", par)], excl=[("ps", 1)])
            def fsc(e):
                ins = None
                for h in range(4):
                    c = h // 2
                    rows = slice(0, 64) if h % 2 == 0 else slice(64, 128)
                    ins = e.matmul(bank(2 + h % 2, 128, c * 128), gkT[rows, c, tok], gqT[rows, c, tok], start=True, stop=True)
                return ins
            pg.add("pe", fsc, reads=[("A:kin", p), ("A:qin", p)], excl=[("ps", 2), ("ps", 3)])
            for q in range(2):
                tt("dve", AT[par][:, q::2, :], bank(2 + q, 256).rearrange("p (c t) -> p c t", c=2),
                   csb["maskT_bf"][:, :].unsqueeze(1).to_broadcast([128, 2, 128]), ALU.mult,
                   reads=["maskT_bf"] + bkey, writes=[("A:AT", par, q)], excl=[("ps", 2 + q)])

        def obanks(t):
            return (6, 7) if t % 2 == 0 else (4, 5)

        def main_o(t):
            tok = slice(t * 128, (t + 1) * 128)
            p = t // 2
            par = t % 2
            ob = obanks(t)
            def fo_(e):
                ins = None
                for h in range(4):
                    ins = e.matmul(bank(ob[h % 2], 128, (h // 2) * 128), AT[par][:, h, :], gv[:, t, h * 128:(h + 1) * 128],
                                   start=(h < 2), stop=(t == 0), skip_group_check=True)
                if t > 0:
                    for h in range(4):
                        c = h // 2
                        rows = slice(0, 64) if h % 2 == 0 else slice(64, 128)
                        ins = e.matmul(bank(ob[h % 2], 128, c * 128), gqT[rows, c, tok], S_bf[rows, c, :],
                                       start=False, stop=True, skip_group_check=True)
                return ins
            pg.add("pe", fo_, reads=[("A:AT", par, 0), ("A:AT", par, 1), ("A:gv", t), ("A:qin", p), ("A:Sbf",)],
                   excl=[("ps", ob[0]), ("ps", ob[1])])
            ssk = "ss4_p%d" % par
            for q in range(4):
                act(osq[:, q * 128:(q + 1) * 128], bank(ob[q // 2], 128, (q % 2) * 128), AF.Square, reads=bkey,
                    writes=[("A:osq", q), (ssk, q)], excl=[("ps", ob[q // 2])], accum=ss4_p[par][:, q:q + 1])

        def main_kv(t):
            p = t // 2
            par = t % 2
            if t < NT - 1:
                def fkv(e):
                    ins = None
                    for h in range(4):
                        c = h // 2
                        rows = slice(0, 64) if h % 2 == 0 else slice(64, 128)
                        ins = e.matmul(bank(0)[rows, c * 128:(c + 1) * 128], kinT[par][:, h * 64:(h + 1) * 64],
                                       gv[:, t, h * 128:(h + 1) * 128], start=True, stop=True)
                    return ins
                pg.add("pe", fkv, reads=[("A:kinT", par), ("A:gv", t)], excl=[("ps", 0)])
                for c in range(2):
                    cs = slice(c * 128, (c + 1) * 128)
                    stt(S32[:, cs], bank(0, 128, c * 128), dec_col(t, c), Sd[:, cs], ALU.mult, ALU.add,
                        reads=[("A:Sd",), ("dec", p)], writes=[("A:S32",)], excl=[("ps", 0)])
                act(S_bf, S32.rearrange("p (c t) -> p c t", c=2), AF.Copy, reads=[("A:S32",)], writes=[("A:Sbf",)])
                if t + 1 < NT - 1:
                    for c in range(2):
                        cs = slice(c * 128, (c + 1) * 128)
                        ts("dve", Sd[:, cs], S32[:, cs], dec_col(t + 1, c), None, ALU.mult, None,
                           reads=[("A:S32",), ("dec", (t + 1) // 2)], writes=[("A:Sd",)])

        def post(t):
            tok = slice(t * 128, (t + 1) * 128)
            g = t // 4
            par = t % 2
            ob = obanks(t)
            ssk, rsk = "ss4_p%d" % par, "rstd4_p%d" % par
            act(ss4_p[par], ss4_p[par], AF.Ln, reads=[(ssk, q) for q in range(4)] + ["eps_col"], writes=[ssk],
                scale=1.0 / 128, bias=eps_col)
            act(rstd4_p[par], ss4_p[par], AF.Exp, reads=[ssk], writes=[rsk], scale=-0.5)
            for q in range(2):
                tt("dve", on[par][:, q * 2:(q + 1) * 2, :], bank(ob[q], 256).rearrange("p (c t) -> p c t", c=2),
                   rstd4_p[par][:, q * 2:(q + 1) * 2].unsqueeze(2).to_broadcast([128, 2, 128]), ALU.mult,
                   reads=[rsk] + bkey, writes=[("A:on", par, q)], excl=[("ps", ob[q])])
            tp = bank_bf(1)[:, 256:768]
            def ftp(e):
                ins = None
                for q in range(4):
                    h = QORD[q]
                    ins = e.transpose(tp[:, h * 128:(h + 1) * 128], on[par][:, q, :], csb["ident_bf"][:, :])
                return ins
            pg.add("pe", ftp, reads=[("A:on", par, 0), ("A:on", par, 1), "ident_bf"], excl=[("ps", 1)])
            tt("dve", mixT[:, 0:4, tok], tp.rearrange("p (h t) -> p h t", h=4), mixT[:, 0:4, tok], ALU.mult,
               reads=[("mix", hc, g) for hc in range(4)], writes=[("mix", hc, g) for hc in range(4)], excl=[("ps", 1)])

        pre(0)
        for t in range(NT):
            main_o(t)
            main_kv(t)
            if t + 1 < NT:
                pre(t + 1)
            post(t)

    dv_aug = arena[:, 0:NT * 520].rearrange("p (t h n) -> p t h n", h=4, n=130)
    do = NT * 520
    dqT = [arena[:, do + i * S:do + (i + 1) * S] for i in range(2)]
    dkT = [arena[:, do + (2 + i) * S:do + (3 + i) * S] for i in range(2)]
    po = do + 4 * S
    PT = [[arena[:, po + (2 * i + m) * 512:po + (2 * i + m + 1) * 512] for m in range(2)] for i in range(2)]
    fo2 = (po + 2048) // 2
    Oc = arena_f[:, fo2:fo2 + 8 * 129].rearrange("p (i n) -> p i n", n=129)
    Dm = arena_f[:, fo2 + 1032:fo2 + 1544].rearrange("p (s n) -> p s n", s=4)
    t2 = arena_f[:, fo2 + 1544:fo2 + 2056].rearrange("p (s n) -> p s n", s=4)
    dno = 2 * (fo2 + 2056)
    Dn = arena[:, dno:dno + 512].rearrange("p (s n) -> p s n", s=4)
    assert dno + 512 <= 24576
    rz = smcol("rz", 8)
    SBANKS = [(2, 3), (7, 1)]
    astep = [0]

    def obank(i):
        return 4 + i // 3, (i % 3) * 129

    def attention(b, h, buf, hook=None, fin_q=None, flush=True):
        qsub = 256 if SLOPES[h] * 511 > 40 else 512
        steps = [(G, kb) for G in range(NG) for kb in range(4 * G + 4)]
        started = {}
        pend = []

        def do_qk_exp(G, kb):
            t = kb - 4 * G
            s0 = max(t, 0)
            c0 = 128 * s0
            ks = slice(kb * 128, (kb + 1) * 128)
            qs = slice(G * 512 + c0, (G + 1) * 512)
            pb = astep[0] % 2
            astep[0] += 1
            sb0, sb1 = SBANKS[pb]
            def fqk(e):
                e.matmul(bank(sb0)[:, c0:512], dkT[buf][0:64, ks], dqT[buf][0:64, qs], start=True, stop=True)
                return e.matmul(bank(sb1)[:, c0:512], dkT[buf][64:128, ks], dqT[buf][64:128, qs], start=True, stop=True)
            pg.add("pe", fqk, reads=[("A:dk", buf, kb // 4), ("A:dq", buf, G)], excl=[("ps", sb0), ("ps", sb1)])
            for m in range(2):
                sbm = (sb0, sb1)[m]
                if qsub == 512:
                    chunks = [(c0, 512)]
                else:
                    chunks = [(max(c0, lo), lo + qsub) for lo in range(0, 512, qsub) if c0 < lo + qsub]
                for (a, bnd) in chunks:
                    delta = 4 * G + (bnd - 1) // 128 - kb
                    act(PT[pb][m][:, a:bnd], bank(sbm)[:, a:bnd], AF.Exp, reads=["bias_tab"], writes=[("A:PT", pb, m)],
                        excl=[("ps", sbm)], bias=csb["bias_tab"][:, h * 16 + delta:h * 16 + delta + 1], scale=0.125)
                if t >= 0:
                    tt("dve", PT[pb][m][:, c0:c0 + 128], PT[pb][m][:, c0:c0 + 128], csb["maskT_bf"][:, :], ALU.mult,
                       reads=[("A:PT", pb, m), "maskT_bf"], writes=[("A:PT", pb, m)])
            return pb, s0

        def do_pv(G, kb, pb, s0):
            st_set = started.setdefault(G, set())
            plan = []
            for m in range(2):
                for s in range(s0, 4):
                    bk, off = obank(4 * m + s)
                    plan.append((bk, off, m, s, bk not in st_set, kb == 4 * G + s))
                    st_set.add(bk)
            def fpv(e):
                ins = None
                for (bk, off, m, s, st, sp) in plan:
                    ins = e.matmul(bank(bk)[:, off:off + 129], PT[pb][m][:, s * 128:(s + 1) * 128], dv_aug[:, kb, h, 0:129],
                                   start=st, stop=sp, skip_group_check=True)
                return ins
            pg.add("pe", fpv, reads=[("A:PT", pb, 0), ("A:PT", pb, 1), ("A:dv", kb)], excl=[("ps", 4), ("ps", 5), ("ps", 6)])
            if kb == 4 * G + 3:
                finalize(G)

        if fin_q is None:
            fin_q = []

        def finalize(G):
            while any(f_ is not None and getattr(f_, "is_f2", False) for f_ in fin_q):
                f_ = fin_q.pop(0)
                if f_ is not None:
                    f_()
            copy("dve", Oc[:, 0:3, :], bank(4, 387).rearrange("p (i n) -> p i n", n=129), reads=[], writes=[("A:Oc", 0)], excl=[("ps", 4)])
            act(Oc[:, 3:6, :], bank(5, 387).rearrange("p (i n) -> p i n", n=129), AF.Copy, reads=[], writes=[("A:Oc", 1)], excl=[("ps", 5)])
            copy("dve", Oc[:, 6:8, :], bank(6, 258).rearrange("p (i n) -> p i n", n=129), reads=[], writes=[("A:Oc", 2)], excl=[("ps", 6)])
            ock = [("A:Oc", i) for i in range(3)]

            def F2():
                pg.add("dve", lambda e: e.reciprocal(out=rz.unsqueeze(2), in_=Oc[:, :, 128:129]), reads=ock, writes=["rz"])
                ts("dve", rz[:, 4:8], rz[:, 4:8], neg_lam, None, ALU.mult, None, reads=["rz", "neg_lam"], writes=["rz"])
                tt("dve", t2, Oc[:, 4:8, 0:128], rz[:, 4:8].unsqueeze(2).to_broadcast([128, 4, 128]), ALU.mult,
                   reads=ock + ["rz"], writes=[("A:t2",)])
                tt("dve", Dm, Oc[:, 0:4, 0:128], rz[:, 0:4].unsqueeze(2).to_broadcast([128, 4, 128]), ALU.mult,
                   reads=ock + ["rz"], writes=[("A:Dm",)])
                tt("pool", Dm, Dm, t2, ALU.add, reads=[("A:Dm",), ("A:t2",)], writes=[("A:Dm",)])

            def F3():
                act(t2, Dm, AF.Square, reads=[("A:Dm",)], writes=[("A:t2",)])
                pg.add("dve", lambda e: e.tensor_reduce(out=ss4, in_=t2, axis=AX.X, op=ALU.add), reads=[("A:t2",)], writes=["ss4"])

            def F4():
                rstd_from_ss(ss4, rstd4, 128, "ss4", "rstd4")

            def F5():
                tt("dve", Dn, Dm, rstd4.unsqueeze(2).to_broadcast([128, 4, 128]), ALU.mult, reads=[("A:Dm",), "rstd4"], writes=[("A:Dn",)])
                tbk = SBANKS[astep[0] % 2][0]
                tp = bank_bf(tbk)[:, 0:512]
                def ftp(e):
                    ins = None
                    for s_ in range(4):
                        ins = e.transpose(tp[:, s_ * 128:(s_ + 1) * 128], Dn[:, s_, :], csb["ident_bf"][:, :])
                    return ins
                pg.add("pe", ftp, reads=[("A:Dn",), "ident_bf"], excl=[("ps", tbk)])

                dst = mixT[:, 4 + h, G * 512:(G + 1) * 512]
                tt("dve", dst, tp, dst, ALU.mult, reads=[("mix", 4 + h, G)], writes=[("mix", 4 + h, G)], excl=[("ps", tbk)])

            F2.is_f2 = True
            fin_q.extend([F2, None, F3, None, F4, None, None, F5])

        for (G, kb) in steps:
            pb, s0 = do_qk_exp(G, kb)
            if pend:
                do_pv(*pend.pop(0))
            pend.append((G, kb, pb, s0))
            if fin_q:
                f_ = fin_q.pop(0)
                if f_ is not None:
                    f_()
            if hook is not None:
                hook()
        while pend:
            do_pv(*pend.pop(0))
        while flush and fin_q:
            f_ = fin_q.pop(0)
            if f_ is not None:
                f_()

    def head_inproj_thunks(h, pieces=False):
        buf = h % 2
        mk = inproj_fm_pieces if pieces else inproj_fm_thunks
        th = []
        th += mk(C_DZ + h * 128, 128, lambda g, p, bk: copy(
            "dve", mixT[:, 4 + h, g * 512:(g + 1) * 512], p, reads=[], writes=[("mix", 4 + h, g)], excl=[("ps", bk)]))
        th += mk(C_DQ + h * 128, 128, lambda g, p, bk: copy(
            "dve", dqT[buf][:, g * 512:(g + 1) * 512], p, reads=[], writes=[("A:dq", buf, g)], excl=[("ps", bk)]))
        th += mk(C_DK + h * 128, 128, lambda g, p, bk: copy(
            "dve", dkT[buf][:, g * 512:(g + 1) * 512], p, reads=[], writes=[("A:dk", buf, g)], excl=[("ps", bk)]))
        return th

    def head_gate(h):
        mk = [("mix", 4 + h, g) for g in range(NG)]
        dst = mixT[:, 4 + h, :]
        act(dst, dst, AF.Silu, reads=mk, writes=mk)
        ts("dve", dst, dst, dgain_sb[:, h:h + 1], 1.0 - LAM_INIT, ALU.mult, ALU.mult, reads=mk + ["dgain"], writes=mk)

    def run_diff(b):
        phase_barrier()
        ipbanks[0] = 2
        pg.add("pool", lambda e: e.memset(dv_aug[:, :, :, 128:129], 1.0), reads=[], writes=[("A:dvones",)])
        dv_th = inproj_tm_thunks(C_DV, lambda t, p, bk: copy("dve", dv_aug[:, t, :, 0:128], p.rearrange("p (h n) -> p h n", h=4),
                                                             reads=[("A:dvones",)], writes=[("A:dv", t)], excl=[("ps", bk)]))
        for f in head_inproj_thunks(0):
            f()
        n_pre = min(4, NT)
        for f in dv_th[:n_pre]:
            f()
        dv_late = dv_th[n_pre:]
        ipbanks[0] = 1
        fq = []
        for h in range(4):
            buf = h % 2
            head_gate(h)
            nxt = head_inproj_thunks(h + 1, pieces=True) if h < 3 else []
            if h == 0 and dv_late:
                merged = []
                while nxt or dv_late:
                    if dv_late:
                        merged.append((1, dv_late.pop(0)))
                    for _ in range(4):
                        if nxt:
                            merged.append((0, nxt.pop(0)))
                nxt = merged
            elif h == 3:
                nxt = [(0, f) for f in gate_thunks(b)]
            else:
                nxt = [(0, f) for f in nxt]
            def hook():
                budget = 2
                while nxt and budget > 0:
                    kind, f = nxt.pop(0)
                    f()
                    if kind == 0:
                        budget -= 1
            attention(b, h, buf, hook, fin_q=fq, flush=(h == 3))
            while nxt:
                nxt.pop(0)[1]()
            if h == 2:
                load_w_out()

    yslot = [0]

    wo_stage = Rbuf[:, 8 * D:16 * D].bitcast(F32).rearrange("p (k n) -> p k n", k=4)

    def load_w_out():
        hkeys = [("hT", g, e_) for g in range(NG) for e_ in ("d", "a")]
        for half in range(2):
            dma("sp", wo_stage, w_out_d[half * 512:(half + 1) * 512, :].rearrange("(k p) n -> p k n", p=128), ("wost", 0),
                writes=hkeys + ["wost"])
            copy("dve", w_out_sb[:, half * 4:(half + 1) * 4, :], wo_stage, reads=["wost"], writes=hkeys + [("wout", half)])

    def gate_thunks(b):
        def mk(kc):
            def f():
                ts("dve", diagg[:, :], csb["ident_f"][:, :], gcol[:, kc * 2 + b:kc * 2 + b + 1], None, ALU.mult, None,
                   reads=["ident_f", "gcol"], writes=["diagg"])
                mm_group(bank(0, 128, (kc % 4) * 128), [(csb["ones_f"][:, :], diagg[:, :])], reads=["ones_f", "diagg"], excl=[("ps", 0)])
                if kc % 4 == 3:
                    copy("dve", gate_bc[:, (kc // 4) * 512:(kc // 4 + 1) * 512], bank(0), reads=[], writes=["gate_bc"], excl=[("ps", 0)])
            return f
        return [mk(kc) for kc in range(8)]

    def run_out(b):
        hkeys = [("hT", g, e_) for g in range(NG) for e_ in ("d", "a")]
        wkeys = [("wout", 0), ("wout", 1)] + hkeys

        ybl = [ybuf[0], ybuf[1], ybuf3]
        ykl = [[("y", 0, 0), ("y", 0, 1)], [("y", 1, 0), ("y", 1, 1)], [("xn", 0), ("xn", 1)]]

        def out_stage1(t, s):
            tok = slice(t * 128, (t + 1) * 128)
            par = t % 2
            yb = ybl[t % 3]
            yk = ykl[t % 3]
            for half in range(2):
                bk = 2 * par + half
                pairs = [(mixT[:, kc, tok], w_out_sb[:, kc, half * 512:(half + 1) * 512]) for kc in range(8)]
                mm_group(bank(bk), pairs, reads=[("mix", kc, t // 4) for kc in range(8)] + wkeys, excl=[("ps", bk)])
                tt("dve", yb[:, half * 512:(half + 1) * 512], bank(bk), gate_bc[:, half * 512:(half + 1) * 512], ALU.mult,
                   reads=["gate_bc"], writes=[yk[half]], excl=[("ps", bk)])
            tt("pool", yb[:, :], yb[:, :], xt[s][:, :], ALU.add, reads=yk + [("xt", s)], writes=yk)

        def out_stage1c(t):
            i3 = t % 3
            yb = ybl[i3]
            yk = ykl[i3]
            ssk, rsk = "ss_p%d" % i3, "rstd_p%d" % i3
            jb = 4
            act(ps[:, jb * 512:jb * 512 + 1024], yb[:, :], AF.Square, reads=yk, writes=[ssk],
                excl=[("ps", jb), ("ps", jb + 1)], accum=ss_p[i3])
            rstd_from_ss(ss_p[i3], rstd_p[i3], D, ssk, rsk)

        def out_stage2(t):
            tok0 = b * S + t * 128
            i3 = t % 3
            yb = ybl[i3]
            yk = ykl[i3]
            stt(yb[:, :], yb[:, :], rstd_p[i3], fgain_sb[:, :], ALU.mult, ALU.mult, reads=yk + ["rstd_p%d" % i3, "fgain"], writes=yk)
            dma("act", y_d[tok0:tok0 + 128, :], yb[:, :], ("yout", i3), reads=yk)

        slots = {0: load_x(b * S)}
        for t in range(NT + 1):
            if t < NT:
                if t + 1 < NT:
                    slots[t + 1] = load_x(b * S + (t + 1) * 128)
                out_stage1(t, slots.pop(t))
            if t >= 1:
                out_stage2(t - 1)
            if t < NT:
                out_stage1c(t)

    for b in range(NSEQ):
        if upto == "mod":
            break
        DVE_KC = [0, 2, 3, 4, 6, 7]
        ACT_KC = [1, 5]

        def ht_stage1a(t, s):
            par = t % 2
            ssk, rsk = "ss_p%d" % par, "rstd_p%d" % par
            act(junk[:, :], xt[s][:, :], AF.Square, reads=[("xt", s)], writes=[("y", 1, 0), ("y", 1, 1), ssk], accum=ss_p[par])
            rstd_from_ss(ss_p[par], rstd_p[par], D, ssk, rsk)

        def ht_stage1b(t, s):
            par = t % 2
            xn = xnb[par][:, :]
            xk = ("xn", par)
            rsk = "rstd_p%d" % par
            ts("dve", xn, xt[s][:, :], rstd_p[par], None, ALU.mult, None, reads=[("xt", s), rsk], writes=[xk])
            bD, bA = 1 + 2 * par, 2 + 2 * par
            def ftr(e):
                ins = None
                for i, kc in enumerate(DVE_KC):
                    ins = e.transpose(bank_bf(bD)[:, i * 128:(i + 1) * 128], xn[:, kc * 128:(kc + 1) * 128], csb["ident_bf"][:, :])
                for i, kc in enumerate(ACT_KC):
                    ins = e.transpose(bank_bf(bA)[:, i * 128:(i + 1) * 128], xn[:, kc * 128:(kc + 1) * 128], csb["ident_bf"][:, :])
                return ins
            pg.add("pe", ftr, reads=[xk, "ident_bf"], excl=[("ps", bD), ("ps", bA)])

        def ht_stage2(t):
            par = t % 2
            bD, bA = 1 + 2 * par, 2 + 2 * par
            for i, kc in enumerate(DVE_KC):
                ts("dve", hT[:, kc, t * 128:(t + 1) * 128], bank_bf(bD)[:, i * 128:(i + 1) * 128],
                   acol[:, kc * 2 + b:kc * 2 + b + 1], scol[:, kc * 2 + b:kc * 2 + b + 1], ALU.mult, ALU.add,
                   reads=["acol", "scol"], writes=[("hT", t // 4, "d")], excl=[("ps", bD)])
            for i, kc in enumerate(ACT_KC):
                act(hT[:, kc, t * 128:(t + 1) * 128], bank_bf(bA)[:, i * 128:(i + 1) * 128], AF.Identity,
                    reads=["acol", "scol"], writes=[("hT", t // 4, "a")], excl=[("ps", bA)],
                    bias=scol[:, kc * 2 + b:kc * 2 + b + 1], scale=acol[:, kc * 2 + b:kc * 2 + b + 1])

        per_g = gla_begin(b)
        ipq = []
        pend_ev = []
        banksets = [[6, 7], [0, 5]]
        it = [0]

        ev_done = [0]
        nmm = [0]

        def ip_step(n):
            while pend_ev:
                ev, bk = pend_ev.pop(0)
                ev(bk)
                ev_done[0] += 1
            for j in range(n):
                if ipq:
                    mm, ev = ipq.pop(0)
                    ring = banksets[0] + banksets[1]
                    bk = ring[nmm[0] % len(ring)]
                    nmm[0] += 1
                    mm(bk)
                    pend_ev.append((ev, bk))
            it[0] += 1

        slots = {0: load_x(b * S)}
        for t in range(NT + 1):
            if t < NT:
                if t + 1 < NT:
                    slots[t + 1] = load_x(b * S + (t + 1) * 128)
                ht_stage1a(t, slots[t])
            if t >= 1:
                ht_stage2(t - 1)
                if (t - 1) % 4 == 3:
                    ipq.extend(per_g[(t - 1) // 4])
            if t < NT:
                ht_stage1b(t, slots.pop(t))
            ip_step(3)
        banksets[1] = [0, 1]
        n_per_group = len(per_g[0])

        def need_group(g):
            while ev_done[0] < n_per_group * (g + 1) and (ipq or pend_ev):
                ip_step(2)
        iptail = (ip_step, need_group, ipq, pend_ev)
        if b == 0:
            for kc in range(8):
                dump("hT%d" % kc, hT[:, kc, :], [("hT", g, e_) for g in range(4) for e_ in ("d", "a")])
        if upto == "hT":
            break
        run_gla(b, iptail)
        if b == 0:
            for hc in range(4):
                dump("mixg%d" % hc, mixT[:, hc, :], [("mix", hc, g) for g in range(NG)])
        if upto == "gla":
            break
        run_diff(b)
        if b == 0:
            for hc in range(4):
                dump("mixd%d" % hc, mixT[:, 4 + hc, :], [("mix", 4 + hc, g) for g in range(NG)])
        if upto == "diff":
            break
        run_out(b)

    pg.finalize(nc, stack)
    tail = [(pg.sems[(("dma", k), 0)], 16 * pg.dma_count[k]) for k in pg.dma_count if k[0] in ("dump", "yout")]
    last = Op("sp", lambda e: e.nop(), len(pg.ops), None)
    last.waits = tail
    last.signal = None
    pg.ops.append(last)
    pg.emit_all(nc)
    return nc, stack


def make_in_maps(x, c, w_ada, b_ada, norm_gain, w_in, w_gla_gate_up, b_gla_gate, gla_out_gain,
                 lambda_q1, lambda_k1, lambda_q2, lambda_k2, diff_out_gain, w_out, final_gain):
    f = lambda a: np.ascontiguousarray(np.asarray(a, dtype=np.float32))
    x = f(x); c = f(c)
    consts = _consts()
    shared = {
        "w_ada": f(w_ada[0]),
        "b_adaT": f(np.asarray(b_ada[0]).reshape(24, 128).T),
        "ngainT": f(np.asarray(norm_gain[0]).reshape(8, 128).T),
        "w_in": f(w_in[0]),
        "wup_aug": f(np.concatenate([np.asarray(w_gla_gate_up[0]), np.asarray(b_gla_gate[0])[None, :]], axis=0)),
        "ggainT": f(np.asarray(gla_out_gain[0]).reshape(4, 128).T),
        "dgainT": f(np.asarray(diff_out_gain[0]).reshape(4, 128).T),
        "lam_bc": f(np.broadcast_to(np.concatenate([np.asarray(lambda_q1[0]), np.asarray(lambda_k1[0]),
                                                    np.asarray(lambda_q2[0]), np.asarray(lambda_k2[0])])[None, :], (128, 256))),
        "w_out": f(w_out[0]),
        "fgain_bc": f(np.broadcast_to(np.asarray(final_gain)[None, :], (128, D))),
    }
    shared.update(consts)
    in_maps = []
    for i in range(NCORES):
        m = dict(shared)
        m["x"] = np.ascontiguousarray(x[2 * i:2 * i + 2].reshape(NSEQ * S, D))
        m["cT"] = np.ascontiguousarray(c[2 * i:2 * i + 2].reshape(2, 8, 128).transpose(2, 1, 0).reshape(128, 16))
        in_maps.append(m)
    return in_maps


def kernel(**inputs):
    in_maps = make_in_maps(**inputs)
    nc, stack = build_program()
    with stack:
        res = run_bass_kernel_spmd(nc, in_maps, core_ids=list(range(NCORES)))
    outs = [np.asarray(r["y"], dtype=np.float32).reshape(NSEQ, S, D) for r in res.results]
    return np.concatenate(outs, axis=0)
```

```python
import math
from contextlib import ExitStack

import numpy as np
import ml_dtypes

import concourse.bass as bass
import concourse.mybir as mybir
from concourse.bass_utils import run_bass_kernel_spmd

F32 = mybir.dt.float32
BF16 = mybir.dt.bfloat16
AF = mybir.ActivationFunctionType
ALU = mybir.AluOpType
AX = mybir.AxisListType

NCORES = 8
D = 1024
S = 2048
NSEQ = 2
NT = S // 128
D_IN = 3600
EPS = 1e-6
LAM_INIT = 0.8 - 0.6 * math.exp(-0.3 * 0)
SLOPES = [2.0 ** (-8.0 * (h + 1) / 4) for h in range(4)]
C_GQ, C_GK, C_GV, C_GZ, C_GR = 0, 256, 512, 1024, 1536
C_DQ, C_DK, C_DV, C_DZ = 1552, 2064, 2576, 3088

EPOCH = 12000


class Op:
    __slots__ = ("eng", "emit", "idx", "deps", "dma_key", "signal", "cnt", "waits")

    def __init__(self, eng, emit, idx, dma_key):
        self.eng = eng
        self.emit = emit
        self.idx = idx
        self.dma_key = dma_key
        self.deps = {}
        self.signal = False
        self.cnt = 0
        self.waits = []


class Prog:
    ENGS = ("pe", "act", "dve", "pool", "sp")

    def __init__(self):
        self.ops = []
        self.last_w = {}
        self.readers = {}
        self.xacc = {}
        self.dma_count = {}
        self.bulk = set()

    def _chan(self, op):
        return ("dma", op.dma_key) if op.dma_key is not None else op.eng

    def add(self, eng, emit, reads=(), writes=(), excl=(), dma_key=None, bulk=False):
        op = Op(eng, emit, len(self.ops), dma_key)
        self.ops.append(op)
        if dma_key is not None:
            self.dma_count[dma_key] = self.dma_count.get(dma_key, 0) + 1
            op.cnt = self.dma_count[dma_key]
            if bulk:
                self.bulk.add(dma_key)
        me = self._chan(op)
        deps = {}
        if any(isinstance(k, tuple) and isinstance(k[0], str) and k[0].startswith("A:") for k in list(reads) + list(writes)):
            reads = list(reads) + ["AR"]

        def dep(i):
            if i is None:
                return
            p = self.ops[i]
            c = self._chan(p)
            if c == "pe" and me == "pe":
                return
            if c == me and op.dma_key is not None and op.dma_key in self.bulk:
                return
            if c not in deps or deps[c] < i:
                deps[c] = i

        for k in reads:
            dep(self.last_w.get(k))
        for k in writes:
            dep(self.last_w.get(k))
            for c, i in self.readers.get(k, {}).items():
                dep(i)
        for k in excl:
            for c, i in self.xacc.get(k, {}).items():
                if c != me:
                    dep(i)
        for k in reads:
            self.readers.setdefault(k, {})[me] = op.idx
        for k in writes:
            self.last_w[k] = op.idx
            self.readers[k] = {}
        for k in excl:
            self.xacc[k] = {me: op.idx}
        op.deps = deps
        for c, i in deps.items():
            self.ops[i].signal = True
        return op

    def finalize(self, nc, stack):
        counters = {}
        for op in self.ops:
            if op.dma_key is None and op.signal:
                counters[op.eng] = counters.get(op.eng, 0) + 1
                op.cnt = counters[op.eng]
        self.sems = {}

        def sem_for(name):
            if name not in self.sems:
                self.sems[name] = stack.enter_context(nc.semaphore("s%d" % len(self.sems)))
            return self.sems[name]

        def target(p):
            if p.dma_key is not None:
                n = self.dma_count[p.dma_key] if p.dma_key in self.bulk else p.cnt
                return ("dma", p.dma_key), 0, 16 * n
            e = (p.cnt - 1) // EPOCH
            return p.eng, e, (p.cnt - 1) % EPOCH + 1

        waited = {e: {} for e in self.ENGS}
        for op in self.ops:
            w = waited[op.eng]
            for c, i in op.deps.items():
                chan, ep, val = target(self.ops[i])
                if w.get(chan, (-1, 0)) >= (ep, val):
                    continue
                w[chan] = (ep, val)
                op.waits.append((sem_for((chan, ep)), val))
        for op in self.ops:
            if op.dma_key is not None:
                op.signal = (sem_for((("dma", op.dma_key), 0)), 16)
            elif op.signal:
                _, ep, _ = target(op)
                op.signal = (sem_for((op.eng, ep)), 1)
            else:
                op.signal = None

    def emit_all(self, nc):
        by_eng = {e: [o for o in self.ops if o.eng == e] for e in self.ENGS}

        def run(engine, ops):
            for op in ops:
                for sem, val in op.waits:
                    engine.wait_ge(sem, val)
                ins = op.emit(engine)
                if op.signal is not None:
                    ins.then_inc(op.signal[0], op.signal[1])

        with nc.Block() as block:
            @block.sync
            def _(e):
                run(e, by_eng["sp"])

            @block.tensor
            def _(e):
                run(e, by_eng["pe"])

            @block.scalar
            def _(e):
                run(e, by_eng["act"])

            @block.vector
            def _(e):
                run(e, by_eng["dve"])

            @block.gpsimd
            def _(e):
                run(e, by_eng["pool"])


def _consts():
    j = np.arange(128)
    c = {}
    c["ident_bf"] = np.eye(128, dtype=np.float32).astype(ml_dtypes.bfloat16)
    c["ident_f"] = np.eye(128, dtype=np.float32)
    c["ones_f"] = np.ones((128, 128), np.float32)
    c["maskT_bf"] = (j[None, :] >= j[:, None]).astype(np.float32).astype(ml_dtypes.bfloat16)
    c["triI_f"] = np.where(j[:, None] <= j[None, :], -1.0 / 16.0, 0.0).astype(np.float32)
    c["triU_f"] = np.where(j[:, None] > j[None, :], -1.0 / 16.0, 0.0).astype(np.float32)
    bt = np.zeros((128, 4, 16), np.float32)
    for h in range(4):
        for d in range(16):
            bt[:, h, d] = SLOPES[h] * (j - 127 - 128 * d)
    c["bias_tab"] = bt.reshape(128, 64)
    return c


CONST_SPECS = [
    ("ident_bf", [128, 128], BF16), ("ident_f", [128, 128], F32), ("ones_f", [128, 128], F32),
    ("maskT_bf", [128, 128], BF16), ("triI_f", [128, 128], F32), ("triU_f", [128, 128], F32),
    ("bias_tab", [128, 64], F32),
]


def build_program(upto="all", dumps=()):
    nc = bass.Bass("TRN2", target_bir_lowering=False)
    pg = Prog()
    stack = ExitStack()
    dram = {}

    def din(name, shape, dt=F32):
        dram[name] = nc.dram_tensor(name, list(shape), dt, kind="ExternalInput").ap()
        return dram[name]

    x_d = din("x", [NSEQ * S, D])
    cT_d = din("cT", [128, 16])
    w_ada_d = din("w_ada", [D, 3 * D])
    b_adaT_d = din("b_adaT", [128, 24])
    ngain_d = din("ngainT", [128, 8])
    w_in_d = din("w_in", [D, D_IN])
    wup_d = din("wup_aug", [17, 256])
    ggain_d = din("ggainT", [128, 4])
    dgain_d = din("dgainT", [128, 4])
    lam_d = din("lam_bc", [128, 256])
    w_out_d = din("w_out", [D, D])
    fgain_d = din("fgain_bc", [128, D])
    cd = {n: din(n, shp, dt) for n, shp, dt in CONST_SPECS}
    y_d = nc.dram_tensor("y", [NSEQ * S, D], F32, kind="ExternalOutput").ap()
    dump_d = {}
    for name, shape in dumps:
        dump_d[name] = nc.dram_tensor("dbg_" + name, list(shape), F32, kind="ExternalOutput").ap()

    def sb(name, shape, dt):
        return stack.enter_context(nc.sbuf_tensor("sb_" + name, list(shape), dt))

    stack.enter_context(nc.allow_low_precision("bf16 matmul operands, fp32 accumulation"))

    w_in_sb = sb("w_in_sb", [128, 8, D_IN], BF16)
    Rbuf = sb("Rbuf", [128, max(8 * S, 16 * D)], BF16)
    hT = Rbuf[:, 0:8 * S].rearrange("p (k t) -> p k t", k=8)
    w_out_sb = Rbuf[:, 0:8 * D].rearrange("p (k n) -> p k n", k=8)
    mixbuf = sb("mixbuf", [128, max(8 * S, 16384)], BF16)
    mixT = mixbuf[:, 0:8 * S].rearrange("p (k t) -> p k t", k=8)
    NXT = 2
    xt = [sb("xt%d" % i, [128, D], F32) for i in range(NXT)]
    xnb_all = sb("xnb_all", [128, 2 * D], BF16)
    xnb = [xnb_all[:, 0:D], xnb_all[:, D:2 * D]]
    ybuf3 = xnb_all[:, :].bitcast(F32)
    ybuf = [sb("ybuf%d" % i, [128, D], F32) for i in range(2)]
    junk = ybuf[1]
    gate_bc = sb("gate_bc", [128, D], F32)
    fgain_sb = sb("fgain_sb", [128, D], F32)
    csb = {n: sb(n, shp, dt) for n, shp, dt in CONST_SPECS}
    cT_sb = sb("cT_sb", [128, 16], F32)
    b_adaT_sb = sb("b_adaT_sb", [128, 24], F32)
    ngain_sb = sb("ngain_sb", [128, 8], F32)
    ggain_sb = sb("ggain_sb", [128, 4], F32)
    dgain_sb = sb("dgain_sb", [128, 4], F32)
    lam_sb = sb("lam_sb", [128, 256], F32)
    wup_f = sb("wup_f", [32, 256], F32)
    wup_bf = sb("wup_bf", [32, 256], BF16)
    small = sb("small", [128, 256], F32)
    modT = sb("modT", [128, 48], F32)
    acol = sb("acol", [128, 16], F32)
    scol = sb("scol", [128, 16], F32)
    gcol = sb("gcol", [128, 16], F32)
    diagg = sb("diagg", [128, 128], F32)
    arena = sb("arena", [128, 24576], BF16)
    ps = stack.enter_context(nc.psum_tensor("ps", [128, 4096], F32))

    def bank(b, n=512, off=0):
        return ps[:, b * 512 + off: b * 512 + off + n]

    def bank_bf(b):
        return ps[:, b * 512:(b + 1) * 512].bitcast(BF16)

    SM = {}
    _sm_next = [0]

    def smcol(name, n=1):
        SM[name] = small[:, _sm_next[0]:_sm_next[0] + n]
        _sm_next[0] += n
        return SM[name]

    eps_col = smcol("eps_col", 1)

    def dma(queue, out, in_, key, reads=(), writes=(), bulk=False, **kw):
        pg.add(queue, lambda e: e.dma_start(out=out, in_=in_, **kw), reads=reads, writes=writes,
               dma_key=key, bulk=bulk)

    def dump(name, src_ap, reads):
        if name in dump_d:
            dma("pool", dump_d[name], src_ap, ("dump", name), reads=reads, max_dma_last_dim=4096)

    def act(out, in_, func, reads, writes, excl=(), bias=0.0, scale=1.0, accum=None):
        def f(e):
            kw = {}
            if accum is not None:
                kw["accum_out"] = accum
            return e.activation(out=out, in_=in_, func=func, bias=bias, scale=scale, **kw)
        pg.add("act", f, reads=reads, writes=writes, excl=excl)

    def tt(eng, out, in0, in1, op, reads, writes, excl=()):
        pg.add(eng, lambda e: e.tensor_tensor(out=out, in0=in0, in1=in1, op=op),
               reads=reads, writes=writes, excl=excl)

    def ts(eng, out, in0, s1, s2, op0, op1, reads, writes, excl=()):
        if s2 is None:
            pg.add(eng, lambda e: e.tensor_scalar(out=out, in0=in0, scalar1=s1, scalar2=None, op0=op0),
                   reads=reads, writes=writes, excl=excl)
        else:
            pg.add(eng, lambda e: e.tensor_scalar(out=out, in0=in0, scalar1=s1, scalar2=s2, op0=op0, op1=op1),
                   reads=reads, writes=writes, excl=excl)

    def stt(out, in0, scalar, in1, op0, op1, reads, writes, excl=()):
        pg.add("dve", lambda e: e.scalar_tensor_tensor(out=out, in0=in0, scalar=scalar, in1=in1, op0=op0, op1=op1),
               reads=reads, writes=writes, excl=excl)

    def copy(eng, out, in_, reads, writes, excl=()):
        pg.add(eng, lambda e: e.tensor_copy(out=out, in_=in_), reads=reads, writes=writes, excl=excl)

    def mm_group(out, pairs, reads, excl, writes=(), first_start=True, last_stop=True):
        def f(e):
            ins = None
            n = len(pairs)
            for i, (l, r) in enumerate(pairs):
                ins = e.matmul(out, l, r, start=(first_start and i == 0), stop=(last_stop and i == n - 1),
                               skip_group_check=not (first_start and last_stop))
            return ins
        pg.add("pe", f, reads=reads, writes=writes, excl=excl)

    def rstd_from_ss(ss, rstd, n, key_ss, key_rstd):
        act(ss, ss, AF.Ln, reads=[key_ss, "eps_col"], writes=[key_ss], scale=1.0 / n, bias=eps_col)
        act(rstd, ss, AF.Exp, reads=[key_ss], writes=[key_rstd], scale=-0.5)

    pg.add("pool", lambda e: e.memset(eps_col, EPS), reads=[], writes=["eps_col"])
    for n, shp, dt in CONST_SPECS:
        dma("sp", csb[n][:, :], cd[n], "const", writes=[n], bulk=True)
    dma("sp", cT_sb[:, :], cT_d, "const", writes=["cT"], bulk=True)
    dma("sp", b_adaT_sb[:, :], b_adaT_d, "const", writes=["b_adaT"], bulk=True)
    dma("sp", ngain_sb[:, :], ngain_d, "const", writes=["ngain"], bulk=True)
    dma("sp", ggain_sb[:, :], ggain_d, "const", writes=["ggain"], bulk=True)
    dma("sp", dgain_sb[:, :], dgain_d, "const", writes=["dgain"], bulk=True)
    dma("sp", lam_sb[:, :], lam_d, "const", writes=["lam_in"], bulk=True)
    dma("sp", wup_f[0:17, :], wup_d, "const", writes=["wup_f"], bulk=True)
    dma("sp", fgain_sb[:, :], fgain_d, "const", writes=["fgain"], bulk=True)
    copy("dve", wup_bf[0:17, :], wup_f[0:17, :], reads=["wup_f"], writes=["wup_bf"])

    sc = smcol("sc", 16)
    tmp16 = smcol("tmp16", 16)
    act(tmp16, cT_sb[:, :], AF.Exp, reads=["cT"], writes=["tmp16"], scale=-1.0)
    ts("dve", tmp16, tmp16, 1.0, None, ALU.add, None, reads=["tmp16"], writes=["tmp16"])
    pg.add("dve", lambda e: e.reciprocal(out=tmp16, in_=tmp16), reads=["tmp16"], writes=["tmp16"])
    tt("dve", sc, cT_sb[:, :], tmp16, ALU.mult, reads=["cT", "tmp16"], writes=["sc"])

    sc_bf = sb("sc_bf", [128, 16], BF16)
    copy("dve", sc_bf[:, :], sc, reads=["sc"], writes=["sc_bf"])
    slab = [mixbuf[:, 0:8192].bitcast(F32).rearrange("p (k n) -> p k n", k=8),
            mixbuf[:, 8192:16384].bitcast(F32).rearrange("p (k n) -> p k n", k=8)]
    slab_bf = [Rbuf[:, 0:4096].rearrange("p (k n) -> p k n", k=8), Rbuf[:, 4096:8192].rearrange("p (k n) -> p k n", k=8)]
    w_ada_v = w_ada_d.rearrange("(k p) n -> p k n", p=128)
    for sl in range(6):
        i2 = sl % 2
        dma("sp", slab[i2], w_ada_v[:, :, sl * 512:(sl + 1) * 512], ("slab", i2), writes=[("slab", i2)])
        copy("dve", slab_bf[i2], slab[i2], reads=[("slab", i2)], writes=[("slabbf", i2)])
        for jj in range(4):
            j = sl * 4 + jj
            pairs = [(slab_bf[i2][:, kc, jj * 128:(jj + 1) * 128], sc_bf[:, kc * 2:(kc + 1) * 2]) for kc in range(8)]
            mm_group(bank(0, 2, j * 2), pairs, reads=[("slabbf", i2), "sc_bf"], excl=[("ps", 0)])
    tt("dve", modT[:, :].rearrange("p (j b) -> p j b", b=2), bank(0, 48).rearrange("p (j b) -> p j b", b=2),
       b_adaT_sb[:, :].unsqueeze(2).to_broadcast([128, 24, 2]), ALU.add,
       reads=["b_adaT"], writes=["modT", ("slab", 0), ("slab", 1), ("slabbf", 0), ("slabbf", 1)], excl=[("ps", 0)])
    ts("dve", acol[:, :], modT[:, 16:32], 1.0, None, ALU.add, None, reads=["modT"], writes=["acol"])
    tt("dve", acol[:, :].rearrange("p (k b) -> p k b", b=2), acol[:, :].rearrange("p (k b) -> p k b", b=2),
       ngain_sb[:, :].unsqueeze(2).to_broadcast([128, 8, 2]), ALU.mult, reads=["acol", "ngain"], writes=["acol"])
    copy("dve", scol[:, :], modT[:, 0:16], reads=["modT"], writes=["scol"])
    copy("dve", gcol[:, :], modT[:, 32:48], reads=["modT"], writes=["gcol"])
    dump("acol", acol[:, :], ["acol"])
    dump("scol", scol[:, :], ["scol"])
    dump("gcol", gcol[:, :], ["gcol"])

    lam_t = smcol("lam_t", 2)
    neg_lam = smcol("neg_lam", 1)
    lamprod = smcol("lamprod", 128)
    tt("dve", lamprod.rearrange("p (a d) -> p a d", a=2), lam_sb[:, :].rearrange("p (a t d) -> p a t d", a=2, t=2)[:, :, 0, :],
       lam_sb[:, :].rearrange("p (a t d) -> p a t d", a=2, t=2)[:, :, 1, :], ALU.mult,
       reads=["lam_in"], writes=["lamprod"])
    pg.add("dve", lambda e: e.tensor_reduce(out=lam_t, in_=lamprod.rearrange("p (a d) -> p a d", a=2), axis=AX.X, op=ALU.add),
           reads=["lamprod"], writes=["lam_t"])
    act(lam_t, lam_t, AF.Exp, reads=["lam_t"], writes=["lam_t"])
    stt(neg_lam, lam_t[:, 1:2], -LAM_INIT, lam_t[:, 0:1], ALU.add, ALU.subtract, reads=["lam_t"], writes=["neg_lam"])
    dump("neg_lam", neg_lam, ["neg_lam"])

    wst = [arena[:, i * 7200:(i + 1) * 7200].bitcast(F32) for i in range(3)]
    for kc in range(8):
        sl = kc % 3
        dma("sp", wst[sl], w_in_d[kc * 128:(kc + 1) * 128, :], ("wst", sl), writes=[("A:wst", sl)])
        copy("dve", w_in_sb[:, kc, :], wst[sl], reads=[("A:wst", sl)], writes=[("win", kc)])

    xslot = [0]

    def load_x(tok0):
        s = xslot[0] % NXT
        xslot[0] += 1
        dma("sp", xt[s][:, :], x_d[tok0:tok0 + 128, :], ("xt", s), writes=[("xt", s)])
        return s

    ss_c = smcol("ss_c", 1)
    rstd_c = smcol("rstd_c", 1)
    ss_p = [smcol("ss_p%d" % i, 1) for i in range(3)]
    rstd_p = [smcol("rstd_p%d" % i, 1) for i in range(3)]
    ss4_p = [smcol("ss4_p%d" % i, 4) for i in range(2)]
    rstd4_p = [smcol("rstd4_p%d" % i, 4) for i in range(2)]


    NG = S // 512
    arena_f = arena[:, :].bitcast(F32)
    dummy = smcol("dummy", 1)
    ipb = [0]

    def phase_barrier():
        pg.add("dve", lambda e: e.memset(dummy, 0.0), reads=[], writes=["AR", "dummy"])

    def win_keys(col):
        return [("win", kc) for kc in range(8)]

    evq = [0]

    def evac_copy(out, in_, reads, writes, excl):
        evq[0] += 1
        if evq[0] % 2 == 0:
            act(out, in_, AF.Copy, reads=reads, writes=writes, excl=excl)
        else:
            copy("dve", out, in_, reads=reads, writes=writes, excl=excl)

    ipbanks = [2]
    ipbl = [[0, 1]]

    def inproj_fm_thunks(col0, M, evac):
        def mk(g):
            def f():
                bk = ipbl[0][ipb[0] % min(ipbanks[0], len(ipbl[0]))]
                ipb[0] += 1
                pairs = [(w_in_sb[:, kc, col0:col0 + M], hT[:, kc, g * 512:(g + 1) * 512]) for kc in range(8)]
                mm_group(bank(bk)[0:M, :], pairs, reads=win_keys(col0) + [("hT", g, "d"), ("hT", g, "a")], excl=[("ps", bk)])
                evac(g, bank(bk)[0:M, :], bk)
            return f
        return [mk(g) for g in range(NG)]

    def inproj_fm(col0, M, evac):
        for f in inproj_fm_thunks(col0, M, evac):
            f()

    def inproj_fm_pairs(col0, M, evac):
        def mk(g):
            def mm(bk):
                pairs = [(w_in_sb[:, kc, col0:col0 + M], hT[:, kc, g * 512:(g + 1) * 512]) for kc in range(8)]
                mm_group(bank(bk)[0:M, :], pairs, reads=win_keys(col0) + [("hT", g, "d"), ("hT", g, "a")], excl=[("ps", bk)])
            def ev(bk):
                evac(g, bank(bk)[0:M, :], bk)
            return (mm, ev)
        return [mk(g) for g in range(NG)]

    def inproj_fm_pieces(col0, M, evac, npieces=4):
        out_list = []
        per = 8 // npieces
        for g in range(NG):
            for pc in range(npieces):
                def f(g=g, pc=pc):
                    bk = 0
                    pairs = [(w_in_sb[:, kc, col0:col0 + M], hT[:, kc, g * 512:(g + 1) * 512])
                             for kc in range(pc * per, (pc + 1) * per)]
                    mm_group(bank(bk)[0:M, :], pairs, reads=win_keys(col0) + [("hT", g, "d"), ("hT", g, "a")], excl=[("ps", bk)],
                             first_start=(pc == 0), last_stop=(pc == npieces - 1))
                    if pc == npieces - 1:
                        evac(g, bank(bk)[0:M, :], bk)
                out_list.append(f)
        return out_list

    def inproj_tm_pairs(col0, evac):
        def mk(t):
            def mm(bk):
                pairs = [(hT[:, kc, t * 128:(t + 1) * 128], w_in_sb[:, kc, col0:col0 + 512]) for kc in range(8)]
                mm_group(bank(bk), pairs, reads=win_keys(col0) + [("hT", t // 4, "d"), ("hT", t // 4, "a")], excl=[("ps", bk)])
            def ev(bk):
                evac(t, bank(bk), bk)
            return (mm, ev)
        return [mk(t) for t in range(NT)]

    def inproj_tm_thunks(col0, evac):
        def mk(t):
            def f():
                bk = ipbl[0][ipb[0] % min(ipbanks[0], len(ipbl[0]))]
                ipb[0] += 1
                pairs = [(hT[:, kc, t * 128:(t + 1) * 128], w_in_sb[:, kc, col0:col0 + 512]) for kc in range(8)]
                mm_group(bank(bk), pairs, reads=win_keys(col0) + [("hT", t // 4, "d"), ("hT", t // 4, "a")], excl=[("ps", bk)])
                evac(t, bank(bk), bk)
            return f
        return [mk(t) for t in range(NT)]

    def inproj_tm(col0, evac):
        for f in inproj_tm_thunks(col0, evac):
            f()

    gqT = arena[:, 0:2 * S].rearrange("p (c t) -> p c t", c=2)
    gkT = arena[:, 2 * S:4 * S].rearrange("p (c t) -> p c t", c=2)
    gv = arena[:, 4 * S:8 * S].rearrange("p (t n) -> p t n", n=512)
    grT = arena[0:32, 8 * S:9 * S]
    tb = 9 * S
    tf = tb // 2
    laA = [arena_f[:, tf + i * 512:tf + (i + 1) * 512] for i in range(2)]
    ebA = [arena_f[:, tf + 1024 + i * 512:tf + 1024 + (i + 1) * 512] for i in range(2)]
    enbA = [arena_f[:, tf + 2048 + i * 512:tf + 2048 + (i + 1) * 512] for i in range(2)]
    assert 2 * (tf + 3072) <= 24576
    kinT = [arena[:, tb + i * 256:tb + (i + 1) * 256] for i in range(2)]
    AT = [arena[:, tb + 512 + i * 512:tb + 512 + (i + 1) * 512].rearrange("p (h t) -> p h t", h=4) for i in range(2)]
    on = [arena[:, tb + 1536 + i * 512:tb + 1536 + (i + 1) * 512].rearrange("p (h t) -> p h t", h=4) for i in range(2)]
    S_bf = arena[:, tb + 2560:tb + 2816].rearrange("p (c t) -> p c t", c=2)
    fB = (tb + 2816) // 2
    S32 = arena_f[:, fB:fB + 256]
    Sd = arena_f[:, fB + 256:fB + 512]
    osq = arena_f[:, fB + 512:fB + 1024]
    assert 2 * (fB + 1024) <= 24576
    dec_all = smcol("dec_all", 2 * NT)
    ss4 = smcol("ss4", 4)
    rstd4 = smcol("rstd4", 4)
    QORD = [0, 2, 1, 3]

    def dec_col(t, c):
        i = (t // 2) * 4 + c * 2 + (t % 2)
        return dec_all[:, i:i + 1]

    def gla_begin(b):
        phase_barrier()
        pg.add("pool", lambda e: e.memset(grT, 1.0), reads=[], writes=[("A:gr",)])
        per_g = {g: [] for g in range(NG)}
        def add_fm(col0, M, evac):
            for g, f in enumerate(inproj_fm_pairs(col0, M, evac)):
                per_g[g].append(f)
        add_fm(C_GR, 16, lambda g, p, bk: copy("dve", grT[0:16, g * 512:(g + 1) * 512], p, reads=[], writes=[("A:gr",)], excl=[("ps", bk)]))
        for c in range(2):
            add_fm(C_GQ + c * 128, 128, lambda g, p, bk, c=c: evac_copy(
                gqT[:, c, g * 512:(g + 1) * 512], p, reads=[], writes=[("A:gq", c, g)], excl=[("ps", bk)]))
            add_fm(C_GK + c * 128, 128, lambda g, p, bk, c=c: evac_copy(
                gkT[:, c, g * 512:(g + 1) * 512], p, reads=[], writes=[("A:gk", c, g)], excl=[("ps", bk)]))
        for hc in range(4):
            add_fm(C_GZ + hc * 128, 128, lambda g, p, bk, hc=hc: evac_copy(
                mixT[:, hc, g * 512:(g + 1) * 512], p, reads=[], writes=[("mix", hc, g)], excl=[("ps", bk)]))
        for t, f in enumerate(inproj_tm_pairs(C_GV, lambda t, p, bk: evac_copy(gv[:, t, :], p, reads=[], writes=[("A:gv", t)], excl=[("ps", bk)]))):
            per_g[t // 4].append(f)
        return per_g

    def gla_gate():
        for hc in range(4):
            mk = [("mix", hc, g) for g in range(NG)]
            dst = mixT[:, hc, :]
            act(dst, dst, AF.Silu, reads=mk, writes=mk)
            ts("dve", dst, dst, ggain_sb[:, hc:hc + 1], None, ALU.mult, None, reads=mk + ["ggain"], writes=mk)

    def run_gla(b, iptail):
        ipbanks[0] = 2
        ipbl[0] = [0, 1]
        ip_step, need_group, ipq, pend_ev = iptail
        gv_thunks = []

        def zstage(p):
            pp = p % 2
            zb = 2 + 2 * pp
            def fz(e):
                ins = None
                for j in range(2):
                    tok = slice(p * 256 + j * 128, p * 256 + (j + 1) * 128)
                    ins = e.matmul(bank(zb, 256, j * 256), grT[0:17, tok], wup_bf[0:17, :], start=True, stop=True)
                return ins
            pg.add("pe", fz, reads=[("A:gr",), "wup_bf"], excl=[("ps", zb)])
            act(laA[pp], bank(zb), AF.Exp, reads=[], writes=[("A:la", pp)], excl=[("ps", zb)], scale=-1.0)
            act(laA[pp], laA[pp], AF.Ln, reads=[("A:la", pp)], writes=[("A:la", pp)], bias=1.0)

        def cstage(p):
            pp = p % 2
            g = p // 2
            bb = 3 + 2 * pp
            tk = slice(p * 256, (p + 1) * 256)
            def fcs(e):
                ins = None
                for c in range(2):
                    for j in range(2):
                        ins = e.matmul(bank(bb, 128, (c * 2 + j) * 128), laA[pp][:, j * 256 + c * 128:j * 256 + (c + 1) * 128],
                                       csb["triI_f"][:, :], start=True, stop=True)
                return ins
            pg.add("pe", fcs, reads=[("A:la", pp), "triI_f"], excl=[("ps", bb)])
            act(ebA[pp], bank(bb), AF.Exp, reads=[], writes=[("A:eb", pp)], excl=[("ps", bb)], bias=math.log(0.125))
            act(enbA[pp], bank(bb), AF.Exp, reads=[], writes=[("A:enb", pp)], excl=[("ps", bb)], scale=-1.0)
            act(dec_all[:, p * 4:(p + 1) * 4].unsqueeze(2), bank(bb).rearrange("p (i t) -> p i t", i=4)[:, :, 127:128], AF.Exp,
                reads=[], writes=[("dec", p)], excl=[("ps", bb)])
            tt("pool", gqT[:, :, tk], gqT[:, :, tk], ebA[pp].rearrange("p (c t) -> p c t", c=2), ALU.mult,
               reads=[("A:gq", 0, g), ("A:gq", 1, g), ("A:eb", pp)], writes=[("A:qin", p)])
            tt("pool", gkT[:, :, tk], gkT[:, :, tk], enbA[pp].rearrange("p (c t) -> p c t", c=2), ALU.mult,
               reads=[("A:gk", 0, g), ("A:gk", 1, g), ("A:enb", pp)], writes=[("A:kin", p)])

        NP = NT // 2
        need_group(0)
        zstage(0)
        for p in range(NP):
            if p + 1 < NP:
                need_group((p + 1) // 2)
                zstage(p + 1)
            if ipq or pend_ev:
                ip_step(2)
                ip_step(2)
            cstage(p)
        while ipq or pend_ev:
            ip_step(2)
        gla_gate()
        pg.add("dve", lambda e: e.memset(S32, 0.0),
               reads=[("A:la", 0), ("A:la", 1), ("A:eb", 0), ("A:eb", 1), ("A:enb", 0), ("A:enb", 1)],
               writes=[("A:S32",), ("A:la", 0), ("A:la", 1), ("A:eb", 0), ("A:eb", 1), ("A:enb", 0), ("A:enb", 1)])
        pg.add("pool", lambda e: e.memset(Sd, 0.0), reads=[("A:S32",)], writes=[("A:Sd",)])
        bkey = [("A:la", 0)]

        def pre(t):
            tok = slice(t * 128, (t + 1) * 128)
            p = t // 2
            par = t % 2
            ktp = bank_bf(1)[:, 0:256]
            def fkt(e):
                e.transpose(ktp[:, 0:128], gkT[:, 0, tok], csb["ident_bf"][:, :])
                return e.transpose(ktp[:, 128:256], gkT[:, 1, tok], csb["ident_bf"][:, :])
            pg.add("pe", fkt, reads=[("A:kin", p), "ident_bf"], excl=[("ps", 1)])
            act(kinT[par], ktp, AF.Copy, reads=bkey, writes=[("A:kinT", par)], excl=[("ps", 1)])
            def fsc(e):
                ins = None
                for h in range(4):
                    c = h // 2
                    rows = slice(0, 64) if h % 2 == 0 else slice(64, 128)
                    ins = e.matmul(bank(2 + h % 2, 128, c * 128), gkT[rows, c, tok], gqT[rows, c, tok], start=True, stop=True)
                return ins
            pg.add("pe", fsc, reads=[("A:kin", p), ("A:qin", p)], excl=[("ps", 2), ("ps", 3)])
            for q in range(2):
                tt("dve", AT[par][:, q::2, :], bank(2 + q, 256).rearrange("p (c t) -> p c t", c=2),
                   csb["maskT_bf"][:, :].unsqueeze(1).to_broadcast([128, 2, 128]), ALU.mult,
                   reads=["maskT_bf"] + bkey, writes=[("A:AT", par, q)], excl=[("ps", 2 + q)])

        def obanks(t):
            return (6, 7) if t % 2 == 0 else (4, 5)

        def main_o(t):
            tok = slice(t * 128, (t + 1) * 128)
            p = t // 2
            par = t % 2
            ob = obanks(t)
            def fo_(e):
                ins = None
                for h in range(4):
                    ins = e.matmul(bank(ob[h % 2], 128, (h // 2) * 128), AT[par][:, h, :], gv[:, t, h * 128:(h + 1) * 128],
                                   start=(h < 2), stop=(t == 0), skip_group_check=True)
                if t > 0:
                    for h in range(4):
                        c = h // 2
                        rows = slice(0, 64) if h % 2 == 0 else slice(64, 128)
                        ins = e.matmul(bank(ob[h % 2], 128, c * 128), gqT[rows, c, tok], S_bf[rows, c, :],
                                       start=False, stop=True, skip_group_check=True)
                return ins
            pg.add("pe", fo_, reads=[("A:AT", par, 0), ("A:AT", par, 1), ("A:gv", t), ("A:qin", p), ("A:Sbf",)],
                   excl=[("ps", ob[0]), ("ps", ob[1])])
            ssk = "ss4_p%d" % par
            for q in range(4):
                act(osq[:, q * 128:(q + 1) * 128], bank(ob[q // 2], 128, (q % 2) * 128), AF.Square, reads=bkey,
                    writes=[("A:osq", q), (ssk, q)], excl=[("ps", ob[q // 2])], accum=ss4_p[par][:, q:q + 1])

        def main_kv(t):
            p = t // 2
            par = t % 2
            if t < NT - 1:
                def fkv(e):
                    ins = None
                    for h in range(4):
                        c = h // 2
                        rows = slice(0, 64) if h % 2 == 0 else slice(64, 128)
                        ins = e.matmul(bank(0)[rows, c * 128:(c + 1) * 128], kinT[par][:, h * 64:(h + 1) * 64],
                                       gv[:, t, h * 128:(h + 1) * 128], start=True, stop=True)
                    return ins
                pg.add("pe", fkv, reads=[("A:kinT", par), ("A:gv", t)], excl=[("ps", 0)])
                for c in range(2):
                    cs = slice(c * 128, (c + 1) * 128)
                    stt(S32[:, cs], bank(0, 128, c * 128), dec_col(t, c), Sd[:, cs], ALU.mult, ALU.add,
                        reads=[("A:Sd",), ("dec", p)], writes=[("A:S32",)], excl=[("ps", 0)])
                act(S_bf, S32.rearrange("p (c t) -> p c t", c=2), AF.Copy, reads=[("A:S32",)], writes=[("A:Sbf",)])
                if t + 1 < NT - 1:
                    for c in range(2):
                        cs = slice(c * 128, (c + 1) * 128)
                        ts("dve", Sd[:, cs], S32[:, cs], dec_col(t + 1, c), None, ALU.mult, None,
                           reads=[("A:S32",), ("dec", (t + 1) // 2)], writes=[("A:Sd",)])

        def post(t):
            tok = slice(t * 128, (t + 1) * 128)
            g = t // 4
            par = t % 2
            ob = obanks(t)
            ssk, rsk = "ss4_p%d" % par, "rstd4_p%d" % par
            act(ss4_p[par], ss4_p[par], AF.Ln, reads=[(ssk, q) for q in range(4)] + ["eps_col"], writes=[ssk],
                scale=1.0 / 128, bias=eps_col)
            act(rstd4_p[par], ss4_p[par], AF.Exp, reads=[ssk], writes=[rsk], scale=-0.5)
            for q in range(2):
                tt("dve", on[par][:, q * 2:(q + 1) * 2, :], bank(ob[q], 256).rearrange("p (c t) -> p c t", c=2),
                   rstd4_p[par][:, q * 2:(q + 1) * 2].unsqueeze(2).to_broadcast([128, 2, 128]), ALU.mult,
                   reads=[rsk] + bkey, writes=[("A:on", par, q)], excl=[("ps", ob[q])])
            tp = bank_bf(1)[:, 256:768]
            def ftp(e):
                ins = None
                for q in range(4):
                    h = QORD[q]
                    ins = e.transpose(tp[:, h * 128:(h + 1) * 128], on[par][:, q, :], csb["ident_bf"][:, :])
                return ins
            pg.add("pe", ftp, reads=[("A:on", par, 0), ("A:on", par, 1), "ident_bf"], excl=[("ps", 1)])
            tt("dve", mixT[:, 0:4, tok], tp.rearrange("p (h t) -> p h t", h=4), mixT[:, 0:4, tok], ALU.mult,
               reads=[("mix", hc, g) for hc in range(4)], writes=[("mix", hc, g) for hc in range(4)], excl=[("ps", 1)])

        pre(0)
        for t in range(NT):
            main_o(t)
            main_kv(t)
            if t + 1 < NT:
                pre(t + 1)
            post(t)

    dv_aug = arena[:, 0:NT * 520].rearrange("p (t h n) -> p t h n", h=4, n=130)
    do = NT * 520
    dqT = [arena[:, do + i * S:do + (i + 1) * S] for i in range(2)]
    dkT = [arena[:, do + (2 + i) * S:do + (3 + i) * S] for i in range(2)]
    po = do + 4 * S
    PT = [[arena[:, po + (2 * i + m) * 512:po + (2 * i + m + 1) * 512] for m in range(2)] for i in range(2)]
    fo2 = (po + 2048) // 2
    Oc = arena_f[:, fo2:fo2 + 8 * 129].rearrange("p (i n) -> p i n", n=129)
    Dm = arena_f[:, fo2 + 1032:fo2 + 1544].rearrange("p (s n) -> p s n", s=4)
    t2 = arena_f[:, fo2 + 1544:fo2 + 2056].rearrange("p (s n) -> p s n", s=4)
    dno = 2 * (fo2 + 2056)
    Dn = arena[:, dno:dno + 512].rearrange("p (s n) -> p s n", s=4)
    assert dno + 512 <= 24576
    rz = smcol("rz", 8)
    SBANKS = [(2, 3), (7, 1)]
    astep = [0]

    def obank(i):
        return 4 + i // 3, (i % 3) * 129

    def attention(b, h, buf, hook=None, fin_q=None, flush=True):
        qsub = 256 if SLOPES[h] * 511 > 40 else 512
        steps = [(G, kb) for G in range(NG) for kb in range(4 * G + 4)]
        started = {}
        pend = []

        def do_qk_exp(G, kb):
            t = kb - 4 * G
            s0 = max(t, 0)
            c0 = 128 * s0
            ks = slice(kb * 128, (kb + 1) * 128)
            qs = slice(G * 512 + c0, (G + 1) * 512)
            pb = astep[0] % 2
            astep[0] += 1
            sb0, sb1 = SBANKS[pb]
            def fqk(e):
                e.matmul(bank(sb0)[:, c0:512], dkT[buf][0:64, ks], dqT[buf][0:64, qs], start=True, stop=True)
                return e.matmul(bank(sb1)[:, c0:512], dkT[buf][64:128, ks], dqT[buf][64:128, qs], start=True, stop=True)
            pg.add("pe", fqk, reads=[("A:dk", buf, kb // 4), ("A:dq", buf, G)], excl=[("ps", sb0), ("ps", sb1)])
            for m in range(2):
                sbm = (sb0, sb1)[m]
                if qsub == 512:
                    chunks = [(c0, 512)]
                else:
                    chunks = [(max(c0, lo), lo + qsub) for lo in range(0, 512, qsub) if c0 < lo + qsub]
                for (a, bnd) in chunks:
                    delta = 4 * G + (bnd - 1) // 128 - kb
                    act(PT[pb][m][:, a:bnd], bank(sbm)[:, a:bnd], AF.Exp, reads=["bias_tab"], writes=[("A:PT", pb, m)],
                        excl=[("ps", sbm)], bias=csb["bias_tab"][:, h * 16 + delta:h * 16 + delta + 1], scale=0.125)
                if t >= 0:
                    tt("dve", PT[pb][m][:, c0:c0 + 128], PT[pb][m][:, c0:c0 + 128], csb["maskT_bf"][:, :], ALU.mult,
                       reads=[("A:PT", pb, m), "maskT_bf"], writes=[("A:PT", pb, m)])
            return pb, s0

        def do_pv(G, kb, pb, s0):
            st_set = started.setdefault(G, set())
            plan = []
            for m in range(2):
                for s in range(s0, 4):
                    bk, off = obank(4 * m + s)
                    plan.append((bk, off, m, s, bk not in st_set, kb == 4 * G + s))
                    st_set.add(bk)
            def fpv(e):
                ins = None
                for (bk, off, m, s, st, sp) in plan:
                    ins = e.matmul(bank(bk)[:, off:off + 129], PT[pb][m][:, s * 128:(s + 1) * 128], dv_aug[:, kb, h, 0:129],
                                   start=st, stop=sp, skip_group_check=True)
                return ins
            pg.add("pe", fpv, reads=[("A:PT", pb, 0), ("A:PT", pb, 1), ("A:dv", kb)], excl=[("ps", 4), ("ps", 5), ("ps", 6)])
            if kb == 4 * G + 3:
                finalize(G)

        if fin_q is None:
            fin_q = []

        def finalize(G):
            while any(f_ is not None and getattr(f_, "is_f2", False) for f_ in fin_q):
                f_ = fin_q.pop(0)
                if f_ is not None:
                    f_()
            copy("dve", Oc[:, 0:3, :], bank(4, 387).rearrange("p (i n) -> p i n", n=129), reads=[], writes=[("A:Oc", 0)], excl=[("ps", 4)])
            copy("dve", Oc[:, 3:6, :], bank(5, 387).rearrange("p (i n) -> p i n", n=129), reads=[], writes=[("A:Oc", 1)], excl=[("ps", 5)])
            copy("dve", Oc[:, 6:8, :], bank(6, 258).rearrange("p (i n) -> p i n", n=129), reads=[], writes=[("A:Oc", 2)], excl=[("ps", 6)])
            ock = [("A:Oc", i) for i in range(3)]

            def F2():
                pg.add("dve", lambda e: e.reciprocal(out=rz.unsqueeze(2), in_=Oc[:, :, 128:129]), reads=ock, writes=["rz"])
                ts("dve", rz[:, 4:8], rz[:, 4:8], neg_lam, None, ALU.mult, None, reads=["rz", "neg_lam"], writes=["rz"])
                tt("dve", t2, Oc[:, 4:8, 0:128], rz[:, 4:8].unsqueeze(2).to_broadcast([128, 4, 128]), ALU.mult,
                   reads=ock + ["rz"], writes=[("A:t2",)])
                tt("dve", Dm, Oc[:, 0:4, 0:128], rz[:, 0:4].unsqueeze(2).to_broadcast([128, 4, 128]), ALU.mult,
                   reads=ock + ["rz"], writes=[("A:Dm",)])
                tt("pool", Dm, Dm, t2, ALU.add, reads=[("A:Dm",), ("A:t2",)], writes=[("A:Dm",)])

            def F3():
                act(t2, Dm, AF.Square, reads=[("A:Dm",)], writes=[("A:t2",)])
                pg.add("dve", lambda e: e.tensor_reduce(out=ss4, in_=t2, axis=AX.X, op=ALU.add), reads=[("A:t2",)], writes=["ss4"])

            def F4():
                rstd_from_ss(ss4, rstd4, 128, "ss4", "rstd4")

            def F5():
                tt("dve", Dn, Dm, rstd4.unsqueeze(2).to_broadcast([128, 4, 128]), ALU.mult, reads=[("A:Dm",), "rstd4"], writes=[("A:Dn",)])
                tbk = SBANKS[astep[0] % 2][0]
                tp = bank_bf(tbk)[:, 0:512]
                def ftp(e):
                    ins = None
                    for s_ in range(4):
                        ins = e.transpose(tp[:, s_ * 128:(s_ + 1) * 128], Dn[:, s_, :], csb["ident_bf"][:, :])
                    return ins
                pg.add("pe", ftp, reads=[("A:Dn",), "ident_bf"], excl=[("ps", tbk)])

                dst = mixT[:, 4 + h, G * 512:(G + 1) * 512]
                tt("dve", dst, tp, dst, ALU.mult, reads=[("mix", 4 + h, G)], writes=[("mix", 4 + h, G)], excl=[("ps", tbk)])

            F2.is_f2 = True
            fin_q.extend([F2, None, F3, None, F4, None, None, F5])

        for (G, kb) in steps:
            pb, s0 = do_qk_exp(G, kb)
            if pend:
                do_pv(*pend.pop(0))
            pend.append((G, kb, pb, s0))
            if fin_q:
                f_ = fin_q.pop(0)
                if f_ is not None:
                    f_()
            if hook is not None:
                hook()
        while pend:
            do_pv(*pend.pop(0))
        while flush and fin_q:
            f_ = fin_q.pop(0)
            if f_ is not None:
                f_()

    def head_inproj_thunks(h, pieces=False):
        buf = h % 2
        mk = inproj_fm_pieces if pieces else inproj_fm_thunks
        th = []
        th += mk(C_DZ + h * 128, 128, lambda g, p, bk: copy(
            "dve", mixT[:, 4 + h, g * 512:(g + 1) * 512], p, reads=[], writes=[("mix", 4 + h, g)], excl=[("ps", bk)]))
        th += mk(C_DQ + h * 128, 128, lambda g, p, bk: copy(
            "dve", dqT[buf][:, g * 512:(g + 1) * 512], p, reads=[], writes=[("A:dq", buf, g)], excl=[("ps", bk)]))
        th += mk(C_DK + h * 128, 128, lambda g, p, bk: copy(
            "dve", dkT[buf][:, g * 512:(g + 1) * 512], p, reads=[], writes=[("A:dk", buf, g)], excl=[("ps", bk)]))
        return th

    def head_gate(h):
        mk = [("mix", 4 + h, g) for g in range(NG)]
        dst = mixT[:, 4 + h, :]
        act(dst, dst, AF.Silu, reads=mk, writes=mk)
        ts("dve", dst, dst, dgain_sb[:, h:h + 1], 1.0 - LAM_INIT, ALU.mult, ALU.mult, reads=mk + ["dgain"], writes=mk)

    def run_diff(b):
        phase_barrier()
        ipbanks[0] = 2
        pg.add("pool", lambda e: e.memset(dv_aug[:, :, :, 128:129], 1.0), reads=[], writes=[("A:dvones",)])
        dv_th = inproj_tm_thunks(C_DV, lambda t, p, bk: copy("dve", dv_aug[:, t, :, 0:128], p.rearrange("p (h n) -> p h n", h=4),
                                                             reads=[("A:dvones",)], writes=[("A:dv", t)], excl=[("ps", bk)]))
        for f in head_inproj_thunks(0):
            f()
        n_pre = min(4, NT)
        for f in dv_th[:n_pre]:
            f()
        dv_late = dv_th[n_pre:]
        ipbanks[0] = 1
        fq = []
        for h in range(4):
            buf = h % 2
            head_gate(h)
            nxt = head_inproj_thunks(h + 1, pieces=True) if h < 3 else []
            if h == 0 and dv_late:
                merged = []
                while nxt or dv_late:
                    if dv_late:
                        merged.append((1, dv_late.pop(0)))
                    for _ in range(4):
                        if nxt:
                            merged.append((0, nxt.pop(0)))
                nxt = merged
            elif h == 3:
                nxt = [(0, f) for f in gate_thunks(b)]
            else:
                nxt = [(0, f) for f in nxt]
            def hook():
                budget = 2
                while nxt and budget > 0:
                    kind, f = nxt.pop(0)
                    f()
                    if kind == 0:
                        budget -= 1
            attention(b, h, buf, hook, fin_q=fq, flush=(h == 3))
            while nxt:
                nxt.pop(0)[1]()
            if h == 2:
                load_w_out()

    yslot = [0]

    wo_stage = Rbuf[:, 8 * D:16 * D].bitcast(F32).rearrange("p (k n) -> p k n", k=4)

    def load_w_out():
        hkeys = [("hT", g, e_) for g in range(NG) for e_ in ("d", "a")]
        for half in range(2):
            dma("sp", wo_stage, w_out_d[half * 512:(half + 1) * 512, :].rearrange("(k p) n -> p k n", p=128), ("wost", 0),
                writes=hkeys + ["wost"])
            copy("dve", w_out_sb[:, half * 4:(half + 1) * 4, :], wo_stage, reads=["wost"], writes=hkeys + [("wout", half)])

    def gate_thunks(b):
        def mk(kc):
            def f():
                ts("dve", diagg[:, :], csb["ident_f"][:, :], gcol[:, kc * 2 + b:kc * 2 + b + 1], None, ALU.mult, None,
                   reads=["ident_f", "gcol"], writes=["diagg"])
                mm_group(bank(0, 128, (kc % 4) * 128), [(csb["ones_f"][:, :], diagg[:, :])], reads=["ones_f", "diagg"], excl=[("ps", 0)])
                if kc % 4 == 3:
                    copy("dve", gate_bc[:, (kc // 4) * 512:(kc // 4 + 1) * 512], bank(0), reads=[], writes=["gate_bc"], excl=[("ps", 0)])
            return f
        return [mk(kc) for kc in range(8)]

    def run_out(b):
        hkeys = [("hT", g, e_) for g in range(NG) for e_ in ("d", "a")]
        wkeys = [("wout", 0), ("wout", 1)] + hkeys

        ybl = [ybuf[0], ybuf[1], ybuf3]
        ykl = [[("y", 0, 0), ("y", 0, 1)], [("y", 1, 0), ("y", 1, 1)], [("xn", 0), ("xn", 1)]]

        def out_stage1(t, s):
            tok = slice(t * 128, (t + 1) * 128)
            par = t % 2
            yb = ybl[t % 3]
            yk = ykl[t % 3]
            for half in range(2):
                bk = 2 * par + half
                pairs = [(mixT[:, kc, tok], w_out_sb[:, kc, half * 512:(half + 1) * 512]) for kc in range(8)]
                mm_group(bank(bk), pairs, reads=[("mix", kc, t // 4) for kc in range(8)] + wkeys, excl=[("ps", bk)])
                tt("dve", yb[:, half * 512:(half + 1) * 512], bank(bk), gate_bc[:, half * 512:(half + 1) * 512], ALU.mult,
                   reads=["gate_bc"], writes=[yk[half]], excl=[("ps", bk)])
            tt("pool", yb[:, :], yb[:, :], xt[s][:, :], ALU.add, reads=yk + [("xt", s)], writes=yk)

        def out_stage1c(t):
            i3 = t % 3
            yb = ybl[i3]
            yk = ykl[i3]
            ssk, rsk = "ss_p%d" % i3, "rstd_p%d" % i3
            jb = 4
            act(ps[:, jb * 512:jb * 512 + 1024], yb[:, :], AF.Square, reads=yk, writes=[ssk],
                excl=[("ps", jb), ("ps", jb + 1)], accum=ss_p[i3])
            rstd_from_ss(ss_p[i3], rstd_p[i3], D, ssk, rsk)

        def out_stage2(t):
            tok0 = b * S + t * 128
            i3 = t % 3
            yb = ybl[i3]
            yk = ykl[i3]
            stt(yb[:, :], yb[:, :], rstd_p[i3], fgain_sb[:, :], ALU.mult, ALU.mult, reads=yk + ["rstd_p%d" % i3, "fgain"], writes=yk)
            dma("act", y_d[tok0:tok0 + 128, :], yb[:, :], ("yout", i3), reads=yk)

        slots = {0: load_x(b * S)}
        for t in range(NT + 1):
            if t < NT:
                if t + 1 < NT:
                    slots[t + 1] = load_x(b * S + (t + 1) * 128)
                out_stage1(t, slots.pop(t))
            if t >= 1:
                out_stage2(t - 1)
            if t < NT:
                out_stage1c(t)

    for b in range(NSEQ):
        if upto == "mod":
            break
        DVE_KC = [0, 2, 3, 4, 6, 7]
        ACT_KC = [1, 5]

        def ht_stage1a(t, s):
            par = t % 2
            ssk, rsk = "ss_p%d" % par, "rstd_p%d" % par
            act(junk[:, :], xt[s][:, :], AF.Square, reads=[("xt", s)], writes=[("y", 1, 0), ("y", 1, 1), ssk], accum=ss_p[par])
            rstd_from_ss(ss_p[par], rstd_p[par], D, ssk, rsk)

        def ht_stage1b(t, s):
            par = t % 2
            xn = xnb[par][:, :]
            xk = ("xn", par)
            rsk = "rstd_p%d" % par
            ts("dve", xn, xt[s][:, :], rstd_p[par], None, ALU.mult, None, reads=[("xt", s), rsk], writes=[xk])
            bD, bA = 1 + 2 * par, 2 + 2 * par
            def ftr(e):
                ins = None
                for i, kc in enumerate(DVE_KC):
                    ins = e.transpose(bank_bf(bD)[:, i * 128:(i + 1) * 128], xn[:, kc * 128:(kc + 1) * 128], csb["ident_bf"][:, :])
                for i, kc in enumerate(ACT_KC):
                    ins = e.transpose(bank_bf(bA)[:, i * 128:(i + 1) * 128], xn[:, kc * 128:(kc + 1) * 128], csb["ident_bf"][:, :])
                return ins
            pg.add("pe", ftr, reads=[xk, "ident_bf"], excl=[("ps", bD), ("ps", bA)])

        def ht_stage2(t):
            par = t % 2
            bD, bA = 1 + 2 * par, 2 + 2 * par
            for i, kc in enumerate(DVE_KC):
                ts("dve", hT[:, kc, t * 128:(t + 1) * 128], bank_bf(bD)[:, i * 128:(i + 1) * 128],
                   acol[:, kc * 2 + b:kc * 2 + b + 1], scol[:, kc * 2 + b:kc * 2 + b + 1], ALU.mult, ALU.add,
                   reads=["acol", "scol"], writes=[("hT", t // 4, "d")], excl=[("ps", bD)])
            for i, kc in enumerate(ACT_KC):
                act(hT[:, kc, t * 128:(t + 1) * 128], bank_bf(bA)[:, i * 128:(i + 1) * 128], AF.Identity,
                    reads=["acol", "scol"], writes=[("hT", t // 4, "a")], excl=[("ps", bA)],
                    bias=scol[:, kc * 2 + b:kc * 2 + b + 1], scale=acol[:, kc * 2 + b:kc * 2 + b + 1])

        per_g = gla_begin(b)
        ipq = []
        pend_ev = []
        banksets = [[6, 7], [0, 5]]
        it = [0]

        ev_done = [0]

        def ip_step(n):
            while pend_ev:
                ev, bk = pend_ev.pop(0)
                ev(bk)
                ev_done[0] += 1
            for j in range(n):
                if ipq:
                    mm, ev = ipq.pop(0)
                    bk = banksets[it[0] % 2][j]
                    mm(bk)
                    pend_ev.append((ev, bk))
            it[0] += 1

        slots = {0: load_x(b * S)}
        for t in range(NT + 1):
            if t < NT:
                if t + 1 < NT:
                    slots[t + 1] = load_x(b * S + (t + 1) * 128)
                ht_stage1a(t, slots[t])
            if t >= 1:
                ht_stage2(t - 1)
                if (t - 1) % 4 == 3:
                    ipq.extend(per_g[(t - 1) // 4])
            if t < NT:
                ht_stage1b(t, slots.pop(t))
            ip_step(2)
        banksets[1] = [0, 1]
        n_per_group = len(per_g[0])

        def need_group(g):
            while ev_done[0] < n_per_group * (g + 1) and (ipq or pend_ev):
                ip_step(2)
        iptail = (ip_step, need_group, ipq, pend_ev)
        if b == 0:
            for kc in range(8):
                dump("hT%d" % kc, hT[:, kc, :], [("hT", g, e_) for g in range(4) for e_ in ("d", "a")])
        if upto == "hT":
            break
        run_gla(b, iptail)
        if b == 0:
            for hc in range(4):
                dump("mixg%d" % hc, mixT[:, hc, :], [("mix", hc, g) for g in range(NG)])
        if upto == "gla":
            break
        run_diff(b)
        if b == 0:
            for hc in range(4):
                dump("mixd%d" % hc, mixT[:, 4 + hc, :], [("mix", 4 + hc, g) for g in range(NG)])
        if upto == "diff":
            break
        run_out(b)

    pg.finalize(nc, stack)
    tail = [(pg.sems[(("dma", k), 0)], 16 * pg.dma_count[k]) for k in pg.dma_count if k[0] in ("dump", "yout")]
    last = Op("sp", lambda e: e.nop(), len(pg.ops), None)
    last.waits = tail
    last.signal = None
    pg.ops.append(last)
    pg.emit_all(nc)
    return nc, stack


def make_in_maps(x, c, w_ada, b_ada, norm_gain, w_in, w_gla_gate_up, b_gla_gate, gla_out_gain,
                 lambda_q1, lambda_k1, lambda_q2, lambda_k2, diff_out_gain, w_out, final_gain):
    f = lambda a: np.ascontiguousarray(np.asarray(a, dtype=np.float32))
    x = f(x); c = f(c)
    consts = _consts()
    shared = {
        "w_ada": f(w_ada[0]),
        "b_adaT": f(np.asarray(b_ada[0]).reshape(24, 128).T),
        "ngainT": f(np.asarray(norm_gain[0]).reshape(8, 128).T),
        "w_in": f(w_in[0]),
        "wup_aug": f(np.concatenate([np.asarray(w_gla_gate_up[0]), np.asarray(b_gla_gate[0])[None, :]], axis=0)),
        "ggainT": f(np.asarray(gla_out_gain[0]).reshape(4, 128).T),
        "dgainT": f(np.asarray(diff_out_gain[0]).reshape(4, 128).T),
        "lam_bc": f(np.broadcast_to(np.concatenate([np.asarray(lambda_q1[0]), np.asarray(lambda_k1[0]),
                                                    np.asarray(lambda_q2[0]), np.asarray(lambda_k2[0])])[None, :], (128, 256))),
        "w_out": f(w_out[0]),
        "fgain_bc": f(np.broadcast_to(np.asarray(final_gain)[None, :], (128, D))),
    }
    shared.update(consts)
    in_maps = []
    for i in range(NCORES):
        m = dict(shared)
        m["x"] = np.ascontiguousarray(x[2 * i:2 * i + 2].reshape(NSEQ * S, D))
        m["cT"] = np.ascontiguousarray(c[2 * i:2 * i + 2].reshape(2, 8, 128).transpose(2, 1, 0).reshape(128, 16))
        in_maps.append(m)
    return in_maps


def kernel(**inputs):
    in_maps = make_in_maps(**inputs)
    nc, stack = build_program()
    with stack:
        res = run_bass_kernel_spmd(nc, in_maps, core_ids=list(range(NCORES)))
    outs = [np.asarray(r["y"], dtype=np.float32).reshape(NSEQ, S, D) for r in res.results]
    return np.concatenate(outs, axis=0)
```

```python
import math
from contextlib import ExitStack

import numpy as np
import ml_dtypes

import concourse.bass as bass
import concourse.mybir as mybir
from concourse.bass_utils import run_bass_kernel_spmd

F32 = mybir.dt.float32
BF16 = mybir.dt.bfloat16
AF = mybir.ActivationFunctionType
ALU = mybir.AluOpType
AX = mybir.AxisListType

NCORES = 8
D = 1024
S = 2048
NSEQ = 2
NT = S // 128
D_IN = 3600
EPS = 1e-6
LAM_INIT = 0.8 - 0.6 * math.exp(-0.3 * 0)
SLOPES = [2.0 ** (-8.0 * (h + 1) / 4) for h in range(4)]
C_GQ, C_GK, C_GV, C_GZ, C_GR = 0, 256, 512, 1024, 1536
C_DQ, C_DK, C_DV, C_DZ = 1552, 2064, 2576, 3088

EPOCH = 12000


class Op:
    __slots__ = ("eng", "emit", "idx", "deps", "dma_key", "signal", "cnt", "waits")

    def __init__(self, eng, emit, idx, dma_key):
        self.eng = eng
        self.emit = emit
        self.idx = idx
        self.dma_key = dma_key
        self.deps = {}
        self.signal = False
        self.cnt = 0
        self.waits = []


class Prog:
    ENGS = ("pe", "act", "dve", "pool", "sp")

    def __init__(self):
        self.ops = []
        self.last_w = {}
        self.readers = {}
        self.xacc = {}
        self.dma_count = {}
        self.bulk = set()

    def _chan(self, op):
        return ("dma", op.dma_key) if op.dma_key is not None else op.eng

    def add(self, eng, emit, reads=(), writes=(), excl=(), dma_key=None, bulk=False):
        op = Op(eng, emit, len(self.ops), dma_key)
        self.ops.append(op)
        if dma_key is not None:
            self.dma_count[dma_key] = self.dma_count.get(dma_key, 0) + 1
            op.cnt = self.dma_count[dma_key]
            if bulk:
                self.bulk.add(dma_key)
        me = self._chan(op)
        deps = {}
        if any(isinstance(k, tuple) and isinstance(k[0], str) and k[0].startswith("A:") for k in list(reads) + list(writes)):
            reads = list(reads) + ["AR"]

        def dep(i):
            if i is None:
                return
            p = self.ops[i]
            c = self._chan(p)
            if c == "pe" and me == "pe":
                return
            if c == me and op.dma_key is not None and op.dma_key in self.bulk:
                return
            if c not in deps or deps[c] < i:
                deps[c] = i

        for k in reads:
            dep(self.last_w.get(k))
        for k in writes:
            dep(self.last_w.get(k))
            for c, i in self.readers.get(k, {}).items():
                dep(i)
        for k in excl:
            for c, i in self.xacc.get(k, {}).items():
                if c != me:
                    dep(i)
        for k in reads:
            self.readers.setdefault(k, {})[me] = op.idx
        for k in writes:
            self.last_w[k] = op.idx
            self.readers[k] = {}
        for k in excl:
            self.xacc[k] = {me: op.idx}
        op.deps = deps
        for c, i in deps.items():
            self.ops[i].signal = True
        return op

    def finalize(self, nc, stack):
        counters = {}
        for op in self.ops:
            if op.dma_key is None and op.signal:
                counters[op.eng] = counters.get(op.eng, 0) + 1
                op.cnt = counters[op.eng]
        self.sems = {}

        def sem_for(name):
            if name not in self.sems:
                self.sems[name] = stack.enter_context(nc.semaphore("s%d" % len(self.sems)))
            return self.sems[name]

        def target(p):
            if p.dma_key is not None:
                n = self.dma_count[p.dma_key] if p.dma_key in self.bulk else p.cnt
                return ("dma", p.dma_key), 0, 16 * n
            e = (p.cnt - 1) // EPOCH
            return p.eng, e, (p.cnt - 1) % EPOCH + 1

        waited = {e: {} for e in self.ENGS}
        for op in self.ops:
            w = waited[op.eng]
            for c, i in op.deps.items():
                chan, ep, val = target(self.ops[i])
                if w.get(chan, (-1, 0)) >= (ep, val):
                    continue
                w[chan] = (ep, val)
                op.waits.append((sem_for((chan, ep)), val))
        for op in self.ops:
            if op.dma_key is not None:
                op.signal = (sem_for((("dma", op.dma_key), 0)), 16)
            elif op.signal:
                _, ep, _ = target(op)
                op.signal = (sem_for((op.eng, ep)), 1)
            else:
                op.signal = None

    def emit_all(self, nc):
        by_eng = {e: [o for o in self.ops if o.eng == e] for e in self.ENGS}

        def run(engine, ops):
            for op in ops:
                for sem, val in op.waits:
                    engine.wait_ge(sem, val)
                ins = op.emit(engine)
                if op.signal is not None:
                    ins.then_inc(op.signal[0], op.signal[1])

        with nc.Block() as block:
            @block.sync
            def _(e):
                run(e, by_eng["sp"])

            @block.tensor
            def _(e):
                run(e, by_eng["pe"])

            @block.scalar
            def _(e):
                run(e, by_eng["act"])

            @block.vector
            def _(e):
                run(e, by_eng["dve"])

            @block.gpsimd
            def _(e):
                run(e, by_eng["pool"])


def _consts():
    j = np.arange(128)
    c = {}
    c["ident_bf"] = np.eye(128, dtype=np.float32).astype(ml_dtypes.bfloat16)
    c["ident_f"] = np.eye(128, dtype=np.float32)
    c["ones_f"] = np.ones((128, 128), np.float32)
    c["maskT_bf"] = (j[None, :] >= j[:, None]).astype(np.float32).astype(ml_dtypes.bfloat16)
    c["triI_f"] = np.where(j[:, None] <= j[None, :], -1.0 / 16.0, 0.0).astype(np.float32)
    c["triU_f"] = np.where(j[:, None] > j[None, :], -1.0 / 16.0, 0.0).astype(np.float32)
    bt = np.zeros((128, 4, 16), np.float32)
    for h in range(4):
        for d in range(16):
            bt[:, h, d] = SLOPES[h] * (j - 127 - 128 * d)
    c["bias_tab"] = bt.reshape(128, 64)
    return c


CONST_SPECS = [
    ("ident_bf", [128, 128], BF16), ("ident_f", [128, 128], F32), ("ones_f", [128, 128], F32),
    ("maskT_bf", [128, 128], BF16), ("triI_f", [128, 128], F32), ("triU_f", [128, 128], F32),
    ("bias_tab", [128, 64], F32),
]


def build_program(upto="all", dumps=()):
    nc = bass.Bass("TRN2", target_bir_lowering=False)
    pg = Prog()
    stack = ExitStack()
    dram = {}

    def din(name, shape, dt=F32):
        dram[name] = nc.dram_tensor(name, list(shape), dt, kind="ExternalInput").ap()
        return dram[name]

    x_d = din("x", [NSEQ * S, D])
    cT_d = din("cT", [128, 16])
    w_ada_d = din("w_ada", [D, 3 * D])
    b_adaT_d = din("b_adaT", [128, 24])
    ngain_d = din("ngainT", [128, 8])
    w_in_d = din("w_in", [D, D_IN])
    wup_d = din("wup_aug", [17, 256])
    ggain_d = din("ggainT", [128, 4])
    dgain_d = din("dgainT", [128, 4])
    lam_d = din("lam_bc", [128, 256])
    w_out_d = din("w_out", [D, D])
    fgain_d = din("fgain_bc", [128, D])
    cd = {n: din(n, shp, dt) for n, shp, dt in CONST_SPECS}
    y_d = nc.dram_tensor("y", [NSEQ * S, D], F32, kind="ExternalOutput").ap()
    dump_d = {}
    for name, shape in dumps:
        dump_d[name] = nc.dram_tensor("dbg_" + name, list(shape), F32, kind="ExternalOutput").ap()

    def sb(name, shape, dt):
        return stack.enter_context(nc.sbuf_tensor("sb_" + name, list(shape), dt))

    stack.enter_context(nc.allow_low_precision("bf16 matmul operands, fp32 accumulation"))

    w_in_sb = sb("w_in_sb", [128, 8, D_IN], BF16)
    Rbuf = sb("Rbuf", [128, max(8 * S, 16 * D)], BF16)
    hT = Rbuf[:, 0:8 * S].rearrange("p (k t) -> p k t", k=8)
    w_out_sb = Rbuf[:, 0:8 * D].rearrange("p (k n) -> p k n", k=8)
    mixbuf = sb("mixbuf", [128, max(8 * S, 16384)], BF16)
    mixT = mixbuf[:, 0:8 * S].rearrange("p (k t) -> p k t", k=8)
    NXT = 2
    xt = [sb("xt%d" % i, [128, D], F32) for i in range(NXT)]
    xnb_all = sb("xnb_all", [128, 2 * D], BF16)
    xnb = [xnb_all[:, 0:D], xnb_all[:, D:2 * D]]
    ybuf3 = xnb_all[:, :].bitcast(F32)
    ybuf = [sb("ybuf%d" % i, [128, D], F32) for i in range(2)]
    junk = ybuf[1]
    gate_bc = sb("gate_bc", [128, D], F32)
    fgain_sb = sb("fgain_sb", [128, D], F32)
    csb = {n: sb(n, shp, dt) for n, shp, dt in CONST_SPECS}
    cT_sb = sb("cT_sb", [128, 16], F32)
    b_adaT_sb = sb("b_adaT_sb", [128, 24], F32)
    ngain_sb = sb("ngain_sb", [128, 8], F32)
    ggain_sb = sb("ggain_sb", [128, 4], F32)
    dgain_sb = sb("dgain_sb", [128, 4], F32)
    lam_sb = sb("lam_sb", [128, 256], F32)
    wup_f = sb("wup_f", [32, 256], F32)
    wup_bf = sb("wup_bf", [32, 256], BF16)
    small = sb("small", [128, 256], F32)
    modT = sb("modT", [128, 48], F32)
    acol = sb("acol", [128, 16], F32)
    scol = sb("scol", [128, 16], F32)
    gcol = sb("gcol", [128, 16], F32)
    diagg = sb("diagg", [128, 128], F32)
    arena = sb("arena", [128, 24576], BF16)
    ps = stack.enter_context(nc.psum_tensor("ps", [128, 4096], F32))

    def bank(b, n=512, off=0):
        return ps[:, b * 512 + off: b * 512 + off + n]

    def bank_bf(b):
        return ps[:, b * 512:(b + 1) * 512].bitcast(BF16)

    SM = {}
    _sm_next = [0]

    def smcol(name, n=1):
        SM[name] = small[:, _sm_next[0]:_sm_next[0] + n]
        _sm_next[0] += n
        return SM[name]

    eps_col = smcol("eps_col", 1)

    def dma(queue, out, in_, key, reads=(), writes=(), bulk=False, **kw):
        pg.add(queue, lambda e: e.dma_start(out=out, in_=in_, **kw), reads=reads, writes=writes,
               dma_key=key, bulk=bulk)

    def dump(name, src_ap, reads):
        if name in dump_d:
            dma("pool", dump_d[name], src_ap, ("dump", name), reads=reads, max_dma_last_dim=4096)

    def act(out, in_, func, reads, writes, excl=(), bias=0.0, scale=1.0, accum=None):
        def f(e):
            kw = {}
            if accum is not None:
                kw["accum_out"] = accum
            return e.activation(out=out, in_=in_, func=func, bias=bias, scale=scale, **kw)
        pg.add("act", f, reads=reads, writes=writes, excl=excl)

    def tt(eng, out, in0, in1, op, reads, writes, excl=()):
        pg.add(eng, lambda e: e.tensor_tensor(out=out, in0=in0, in1=in1, op=op),
               reads=reads, writes=writes, excl=excl)

    def ts(eng, out, in0, s1, s2, op0, op1, reads, writes, excl=()):
        if s2 is None:
            pg.add(eng, lambda e: e.tensor_scalar(out=out, in0=in0, scalar1=s1, scalar2=None, op0=op0),
                   reads=reads, writes=writes, excl=excl)
        else:
            pg.add(eng, lambda e: e.tensor_scalar(out=out, in0=in0, scalar1=s1, scalar2=s2, op0=op0, op1=op1),
                   reads=reads, writes=writes, excl=excl)

    def stt(out, in0, scalar, in1, op0, op1, reads, writes, excl=()):
        pg.add("dve", lambda e: e.scalar_tensor_tensor(out=out, in0=in0, scalar=scalar, in1=in1, op0=op0, op1=op1),
               reads=reads, writes=writes, excl=excl)

    def copy(eng, out, in_, reads, writes, excl=()):
        pg.add(eng, lambda e: e.tensor_copy(out=out, in_=in_), reads=reads, writes=writes, excl=excl)

    def mm_group(out, pairs, reads, excl, writes=(), first_start=True, last_stop=True):
        def f(e):
            ins = None
            n = len(pairs)
            for i, (l, r) in enumerate(pairs):
                ins = e.matmul(out, l, r, start=(first_start and i == 0), stop=(last_stop and i == n - 1),
                               skip_group_check=not (first_start and last_stop))
            return ins
        pg.add("pe", f, reads=reads, writes=writes, excl=excl)

    def rstd_from_ss(ss, rstd, n, key_ss, key_rstd):
        act(ss, ss, AF.Ln, reads=[key_ss, "eps_col"], writes=[key_ss], scale=1.0 / n, bias=eps_col)
        act(rstd, ss, AF.Exp, reads=[key_ss], writes=[key_rstd], scale=-0.5)

    pg.add("pool", lambda e: e.memset(eps_col, EPS), reads=[], writes=["eps_col"])
    for n, shp, dt in CONST_SPECS:
        dma("sp", csb[n][:, :], cd[n], "const", writes=[n], bulk=True)
    dma("sp", cT_sb[:, :], cT_d, "const", writes=["cT"], bulk=True)
    dma("sp", b_adaT_sb[:, :], b_adaT_d, "const", writes=["b_adaT"], bulk=True)
    dma("sp", ngain_sb[:, :], ngain_d, "const", writes=["ngain"], bulk=True)
    dma("sp", ggain_sb[:, :], ggain_d, "const", writes=["ggain"], bulk=True)
    dma("sp", dgain_sb[:, :], dgain_d, "const", writes=["dgain"], bulk=True)
    dma("sp", lam_sb[:, :], lam_d, "const", writes=["lam_in"], bulk=True)
    dma("sp", wup_f[0:17, :], wup_d, "const", writes=["wup_f"], bulk=True)
    dma("sp", fgain_sb[:, :], fgain_d, "const", writes=["fgain"], bulk=True)
    copy("dve", wup_bf[0:17, :], wup_f[0:17, :], reads=["wup_f"], writes=["wup_bf"])

    sc = smcol("sc", 16)
    tmp16 = smcol("tmp16", 16)
    act(tmp16, cT_sb[:, :], AF.Exp, reads=["cT"], writes=["tmp16"], scale=-1.0)
    ts("dve", tmp16, tmp16, 1.0, None, ALU.add, None, reads=["tmp16"], writes=["tmp16"])
    pg.add("dve", lambda e: e.reciprocal(out=tmp16, in_=tmp16), reads=["tmp16"], writes=["tmp16"])
    tt("dve", sc, cT_sb[:, :], tmp16, ALU.mult, reads=["cT", "tmp16"], writes=["sc"])

    sc_bf = sb("sc_bf", [128, 16], BF16)
    copy("dve", sc_bf[:, :], sc, reads=["sc"], writes=["sc_bf"])
    slab = [mixbuf[:, 0:8192].bitcast(F32).rearrange("p (k n) -> p k n", k=8),
            mixbuf[:, 8192:16384].bitcast(F32).rearrange("p (k n) -> p k n", k=8)]
    slab_bf = [Rbuf[:, 0:4096].rearrange("p (k n) -> p k n", k=8), Rbuf[:, 4096:8192].rearrange("p (k n) -> p k n", k=8)]
    w_ada_v = w_ada_d.rearrange("(k p) n -> p k n", p=128)
    for sl in range(6):
        i2 = sl % 2
        dma("sp", slab[i2], w_ada_v[:, :, sl * 512:(sl + 1) * 512], ("slab", i2), writes=[("slab", i2)])
        copy("dve", slab_bf[i2], slab[i2], reads=[("slab", i2)], writes=[("slabbf", i2)])
        for jj in range(4):
            j = sl * 4 + jj
            pairs = [(slab_bf[i2][:, kc, jj * 128:(jj + 1) * 128], sc_bf[:, kc * 2:(kc + 1) * 2]) for kc in range(8)]
            mm_group(bank(0, 2, j * 2), pairs, reads=[("slabbf", i2), "sc_bf"], excl=[("ps", 0)])
    tt("dve", modT[:, :].rearrange("p (j b) -> p j b", b=2), bank(0, 48).rearrange("p (j b) -> p j b", b=2),
       b_adaT_sb[:, :].unsqueeze(2).to_broadcast([128, 24, 2]), ALU.add,
       reads=["b_adaT"], writes=["modT", ("slab", 0), ("slab", 1), ("slabbf", 0), ("slabbf", 1)], excl=[("ps", 0)])
    ts("dve", acol[:, :], modT[:, 16:32], 1.0, None, ALU.add, None, reads=["modT"], writes=["acol"])
    tt("dve", acol[:, :].rearrange("p (k b) -> p k b", b=2), acol[:, :].rearrange("p (k b) -> p k b", b=2),
       ngain_sb[:, :].unsqueeze(2).to_broadcast([128, 8, 2]), ALU.mult, reads=["acol", "ngain"], writes=["acol"])
    copy("dve", scol[:, :], modT[:, 0:16], reads=["modT"], writes=["scol"])
    copy("dve", gcol[:, :], modT[:, 32:48], reads=["modT"], writes=["gcol"])
    dump("acol", acol[:, :], ["acol"])
    dump("scol", scol[:, :], ["scol"])
    dump("gcol", gcol[:, :], ["gcol"])

    lam_t = smcol("lam_t", 2)
    neg_lam = smcol("neg_lam", 1)
    lamprod = smcol("lamprod", 128)
    tt("dve", lamprod.rearrange("p (a d) -> p a d", a=2), lam_sb[:, :].rearrange("p (a t d) -> p a t d", a=2, t=2)[:, :, 0, :],
       lam_sb[:, :].rearrange("p (a t d) -> p a t d", a=2, t=2)[:, :, 1, :], ALU.mult,
       reads=["lam_in"], writes=["lamprod"])
    pg.add("dve", lambda e: e.tensor_reduce(out=lam_t, in_=lamprod.rearrange("p (a d) -> p a d", a=2), axis=AX.X, op=ALU.add),
           reads=["lamprod"], writes=["lam_t"])
    act(lam_t, lam_t, AF.Exp, reads=["lam_t"], writes=["lam_t"])
    stt(neg_lam, lam_t[:, 1:2], -LAM_INIT, lam_t[:, 0:1], ALU.add, ALU.subtract, reads=["lam_t"], writes=["neg_lam"])
    dump("neg_lam", neg_lam, ["neg_lam"])

    wst = [arena[:, i * 7200:(i + 1) * 7200].bitcast(F32) for i in range(3)]
    for kc in range(8):
        sl = kc % 3
        dma("sp", wst[sl], w_in_d[kc * 128:(kc + 1) * 128, :], ("wst", sl), writes=[("A:wst", sl)])
        copy("dve", w_in_sb[:, kc, :], wst[sl], reads=[("A:wst", sl)], writes=[("win", kc)])

    xslot = [0]

    def load_x(tok0):
        s = xslot[0] % NXT
        xslot[0] += 1
        dma("sp", xt[s][:, :], x_d[tok0:tok0 + 128, :], ("xt", s), writes=[("xt", s)])
        return s

    ss_c = smcol("ss_c", 1)
    rstd_c = smcol("rstd_c", 1)
    ss_p = [smcol("ss_p%d" % i, 1) for i in range(3)]
    rstd_p = [smcol("rstd_p%d" % i, 1) for i in range(3)]
    ss4_p = [smcol("ss4_p%d" % i, 4) for i in range(2)]
    rstd4_p = [smcol("rstd4_p%d" % i, 4) for i in range(2)]


    NG = S // 512
    arena_f = arena[:, :].bitcast(F32)
    dummy = smcol("dummy", 1)
    ipb = [0]

    def phase_barrier():
        pg.add("dve", lambda e: e.memset(dummy, 0.0), reads=[], writes=["AR", "dummy"])

    def win_keys(col):
        return [("win", kc) for kc in range(8)]

    evq = [0]

    def evac_copy(out, in_, reads, writes, excl):
        evq[0] += 1
        if evq[0] % 2 == 0:
            act(out, in_, AF.Copy, reads=reads, writes=writes, excl=excl)
        else:
            copy("dve", out, in_, reads=reads, writes=writes, excl=excl)

    ipbanks = [2]
    ipbl = [[0, 1]]

    def inproj_fm_thunks(col0, M, evac):
        def mk(g):
            def f():
                bk = ipbl[0][ipb[0] % min(ipbanks[0], len(ipbl[0]))]
                ipb[0] += 1
                pairs = [(w_in_sb[:, kc, col0:col0 + M], hT[:, kc, g * 512:(g + 1) * 512]) for kc in range(8)]
                mm_group(bank(bk)[0:M, :], pairs, reads=win_keys(col0) + [("hT", g, "d"), ("hT", g, "a")], excl=[("ps", bk)])
                evac(g, bank(bk)[0:M, :], bk)
            return f
        return [mk(g) for g in range(NG)]

    def inproj_fm(col0, M, evac):
        for f in inproj_fm_thunks(col0, M, evac):
            f()

    def inproj_fm_pairs(col0, M, evac):
        def mk(g):
            def mm(bk):
                pairs = [(w_in_sb[:, kc, col0:col0 + M], hT[:, kc, g * 512:(g + 1) * 512]) for kc in range(8)]
                mm_group(bank(bk)[0:M, :], pairs, reads=win_keys(col0) + [("hT", g, "d"), ("hT", g, "a")], excl=[("ps", bk)])
            def ev(bk):
                evac(g, bank(bk)[0:M, :], bk)
            return (mm, ev)
        return [mk(g) for g in range(NG)]

    def inproj_fm_pieces(col0, M, evac, npieces=4):
        out_list = []
        per = 8 // npieces
        for g in range(NG):
            for pc in range(npieces):
                def f(g=g, pc=pc):
                    bk = 0
                    pairs = [(w_in_sb[:, kc, col0:col0 + M], hT[:, kc, g * 512:(g + 1) * 512])
                             for kc in range(pc * per, (pc + 1) * per)]
                    mm_group(bank(bk)[0:M, :], pairs, reads=win_keys(col0) + [("hT", g, "d"), ("hT", g, "a")], excl=[("ps", bk)],
                             first_start=(pc == 0), last_stop=(pc == npieces - 1))
                    if pc == npieces - 1:
                        evac(g, bank(bk)[0:M, :], bk)
                out_list.append(f)
        return out_list

    def inproj_tm_pairs(col0, evac):
        def mk(t):
            def mm(bk):
                pairs = [(hT[:, kc, t * 128:(t + 1) * 128], w_in_sb[:, kc, col0:col0 + 512]) for kc in range(8)]
                mm_group(bank(bk), pairs, reads=win_keys(col0) + [("hT", t // 4, "d"), ("hT", t // 4, "a")], excl=[("ps", bk)])
            def ev(bk):
                evac(t, bank(bk), bk)
            return (mm, ev)
        return [mk(t) for t in range(NT)]

    def inproj_tm_thunks(col0, evac):
        def mk(t):
            def f():
                bk = ipbl[0][ipb[0] % min(ipbanks[0], len(ipbl[0]))]
                ipb[0] += 1
                pairs = [(hT[:, kc, t * 128:(t + 1) * 128], w_in_sb[:, kc, col0:col0 + 512]) for kc in range(8)]
                mm_group(bank(bk), pairs, reads=win_keys(col0) + [("hT", t // 4, "d"), ("hT", t // 4, "a")], excl=[("ps", bk)])
                evac(t, bank(bk), bk)
            return f
        return [mk(t) for t in range(NT)]

    def inproj_tm(col0, evac):
        for f in inproj_tm_thunks(col0, evac):
            f()

    gqT = arena[:, 0:2 * S].rearrange("p (c t) -> p c t", c=2)
    gkT = arena[:, 2 * S:4 * S].rearrange("p (c t) -> p c t", c=2)
    gv = arena[:, 4 * S:8 * S].rearrange("p (t n) -> p t n", n=512)
    grT = arena[0:32, 8 * S:9 * S]
    tb = 9 * S
    tf = tb // 2
    laA = [arena_f[:, tf + i * 512:tf + (i + 1) * 512] for i in range(2)]
    ebA = [arena_f[:, tf + 1024 + i * 512:tf + 1024 + (i + 1) * 512] for i in range(2)]
    enbA = [arena_f[:, tf + 2048 + i * 512:tf + 2048 + (i + 1) * 512] for i in range(2)]
    assert 2 * (tf + 3072) <= 24576
    kinT = [arena[:, tb + i * 256:tb + (i + 1) * 256] for i in range(2)]
    AT = [arena[:, tb + 512 + i * 512:tb + 512 + (i + 1) * 512].rearrange("p (h t) -> p h t", h=4) for i in range(2)]
    on = [arena[:, tb + 1536 + i * 512:tb + 1536 + (i + 1) * 512].rearrange("p (h t) -> p h t", h=4) for i in range(2)]
    S_bf = arena[:, tb + 2560:tb + 2816].rearrange("p (c t) -> p c t", c=2)
    fB = (tb + 2816) // 2
    S32 = arena_f[:, fB:fB + 256]
    Sd = arena_f[:, fB + 256:fB + 512]
    osq = arena_f[:, fB + 512:fB + 1024]
    assert 2 * (fB + 1024) <= 24576
    dec_all = smcol("dec_all", 2 * NT)
    ss4 = smcol("ss4", 4)
    rstd4 = smcol("rstd4", 4)
    QORD = [0, 2, 1, 3]

    def dec_col(t, c):
        i = (t // 2) * 4 + c * 2 + (t % 2)
        return dec_all[:, i:i + 1]

    def gla_begin(b):
        phase_barrier()
        pg.add("pool", lambda e: e.memset(grT, 1.0), reads=[], writes=[("A:gr",)])
        per_g = {g: [] for g in range(NG)}
        def add_fm(col0, M, evac):
            for g, f in enumerate(inproj_fm_pairs(col0, M, evac)):
                per_g[g].append(f)
        add_fm(C_GR, 16, lambda g, p, bk: copy("dve", grT[0:16, g * 512:(g + 1) * 512], p, reads=[], writes=[("A:gr",)], excl=[("ps", bk)]))
        for c in range(2):
            add_fm(C_GQ + c * 128, 128, lambda g, p, bk, c=c: evac_copy(
                gqT[:, c, g * 512:(g + 1) * 512], p, reads=[], writes=[("A:gq", c, g)], excl=[("ps", bk)]))
            add_fm(C_GK + c * 128, 128, lambda g, p, bk, c=c: evac_copy(
                gkT[:, c, g * 512:(g + 1) * 512], p, reads=[], writes=[("A:gk", c, g)], excl=[("ps", bk)]))
        for hc in range(4):
            add_fm(C_GZ + hc * 128, 128, lambda g, p, bk, hc=hc: evac_copy(
                mixT[:, hc, g * 512:(g + 1) * 512], p, reads=[], writes=[("mix", hc, g)], excl=[("ps", bk)]))
        for t, f in enumerate(inproj_tm_pairs(C_GV, lambda t, p, bk: evac_copy(gv[:, t, :], p, reads=[], writes=[("A:gv", t)], excl=[("ps", bk)]))):
            per_g[t // 4].append(f)
        return per_g

    def gla_gate():
        for hc in range(4):
            mk = [("mix", hc, g) for g in range(NG)]
            dst = mixT[:, hc, :]
            act(dst, dst, AF.Silu, reads=mk, writes=mk)
            ts("dve", dst, dst, ggain_sb[:, hc:hc + 1], None, ALU.mult, None, reads=mk + ["ggain"], writes=mk)

    def run_gla(b, iptail):
        ipbanks[0] = 2
        ipbl[0] = [0, 1]
        ip_step, need_group, ipq, pend_ev = iptail
        gv_thunks = []

        def zstage(p):
            pp = p % 2
            zb = 2 + 2 * pp
            def fz(e):
                ins = None
                for j in range(2):
                    tok = slice(p * 256 + j * 128, p * 256 + (j + 1) * 128)
                    ins = e.matmul(bank(zb, 256, j * 256), grT[0:17, tok], wup_bf[0:17, :], start=True, stop=True)
                return ins
            pg.add("pe", fz, reads=[("A:gr",), "wup_bf"], excl=[("ps", zb)])
            act(laA[pp], bank(zb), AF.Exp, reads=[], writes=[("A:la", pp)], excl=[("ps", zb)], scale=-1.0)
            act(laA[pp], laA[pp], AF.Ln, reads=[("A:la", pp)], writes=[("A:la", pp)], bias=1.0)

        def cstage(p):
            pp = p % 2
            g = p // 2
            bb = 3 + 2 * pp
            tk = slice(p * 256, (p + 1) * 256)
            def fcs(e):
                ins = None
                for c in range(2):
                    for j in range(2):
                        ins = e.matmul(bank(bb, 128, (c * 2 + j) * 128), laA[pp][:, j * 256 + c * 128:j * 256 + (c + 1) * 128],
                                       csb["triI_f"][:, :], start=True, stop=True)
                return ins
            pg.add("pe", fcs, reads=[("A:la", pp), "triI_f"], excl=[("ps", bb)])
            act(ebA[pp], bank(bb), AF.Exp, reads=[], writes=[("A:eb", pp)], excl=[("ps", bb)], bias=math.log(0.125))
            act(enbA[pp], bank(bb), AF.Exp, reads=[], writes=[("A:enb", pp)], excl=[("ps", bb)], scale=-1.0)
            act(dec_all[:, p * 4:(p + 1) * 4].unsqueeze(2), bank(bb).rearrange("p (i t) -> p i t", i=4)[:, :, 127:128], AF.Exp,
                reads=[], writes=[("dec", p)], excl=[("ps", bb)])
            tt("pool", gqT[:, :, tk], gqT[:, :, tk], ebA[pp].rearrange("p (c t) -> p c t", c=2), ALU.mult,
               reads=[("A:gq", 0, g), ("A:gq", 1, g), ("A:eb", pp)], writes=[("A:qin", p)])
            tt("pool", gkT[:, :, tk], gkT[:, :, tk], enbA[pp].rearrange("p (c t) -> p c t", c=2), ALU.mult,
               reads=[("A:gk", 0, g), ("A:gk", 1, g), ("A:enb", pp)], writes=[("A:kin", p)])

        NP = NT // 2
        need_group(0)
        zstage(0)
        for p in range(NP):
            if p + 1 < NP:
                need_group((p + 1) // 2)
                zstage(p + 1)
            if ipq or pend_ev:
                ip_step(2)
                ip_step(2)
            cstage(p)
        while ipq or pend_ev:
            ip_step(2)
        gla_gate()
        pg.add("dve", lambda e: e.memset(S32, 0.0),
               reads=[("A:la", 0), ("A:la", 1), ("A:eb", 0), ("A:eb", 1), ("A:enb", 0), ("A:enb", 1)],
               writes=[("A:S32",), ("A:la", 0), ("A:la", 1), ("A:eb", 0), ("A:eb", 1), ("A:enb", 0), ("A:enb", 1)])
        pg.add("pool", lambda e: e.memset(Sd, 0.0), reads=[("A:S32",)], writes=[("A:Sd",)])
        bkey = [("A:la", 0)]

        def pre(t):
            tok = slice(t * 128, (t + 1) * 128)
            p = t // 2
            par = t % 2
            ktp = bank_bf(1)[:, 0:256]
            def fkt(e):
                e.transpose(ktp[:, 0:128], gkT[:, 0, tok], csb["ident_bf"][:, :])
                return e.transpose(ktp[:, 128:256], gkT[:, 1, tok], csb["ident_bf"][:, :])
            pg.add("pe", fkt, reads=[("A:kin", p), "ident_bf"], excl=[("ps", 1)])
            act(kinT[par], ktp, AF.Copy, reads=bkey, writes=[("A:kinT", par)], excl=[("ps", 1)])
            def fsc(e):
                ins = None
                for h in range(4):
                    c = h // 2
                    rows = slice(0, 64) if h % 2 == 0 else slice(64, 128)
                    ins = e.matmul(bank(2 + h % 2, 128, c * 128), gkT[rows, c, tok], gqT[rows, c, tok], start=True, stop=True)
                return ins
            pg.add("pe", fsc, reads=[("A:kin", p), ("A:qin", p)], excl=[("ps", 2), ("ps", 3)])
            for q in range(2):
                tt("dve", AT[par][:, q::2, :], bank(2 + q, 256).rearrange("p (c t) -> p c t", c=2),
                   csb["maskT_bf"][:, :].unsqueeze(1).to_broadcast([128, 2, 128]), ALU.mult,
                   reads=["maskT_bf"] + bkey, writes=[("A:AT", par, q)], excl=[("ps", 2 + q)])

        def obanks(t):
            return (6, 7) if t % 2 == 0 else (4, 5)

        def main_o(t):
            tok = slice(t * 128, (t + 1) * 128)
            p = t // 2
            par = t % 2
            ob = obanks(t)
            def fo_(e):
                ins = None
                for h in range(4):
                    ins = e.matmul(bank(ob[h % 2], 128, (h // 2) * 128), AT[par][:, h, :], gv[:, t, h * 128:(h + 1) * 128],
                                   start=(h < 2), stop=(t == 0), skip_group_check=True)
                if t > 0:
                    for h in range(4):
                        c = h // 2
                        rows = slice(0, 64) if h % 2 == 0 else slice(64, 128)
                        ins = e.matmul(bank(ob[h % 2], 128, c * 128), gqT[rows, c, tok], S_bf[rows, c, :],
                                       start=False, stop=True, skip_group_check=True)
                return ins
            pg.add("pe", fo_, reads=[("A:AT", par, 0), ("A:AT", par, 1), ("A:gv", t), ("A:qin", p), ("A:Sbf",)],
                   excl=[("ps", ob[0]), ("ps", ob[1])])
            ssk = "ss4_p%d" % par
            for q in range(4):
                act(osq[:, q * 128:(q + 1) * 128], bank(ob[q // 2], 128, (q % 2) * 128), AF.Square, reads=bkey,
                    writes=[("A:osq", q), (ssk, q)], excl=[("ps", ob[q // 2])], accum=ss4_p[par][:, q:q + 1])

        def main_kv(t):
            p = t // 2
            par = t % 2
            if t < NT - 1:
                def fkv(e):
                    ins = None
                    for h in range(4):
                        c = h // 2
                        rows = slice(0, 64) if h % 2 == 0 else slice(64, 128)
                        ins = e.matmul(bank(0)[rows, c * 128:(c + 1) * 128], kinT[par][:, h * 64:(h + 1) * 64],
                                       gv[:, t, h * 128:(h + 1) * 128], start=True, stop=True)
                    return ins
                pg.add("pe", fkv, reads=[("A:kinT", par), ("A:gv", t)], excl=[("ps", 0)])
                for c in range(2):
                    cs = slice(c * 128, (c + 1) * 128)
                    stt(S32[:, cs], bank(0, 128, c * 128), dec_col(t, c), Sd[:, cs], ALU.mult, ALU.add,
                        reads=[("A:Sd",), ("dec", p)], writes=[("A:S32",)], excl=[("ps", 0)])
                act(S_bf, S32.rearrange("p (c t) -> p c t", c=2), AF.Copy, reads=[("A:S32",)], writes=[("A:Sbf",)])
                if t + 1 < NT - 1:
                    for c in range(2):
                        cs = slice(c * 128, (c + 1) * 128)
                        ts("dve", Sd[:, cs], S32[:, cs], dec_col(t + 1, c), None, ALU.mult, None,
                           reads=[("A:S32",), ("dec", (t + 1) // 2)], writes=[("A:Sd",)])

        def post(t):
            tok = slice(t * 128, (t + 1) * 128)
            g = t // 4
            par = t % 2
            ob = obanks(t)
            ssk, rsk = "ss4_p%d" % par, "rstd4_p%d" % par
            act(ss4_p[par], ss4_p[par], AF.Ln, reads=[(ssk, q) for q in range(4)] + ["eps_col"], writes=[ssk],
                scale=1.0 / 128, bias=eps_col)
            act(rstd4_p[par], ss4_p[par], AF.Exp, reads=[ssk], writes=[rsk], scale=-0.5)
            for q in range(2):
                tt("dve", on[par][:, q * 2:(q + 1) * 2, :], bank(ob[q], 256).rearrange("p (c t) -> p c t", c=2),
                   rstd4_p[par][:, q * 2:(q + 1) * 2].unsqueeze(2).to_broadcast([128, 2, 128]), ALU.mult,
                   reads=[rsk] + bkey, writes=[("A:on", par, q)], excl=[("ps", ob[q])])
            tp = bank_bf(1)[:, 256:768]
            def ftp(e):
                ins = None
                for q in range(4):
                    h = QORD[q]
                    ins = e.transpose(tp[:, h * 128:(h + 1) * 128], on[par][:, q, :], csb["ident_bf"][:, :])
                return ins
            pg.add("pe", ftp, reads=[("A:on", par, 0), ("A:on", par, 1), "ident_bf"], excl=[("ps", 1)])
            tt("dve", mixT[:, 0:4, tok], tp.rearrange("p (h t) -> p h t", h=4), mixT[:, 0:4, tok], ALU.mult,
               reads=[("mix", hc, g) for hc in range(4)], writes=[("mix", hc, g) for hc in range(4)], excl=[("ps", 1)])

        pre(0)
        for t in range(NT):
            main_o(t)
            main_kv(t)
            if t + 1 < NT:
                pre(t + 1)
            post(t)

    dv_aug = arena[:, 0:NT * 520].rearrange("p (t h n) -> p t h n", h=4, n=130)
    do = NT * 520
    dqT = [arena[:, do + i * S:do + (i + 1) * S] for i in range(2)]
    dkT = [arena[:, do + (2 + i) * S:do + (3 + i) * S] for i in range(2)]
    po = do + 4 * S
    PT = [[arena[:, po + (2 * i + m) * 512:po + (2 * i + m + 1) * 512] for m in range(2)] for i in range(2)]
    fo2 = (po + 2048) // 2
    Oc = arena_f[:, fo2:fo2 + 8 * 129].rearrange("p (i n) -> p i n", n=129)
    Dm = arena_f[:, fo2 + 1032:fo2 + 1544].rearrange("p (s n) -> p s n", s=4)
    t2 = arena_f[:, fo2 + 1544:fo2 + 2056].rearrange("p (s n) -> p s n", s=4)
    dno = 2 * (fo2 + 2056)
    Dn = arena[:, dno:dno + 512].rearrange("p (s n) -> p s n", s=4)
    assert dno + 512 <= 24576
    rz = smcol("rz", 8)
    SBANKS = [(2, 3), (7, 1)]
    astep = [0]

    def obank(i):
        return 4 + i // 3, (i % 3) * 129

    def attention(b, h, buf, hook=None, fin_q=None, flush=True):
        qsub = 256 if SLOPES[h] * 511 > 40 else 512
        steps = [(G, kb) for G in range(NG) for kb in range(4 * G + 4)]
        started = {}
        pend = []

        def do_qk_exp(G, kb):
            t = kb - 4 * G
            s0 = max(t, 0)
            c0 = 128 * s0
            ks = slice(kb * 128, (kb + 1) * 128)
            qs = slice(G * 512 + c0, (G + 1) * 512)
            pb = astep[0] % 2
            astep[0] += 1
            sb0, sb1 = SBANKS[pb]
            def fqk(e):
                e.matmul(bank(sb0)[:, c0:512], dkT[buf][0:64, ks], dqT[buf][0:64, qs], start=True, stop=True)
                return e.matmul(bank(sb1)[:, c0:512], dkT[buf][64:128, ks], dqT[buf][64:128, qs], start=True, stop=True)
            pg.add("pe", fqk, reads=[("A:dk", buf, kb // 4), ("A:dq", buf, G)], excl=[("ps", sb0), ("ps", sb1)])
            for m in range(2):
                sbm = (sb0, sb1)[m]
                if qsub == 512:
                    chunks = [(c0, 512)]
                else:
                    chunks = [(max(c0, lo), lo + qsub) for lo in range(0, 512, qsub) if c0 < lo + qsub]
                for (a, bnd) in chunks:
                    delta = 4 * G + (bnd - 1) // 128 - kb
                    act(PT[pb][m][:, a:bnd], bank(sbm)[:, a:bnd], AF.Exp, reads=["bias_tab"], writes=[("A:PT", pb, m)],
                        excl=[("ps", sbm)], bias=csb["bias_tab"][:, h * 16 + delta:h * 16 + delta + 1], scale=0.125)
                if t >= 0:
                    tt("dve", PT[pb][m][:, c0:c0 + 128], PT[pb][m][:, c0:c0 + 128], csb["maskT_bf"][:, :], ALU.mult,
                       reads=[("A:PT", pb, m), "maskT_bf"], writes=[("A:PT", pb, m)])
            return pb, s0

        def do_pv(G, kb, pb, s0):
            st_set = started.setdefault(G, set())
            plan = []
            for m in range(2):
                for s in range(s0, 4):
                    bk, off = obank(4 * m + s)
                    plan.append((bk, off, m, s, bk not in st_set, kb == 4 * G + s))
                    st_set.add(bk)
            def fpv(e):
                ins = None
                for (bk, off, m, s, st, sp) in plan:
                    ins = e.matmul(bank(bk)[:, off:off + 129], PT[pb][m][:, s * 128:(s + 1) * 128], dv_aug[:, kb, h, 0:129],
                                   start=st, stop=sp, skip_group_check=True)
                return ins
            pg.add("pe", fpv, reads=[("A:PT", pb, 0), ("A:PT", pb, 1), ("A:dv", kb)], excl=[("ps", 4), ("ps", 5), ("ps", 6)])
            if kb == 4 * G + 3:
                finalize(G)

        if fin_q is None:
            fin_q = []

        def finalize(G):
            while any(f_ is not None and getattr(f_, "is_f2", False) for f_ in fin_q):
                f_ = fin_q.pop(0)
                if f_ is not None:
                    f_()
            copy("dve", Oc[:, 0:3, :], bank(4, 387).rearrange("p (i n) -> p i n", n=129), reads=[], writes=[("A:Oc", 0)], excl=[("ps", 4)])
            copy("dve", Oc[:, 3:6, :], bank(5, 387).rearrange("p (i n) -> p i n", n=129), reads=[], writes=[("A:Oc", 1)], excl=[("ps", 5)])
            copy("dve", Oc[:, 6:8, :], bank(6, 258).rearrange("p (i n) -> p i n", n=129), reads=[], writes=[("A:Oc", 2)], excl=[("ps", 6)])
            ock = [("A:Oc", i) for i in range(3)]

            def F2():
                pg.add("dve", lambda e: e.reciprocal(out=rz.unsqueeze(2), in_=Oc[:, :, 128:129]), reads=ock, writes=["rz"])
                ts("dve", rz[:, 4:8], rz[:, 4:8], neg_lam, None, ALU.mult, None, reads=["rz", "neg_lam"], writes=["rz"])
                tt("dve", t2, Oc[:, 4:8, 0:128], rz[:, 4:8].unsqueeze(2).to_broadcast([128, 4, 128]), ALU.mult,
                   reads=ock + ["rz"], writes=[("A:t2",)])
                tt("dve", Dm, Oc[:, 0:4, 0:128], rz[:, 0:4].unsqueeze(2).to_broadcast([128, 4, 128]), ALU.mult,
                   reads=ock + ["rz"], writes=[("A:Dm",)])
                tt("pool", Dm, Dm, t2, ALU.add, reads=[("A:Dm",), ("A:t2",)], writes=[("A:Dm",)])

            def F3():
                tt("dve", t2, Dm, Dm, ALU.mult, reads=[("A:Dm",)], writes=[("A:t2",)])
                pg.add("dve", lambda e: e.tensor_reduce(out=ss4, in_=t2, axis=AX.X, op=ALU.add), reads=[("A:t2",)], writes=["ss4"])

            def F4():
                rstd_from_ss(ss4, rstd4, 128, "ss4", "rstd4")

            def F5():
                tt("dve", Dn, Dm, rstd4.unsqueeze(2).to_broadcast([128, 4, 128]), ALU.mult, reads=[("A:Dm",), "rstd4"], writes=[("A:Dn",)])
                tbk = SBANKS[astep[0] % 2][0]
                tp = bank_bf(tbk)[:, 0:512]
                def ftp(e):
                    ins = None
                    for s_ in range(4):
                        ins = e.transpose(tp[:, s_ * 128:(s_ + 1) * 128], Dn[:, s_, :], csb["ident_bf"][:, :])
                    return ins
                pg.add("pe", ftp, reads=[("A:Dn",), "ident_bf"], excl=[("ps", tbk)])

                dst = mixT[:, 4 + h, G * 512:(G + 1) * 512]
                tt("dve", dst, tp, dst, ALU.mult, reads=[("mix", 4 + h, G)], writes=[("mix", 4 + h, G)], excl=[("ps", tbk)])

            F2.is_f2 = True
            fin_q.extend([F2, None, F3, None, F4, None, None, F5])

        for (G, kb) in steps:
            pb, s0 = do_qk_exp(G, kb)
            if pend:
                do_pv(*pend.pop(0))
            pend.append((G, kb, pb, s0))
            if fin_q:
                f_ = fin_q.pop(0)
                if f_ is not None:
                    f_()
            if hook is not None:
                hook()
        while pend:
            do_pv(*pend.pop(0))
        while flush and fin_q:
            f_ = fin_q.pop(0)
            if f_ is not None:
                f_()

    def head_inproj_thunks(h, pieces=False):
        buf = h % 2
        mk = inproj_fm_pieces if pieces else inproj_fm_thunks
        th = []
        th += mk(C_DZ + h * 128, 128, lambda g, p, bk: copy(
            "dve", mixT[:, 4 + h, g * 512:(g + 1) * 512], p, reads=[], writes=[("mix", 4 + h, g)], excl=[("ps", bk)]))
        th += mk(C_DQ + h * 128, 128, lambda g, p, bk: copy(
            "dve", dqT[buf][:, g * 512:(g + 1) * 512], p, reads=[], writes=[("A:dq", buf, g)], excl=[("ps", bk)]))
        th += mk(C_DK + h * 128, 128, lambda g, p, bk: copy(
            "dve", dkT[buf][:, g * 512:(g + 1) * 512], p, reads=[], writes=[("A:dk", buf, g)], excl=[("ps", bk)]))
        return th

    def head_gate(h):
        mk = [("mix", 4 + h, g) for g in range(NG)]
        dst = mixT[:, 4 + h, :]
        act(dst, dst, AF.Silu, reads=mk, writes=mk)
        ts("dve", dst, dst, dgain_sb[:, h:h + 1], 1.0 - LAM_INIT, ALU.mult, ALU.mult, reads=mk + ["dgain"], writes=mk)

    def run_diff(b):
        phase_barrier()
        ipbanks[0] = 2
        pg.add("pool", lambda e: e.memset(dv_aug[:, :, :, 128:129], 1.0), reads=[], writes=[("A:dvones",)])
        dv_th = inproj_tm_thunks(C_DV, lambda t, p, bk: copy("dve", dv_aug[:, t, :, 0:128], p.rearrange("p (h n) -> p h n", h=4),
                                                             reads=[("A:dvones",)], writes=[("A:dv", t)], excl=[("ps", bk)]))
        for f in head_inproj_thunks(0):
            f()
        n_pre = min(4, NT)
        for f in dv_th[:n_pre]:
            f()
        dv_late = dv_th[n_pre:]
        ipbanks[0] = 1
        fq = []
        for h in range(4):
            buf = h % 2
            head_gate(h)
            nxt = head_inproj_thunks(h + 1, pieces=True) if h < 3 else []
            if h == 0 and dv_late:
                merged = []
                while nxt or dv_late:
                    if dv_late:
                        merged.append((1, dv_late.pop(0)))
                    for _ in range(4):
                        if nxt:
                            merged.append((0, nxt.pop(0)))
                nxt = merged
            elif h == 3:
                nxt = [(0, f) for f in gate_thunks(b)]
            else:
                nxt = [(0, f) for f in nxt]
            def hook():
                budget = 2
                while nxt and budget > 0:
                    kind, f = nxt.pop(0)
                    f()
                    if kind == 0:
                        budget -= 1
            attention(b, h, buf, hook, fin_q=fq, flush=(h == 3))
            while nxt:
                nxt.pop(0)[1]()
            if h == 2:
                load_w_out()

    yslot = [0]

    wo_stage = Rbuf[:, 8 * D:16 * D].bitcast(F32).rearrange("p (k n) -> p k n", k=4)

    def load_w_out():
        hkeys = [("hT", g, e_) for g in range(NG) for e_ in ("d", "a")]
        for half in range(2):
            dma("sp", wo_stage, w_out_d[half * 512:(half + 1) * 512, :].rearrange("(k p) n -> p k n", p=128), ("wost", 0),
                writes=hkeys + ["wost"])
            copy("dve", w_out_sb[:, half * 4:(half + 1) * 4, :], wo_stage, reads=["wost"], writes=hkeys + [("wout", half)])

    def gate_thunks(b):
        def mk(kc):
            def f():
                ts("dve", diagg[:, :], csb["ident_f"][:, :], gcol[:, kc * 2 + b:kc * 2 + b + 1], None, ALU.mult, None,
                   reads=["ident_f", "gcol"], writes=["diagg"])
                mm_group(bank(0, 128, (kc % 4) * 128), [(csb["ones_f"][:, :], diagg[:, :])], reads=["ones_f", "diagg"], excl=[("ps", 0)])
                if kc % 4 == 3:
                    copy("dve", gate_bc[:, (kc // 4) * 512:(kc // 4 + 1) * 512], bank(0), reads=[], writes=["gate_bc"], excl=[("ps", 0)])
            return f
        return [mk(kc) for kc in range(8)]

    def run_out(b):
        hkeys = [("hT", g, e_) for g in range(NG) for e_ in ("d", "a")]
        wkeys = [("wout", 0), ("wout", 1)] + hkeys

        ybl = [ybuf[0], ybuf[1], ybuf3]
        ykl = [[("y", 0, 0), ("y", 0, 1)], [("y", 1, 0), ("y", 1, 1)], [("xn", 0), ("xn", 1)]]

        def out_stage1(t, s):
            tok = slice(t * 128, (t + 1) * 128)
            par = t % 2
            yb = ybl[t % 3]
            yk = ykl[t % 3]
            for half in range(2):
                bk = 2 * par + half
                pairs = [(mixT[:, kc, tok], w_out_sb[:, kc, half * 512:(half + 1) * 512]) for kc in range(8)]
                mm_group(bank(bk), pairs, reads=[("mix", kc, t // 4) for kc in range(8)] + wkeys, excl=[("ps", bk)])
                tt("dve", yb[:, half * 512:(half + 1) * 512], bank(bk), gate_bc[:, half * 512:(half + 1) * 512], ALU.mult,
                   reads=["gate_bc"], writes=[yk[half]], excl=[("ps", bk)])
            tt("pool", yb[:, :], yb[:, :], xt[s][:, :], ALU.add, reads=yk + [("xt", s)], writes=yk)

        def out_stage1c(t):
            i3 = t % 3
            yb = ybl[i3]
            yk = ykl[i3]
            ssk, rsk = "ss_p%d" % i3, "rstd_p%d" % i3
            jb = 4
            act(ps[:, jb * 512:jb * 512 + 1024], yb[:, :], AF.Square, reads=yk, writes=[ssk],
                excl=[("ps", jb), ("ps", jb + 1)], accum=ss_p[i3])
            rstd_from_ss(ss_p[i3], rstd_p[i3], D, ssk, rsk)

        def out_stage2(t):
            tok0 = b * S + t * 128
            i3 = t % 3
            yb = ybl[i3]
            yk = ykl[i3]
            stt(yb[:, :], yb[:, :], rstd_p[i3], fgain_sb[:, :], ALU.mult, ALU.mult, reads=yk + ["rstd_p%d" % i3, "fgain"], writes=yk)
            dma("act", y_d[tok0:tok0 + 128, :], yb[:, :], ("yout", i3), reads=yk)

        slots = {0: load_x(b * S)}
        for t in range(NT + 1):
            if t < NT:
                if t + 1 < NT:
                    slots[t + 1] = load_x(b * S + (t + 1) * 128)
                out_stage1(t, slots.pop(t))
            if t >= 1:
                out_stage2(t - 1)
            if t < NT:
                out_stage1c(t)

    for b in range(NSEQ):
        if upto == "mod":
            break
        DVE_KC = [0, 2, 3, 4, 6, 7]
        ACT_KC = [1, 5]

        def ht_stage1a(t, s):
            par = t % 2
            ssk, rsk = "ss_p%d" % par, "rstd_p%d" % par
            act(junk[:, :], xt[s][:, :], AF.Square, reads=[("xt", s)], writes=[("y", 1, 0), ("y", 1, 1), ssk], accum=ss_p[par])
            rstd_from_ss(ss_p[par], rstd_p[par], D, ssk, rsk)

        def ht_stage1b(t, s):
            par = t % 2
            xn = xnb[par][:, :]
            xk = ("xn", par)
            rsk = "rstd_p%d" % par
            ts("dve", xn, xt[s][:, :], rstd_p[par], None, ALU.mult, None, reads=[("xt", s), rsk], writes=[xk])
            bD, bA = 1 + 2 * par, 2 + 2 * par
            def ftr(e):
                ins = None
                for i, kc in enumerate(DVE_KC):
                    ins = e.transpose(bank_bf(bD)[:, i * 128:(i + 1) * 128], xn[:, kc * 128:(kc + 1) * 128], csb["ident_bf"][:, :])
                for i, kc in enumerate(ACT_KC):
                    ins = e.transpose(bank_bf(bA)[:, i * 128:(i + 1) * 128], xn[:, kc * 128:(kc + 1) * 128], csb["ident_bf"][:, :])
                return ins
            pg.add("pe", ftr, reads=[xk, "ident_bf"], excl=[("ps", bD), ("ps", bA)])

        def ht_stage2(t):
            par = t % 2
            bD, bA = 1 + 2 * par, 2 + 2 * par
            for i, kc in enumerate(DVE_KC):
                ts("dve", hT[:, kc, t * 128:(t + 1) * 128], bank_bf(bD)[:, i * 128:(i + 1) * 128],
                   acol[:, kc * 2 + b:kc * 2 + b + 1], scol[:, kc * 2 + b:kc * 2 + b + 1], ALU.mult, ALU.add,
                   reads=["acol", "scol"], writes=[("hT", t // 4, "d")], excl=[("ps", bD)])
            for i, kc in enumerate(ACT_KC):
                act(hT[:, kc, t * 128:(t + 1) * 128], bank_bf(bA)[:, i * 128:(i + 1) * 128], AF.Identity,
                    reads=["acol", "scol"], writes=[("hT", t // 4, "a")], excl=[("ps", bA)],
                    bias=scol[:, kc * 2 + b:kc * 2 + b + 1], scale=acol[:, kc * 2 + b:kc * 2 + b + 1])

        per_g = gla_begin(b)
        ipq = []
        pend_ev = []
        banksets = [[6, 7], [0, 5]]
        it = [0]

        ev_done = [0]

        def ip_step(n):
            while pend_ev:
                ev, bk = pend_ev.pop(0)
                ev(bk)
                ev_done[0] += 1
            for j in range(n):
                if ipq:
                    mm, ev = ipq.pop(0)
                    bk = banksets[it[0] % 2][j]
                    mm(bk)
                    pend_ev.append((ev, bk))
            it[0] += 1

        slots = {0: load_x(b * S)}
        for t in range(NT + 1):
            if t < NT:
                if t + 1 < NT:
                    slots[t + 1] = load_x(b * S + (t + 1) * 128)
                ht_stage1a(t, slots[t])
            if t >= 1:
                ht_stage2(t - 1)
                if (t - 1) % 4 == 3:
                    ipq.extend(per_g[(t - 1) // 4])
            if t < NT:
                ht_stage1b(t, slots.pop(t))
            ip_step(2)
        banksets[1] = [0, 1]
        n_per_group = len(per_g[0])

        def need_group(g):
            while ev_done[0] < n_per_group * (g + 1) and (ipq or pend_ev):
                ip_step(2)
        iptail = (ip_step, need_group, ipq, pend_ev)
        if b == 0:
            for kc in range(8):
                dump("hT%d" % kc, hT[:, kc, :], [("hT", g, e_) for g in range(4) for e_ in ("d", "a")])
        if upto == "hT":
            break
        run_gla(b, iptail)
        if b == 0:
            for hc in range(4):
                dump("mixg%d" % hc, mixT[:, hc, :], [("mix", hc, g) for g in range(NG)])
        if upto == "gla":
            break
        run_diff(b)
        if b == 0:
            for hc in range(4):
                dump("mixd%d" % hc, mixT[:, 4 + hc, :], [("mix", 4 + hc, g) for g in range(NG)])
        if upto == "diff":
            break
        run_out(b)

    pg.finalize(nc, stack)
    tail = [(pg.sems[(("dma", k), 0)], 16 * pg.dma_count[k]) for k in pg.dma_count if k[0] in ("dump", "yout")]
    last = Op("sp", lambda e: e.nop(), len(pg.ops), None)
    last.waits = tail
    last.signal = None
    pg.ops.append(last)
    pg.emit_all(nc)
    return nc, stack


def make_in_maps(x, c, w_ada, b_ada, norm_gain, w_in, w_gla_gate_up, b_gla_gate, gla_out_gain,
                 lambda_q1, lambda_k1, lambda_q2, lambda_k2, diff_out_gain, w_out, final_gain):
    f = lambda a: np.ascontiguousarray(np.asarray(a, dtype=np.float32))
    x = f(x); c = f(c)
    consts = _consts()
    shared = {
        "w_ada": f(w_ada[0]),
        "b_adaT": f(np.asarray(b_ada[0]).reshape(24, 128).T),
        "ngainT": f(np.asarray(norm_gain[0]).reshape(8, 128).T),
        "w_in": f(w_in[0]),
        "wup_aug": f(np.concatenate([np.asarray(w_gla_gate_up[0]), np.asarray(b_gla_gate[0])[None, :]], axis=0)),
        "ggainT": f(np.asarray(gla_out_gain[0]).reshape(4, 128).T),
        "dgainT": f(np.asarray(diff_out_gain[0]).reshape(4, 128).T),
        "lam_bc": f(np.broadcast_to(np.concatenate([np.asarray(lambda_q1[0]), np.asarray(lambda_k1[0]),
                                                    np.asarray(lambda_q2[0]), np.asarray(lambda_k2[0])])[None, :], (128, 256))),
        "w_out": f(w_out[0]),
        "fgain_bc": f(np.broadcast_to(np.asarray(final_gain)[None, :], (128, D))),
    }
    shared.update(consts)
    in_maps = []
    for i in range(NCORES):
        m = dict(shared)
        m["x"] = np.ascontiguousarray(x[2 * i:2 * i + 2].reshape(NSEQ * S, D))
        m["cT"] = np.ascontiguousarray(c[2 * i:2 * i + 2].reshape(2, 8, 128).transpose(2, 1, 0).reshape(128, 16))
        in_maps.append(m)
    return in_maps


def kernel(**inputs):
    in_maps = make_in_maps(**inputs)
    nc, stack = build_program()
    with stack:
        res = run_bass_kernel_spmd(nc, in_maps, core_ids=list(range(NCORES)))
    outs = [np.asarray(r["y"], dtype=np.float32).reshape(NSEQ, S, D) for r in res.results]
    return np.concatenate(outs, axis=0)
```

```python
import math
from contextlib import ExitStack

import numpy as np
import ml_dtypes

import concourse.bass as bass
import concourse.mybir as mybir
from concourse.bass_utils import run_bass_kernel_spmd

F32 = mybir.dt.float32
BF16 = mybir.dt.bfloat16
AF = mybir.ActivationFunctionType
ALU = mybir.AluOpType
AX = mybir.AxisListType

NCORES = 8
D = 1024
S = 2048
NSEQ = 2
NT = S // 128
D_IN = 3600
EPS = 1e-6
LAM_INIT = 0.8 - 0.6 * math.exp(-0.3 * 0)
SLOPES = [2.0 ** (-8.0 * (h + 1) / 4) for h in range(4)]
C_GQ, C_GK, C_GV, C_GZ, C_GR = 0, 256, 512, 1024, 1536
C_DQ, C_DK, C_DV, C_DZ = 1552, 2064, 2576, 3088

EPOCH = 12000


class Op:
    __slots__ = ("eng", "emit", "idx", "deps", "dma_key", "signal", "cnt", "waits")

    def __init__(self, eng, emit, idx, dma_key):
        self.eng = eng
        self.emit = emit
        self.idx = idx
        self.dma_key = dma_key
        self.deps = {}
        self.signal = False
        self.cnt = 0
        self.waits = []


class Prog:
    ENGS = ("pe", "act", "dve", "pool", "sp")

    def __init__(self):
        self.ops = []
        self.last_w = {}
        self.readers = {}
        self.xacc = {}
        self.dma_count = {}
        self.bulk = set()

    def _chan(self, op):
        return ("dma", op.dma_key) if op.dma_key is not None else op.eng

    def add(self, eng, emit, reads=(), writes=(), excl=(), dma_key=None, bulk=False):
        op = Op(eng, emit, len(self.ops), dma_key)
        self.ops.append(op)
        if dma_key is not None:
            self.dma_count[dma_key] = self.dma_count.get(dma_key, 0) + 1
            op.cnt = self.dma_count[dma_key]
            if bulk:
                self.bulk.add(dma_key)
        me = self._chan(op)
        deps = {}
        if any(isinstance(k, tuple) and isinstance(k[0], str) and k[0].startswith("A:") for k in list(reads) + list(writes)):
            reads = list(reads) + ["AR"]

        def dep(i):
            if i is None:
                return
            p = self.ops[i]
            c = self._chan(p)
            if c == "pe" and me == "pe":
                return
            if c == me and op.dma_key is not None and op.dma_key in self.bulk:
                return
            if c not in deps or deps[c] < i:
                deps[c] = i

        for k in reads:
            dep(self.last_w.get(k))
        for k in writes:
            dep(self.last_w.get(k))
            for c, i in self.readers.get(k, {}).items():
                dep(i)
        for k in excl:
            for c, i in self.xacc.get(k, {}).items():
                if c != me:
                    dep(i)
        for k in reads:
            self.readers.setdefault(k, {})[me] = op.idx
        for k in writes:
            self.last_w[k] = op.idx
            self.readers[k] = {}
        for k in excl:
            self.xacc[k] = {me: op.idx}
        op.deps = deps
        for c, i in deps.items():
            self.ops[i].signal = True
        return op

    def finalize(self, nc, stack):
        counters = {}
        for op in self.ops:
            if op.dma_key is None and op.signal:
                counters[op.eng] = counters.get(op.eng, 0) + 1
                op.cnt = counters[op.eng]
        self.sems = {}

        def sem_for(name):
            if name not in self.sems:
                self.sems[name] = stack.enter_context(nc.semaphore("s%d" % len(self.sems)))
            return self.sems[name]

        def target(p):
            if p.dma_key is not None:
                n = self.dma_count[p.dma_key] if p.dma_key in self.bulk else p.cnt
                return ("dma", p.dma_key), 0, 16 * n
            e = (p.cnt - 1) // EPOCH
            return p.eng, e, (p.cnt - 1) % EPOCH + 1

        waited = {e: {} for e in self.ENGS}
        for op in self.ops:
            w = waited[op.eng]
            for c, i in op.deps.items():
                chan, ep, val = target(self.ops[i])
                if w.get(chan, (-1, 0)) >= (ep, val):
                    continue
                w[chan] = (ep, val)
                op.waits.append((sem_for((chan, ep)), val))
        for op in self.ops:
            if op.dma_key is not None:
                op.signal = (sem_for((("dma", op.dma_key), 0)), 16)
            elif op.signal:
                _, ep, _ = target(op)
                op.signal = (sem_for((op.eng, ep)), 1)
            else:
                op.signal = None

    def emit_all(self, nc):
        by_eng = {e: [o for o in self.ops if o.eng == e] for e in self.ENGS}

        def run(engine, ops):
            for op in ops:
                for sem, val in op.waits:
                    engine.wait_ge(sem, val)
                ins = op.emit(engine)
                if op.signal is not None:
                    ins.then_inc(op.signal[0], op.signal[1])

        with nc.Block() as block:
            @block.sync
            def _(e):
                run(e, by_eng["sp"])

            @block.tensor
            def _(e):
                run(e, by_eng["pe"])

            @block.scalar
            def _(e):
                run(e, by_eng["act"])

            @block.vector
            def _(e):
                run(e, by_eng["dve"])

            @block.gpsimd
            def _(e):
                run(e, by_eng["pool"])


def _consts():
    j = np.arange(128)
    c = {}
    c["ident_bf"] = np.eye(128, dtype=np.float32).astype(ml_dtypes.bfloat16)
    c["ident_f"] = np.eye(128, dtype=np.float32)
    c["ones_f"] = np.ones((128, 128), np.float32)
    c["maskT_bf"] = (j[None, :] >= j[:, None]).astype(np.float32).astype(ml_dtypes.bfloat16)
    c["triI_f"] = np.where(j[:, None] <= j[None, :], -1.0 / 16.0, 0.0).astype(np.float32)
    c["triU_f"] = np.where(j[:, None] > j[None, :], -1.0 / 16.0, 0.0).astype(np.float32)
    bt = np.zeros((128, 4, 16), np.float32)
    for h in range(4):
        for d in range(16):
            bt[:, h, d] = SLOPES[h] * (j - 127 - 128 * d)
    c["bias_tab"] = bt.reshape(128, 64)
    return c


CONST_SPECS = [
    ("ident_bf", [128, 128], BF16), ("ident_f", [128, 128], F32), ("ones_f", [128, 128], F32),
    ("maskT_bf", [128, 128], BF16), ("triI_f", [128, 128], F32), ("triU_f", [128, 128], F32),
    ("bias_tab", [128, 64], F32),
]


def build_program(upto="all", dumps=()):
    nc = bass.Bass("TRN2", target_bir_lowering=False)
    pg = Prog()
    stack = ExitStack()
    dram = {}

    def din(name, shape, dt=F32):
        dram[name] = nc.dram_tensor(name, list(shape), dt, kind="ExternalInput").ap()
        return dram[name]

    x_d = din("x", [NSEQ * S, D])
    cT_d = din("cT", [128, 16])
    w_ada_d = din("w_ada", [D, 3 * D])
    b_adaT_d = din("b_adaT", [128, 24])
    ngain_d = din("ngainT", [128, 8])
    w_in_d = din("w_in", [D, D_IN])
    wup_d = din("wup_aug", [17, 256])
    ggain_d = din("ggainT", [128, 4])
    dgain_d = din("dgainT", [128, 4])
    lam_d = din("lam_bc", [128, 256])
    w_out_d = din("w_out", [D, D])
    fgain_d = din("fgain_bc", [128, D])
    cd = {n: din(n, shp, dt) for n, shp, dt in CONST_SPECS}
    y_d = nc.dram_tensor("y", [NSEQ * S, D], F32, kind="ExternalOutput").ap()
    dump_d = {}
    for name, shape in dumps:
        dump_d[name] = nc.dram_tensor("dbg_" + name, list(shape), F32, kind="ExternalOutput").ap()

    def sb(name, shape, dt):
        return stack.enter_context(nc.sbuf_tensor("sb_" + name, list(shape), dt))

    stack.enter_context(nc.allow_low_precision("bf16 matmul operands, fp32 accumulation"))

    w_in_sb = sb("w_in_sb", [128, 8, D_IN], BF16)
    Rbuf = sb("Rbuf", [128, max(8 * S, 16 * D)], BF16)
    hT = Rbuf[:, 0:8 * S].rearrange("p (k t) -> p k t", k=8)
    w_out_sb = Rbuf[:, 0:8 * D].rearrange("p (k n) -> p k n", k=8)
    mixbuf = sb("mixbuf", [128, max(8 * S, 16384)], BF16)
    mixT = mixbuf[:, 0:8 * S].rearrange("p (k t) -> p k t", k=8)
    NXT = 2
    xt = [sb("xt%d" % i, [128, D], F32) for i in range(NXT)]
    xnb_all = sb("xnb_all", [128, 2 * D], BF16)
    xnb = [xnb_all[:, 0:D], xnb_all[:, D:2 * D]]
    ybuf3 = xnb_all[:, :].bitcast(F32)
    ybuf = [sb("ybuf%d" % i, [128, D], F32) for i in range(2)]
    junk = ybuf[1]
    gate_bc = sb("gate_bc", [128, D], F32)
    fgain_sb = sb("fgain_sb", [128, D], F32)
    csb = {n: sb(n, shp, dt) for n, shp, dt in CONST_SPECS}
    cT_sb = sb("cT_sb", [128, 16], F32)
    b_adaT_sb = sb("b_adaT_sb", [128, 24], F32)
    ngain_sb = sb("ngain_sb", [128, 8], F32)
    ggain_sb = sb("ggain_sb", [128, 4], F32)
    dgain_sb = sb("dgain_sb", [128, 4], F32)
    lam_sb = sb("lam_sb", [128, 256], F32)
    wup_f = sb("wup_f", [32, 256], F32)
    wup_bf = sb("wup_bf", [32, 256], BF16)
    small = sb("small", [128, 256], F32)
    modT = sb("modT", [128, 48], F32)
    acol = sb("acol", [128, 16], F32)
    scol = sb("scol", [128, 16], F32)
    gcol = sb("gcol", [128, 16], F32)
    diagg = sb("diagg", [128, 128], F32)
    arena = sb("arena", [128, 24576], BF16)
    ps = stack.enter_context(nc.psum_tensor("ps", [128, 4096], F32))

    def bank(b, n=512, off=0):
        return ps[:, b * 512 + off: b * 512 + off + n]

    def bank_bf(b):
        return ps[:, b * 512:(b + 1) * 512].bitcast(BF16)

    SM = {}
    _sm_next = [0]

    def smcol(name, n=1):
        SM[name] = small[:, _sm_next[0]:_sm_next[0] + n]
        _sm_next[0] += n
        return SM[name]

    eps_col = smcol("eps_col", 1)

    def dma(queue, out, in_, key, reads=(), writes=(), bulk=False, **kw):
        pg.add(queue, lambda e: e.dma_start(out=out, in_=in_, **kw), reads=reads, writes=writes,
               dma_key=key, bulk=bulk)

    def dump(name, src_ap, reads):
        if name in dump_d:
            dma("pool", dump_d[name], src_ap, ("dump", name), reads=reads, max_dma_last_dim=4096)

    def act(out, in_, func, reads, writes, excl=(), bias=0.0, scale=1.0, accum=None):
        def f(e):
            kw = {}
            if accum is not None:
                kw["accum_out"] = accum
            return e.activation(out=out, in_=in_, func=func, bias=bias, scale=scale, **kw)
        pg.add("act", f, reads=reads, writes=writes, excl=excl)

    def tt(eng, out, in0, in1, op, reads, writes, excl=()):
        pg.add(eng, lambda e: e.tensor_tensor(out=out, in0=in0, in1=in1, op=op),
               reads=reads, writes=writes, excl=excl)

    def ts(eng, out, in0, s1, s2, op0, op1, reads, writes, excl=()):
        if s2 is None:
            pg.add(eng, lambda e: e.tensor_scalar(out=out, in0=in0, scalar1=s1, scalar2=None, op0=op0),
                   reads=reads, writes=writes, excl=excl)
        else:
            pg.add(eng, lambda e: e.tensor_scalar(out=out, in0=in0, scalar1=s1, scalar2=s2, op0=op0, op1=op1),
                   reads=reads, writes=writes, excl=excl)

    def stt(out, in0, scalar, in1, op0, op1, reads, writes, excl=()):
        pg.add("dve", lambda e: e.scalar_tensor_tensor(out=out, in0=in0, scalar=scalar, in1=in1, op0=op0, op1=op1),
               reads=reads, writes=writes, excl=excl)

    def copy(eng, out, in_, reads, writes, excl=()):
        pg.add(eng, lambda e: e.tensor_copy(out=out, in_=in_), reads=reads, writes=writes, excl=excl)

    def mm_group(out, pairs, reads, excl, writes=(), first_start=True, last_stop=True):
        def f(e):
            ins = None
            n = len(pairs)
            for i, (l, r) in enumerate(pairs):
                ins = e.matmul(out, l, r, start=(first_start and i == 0), stop=(last_stop and i == n - 1),
                               skip_group_check=not (first_start and last_stop))
            return ins
        pg.add("pe", f, reads=reads, writes=writes, excl=excl)

    def rstd_from_ss(ss, rstd, n, key_ss, key_rstd):
        act(ss, ss, AF.Ln, reads=[key_ss, "eps_col"], writes=[key_ss], scale=1.0 / n, bias=eps_col)
        act(rstd, ss, AF.Exp, reads=[key_ss], writes=[key_rstd], scale=-0.5)

    pg.add("pool", lambda e: e.memset(eps_col, EPS), reads=[], writes=["eps_col"])
    for n, shp, dt in CONST_SPECS:
        dma("sp", csb[n][:, :], cd[n], "const", writes=[n], bulk=True)
    dma("sp", cT_sb[:, :], cT_d, "const", writes=["cT"], bulk=True)
    dma("sp", b_adaT_sb[:, :], b_adaT_d, "const", writes=["b_adaT"], bulk=True)
    dma("sp", ngain_sb[:, :], ngain_d, "const", writes=["ngain"], bulk=True)
    dma("sp", ggain_sb[:, :], ggain_d, "const", writes=["ggain"], bulk=True)
    dma("sp", dgain_sb[:, :], dgain_d, "const", writes=["dgain"], bulk=True)
    dma("sp", lam_sb[:, :], lam_d, "const", writes=["lam_in"], bulk=True)
    dma("sp", wup_f[0:17, :], wup_d, "const", writes=["wup_f"], bulk=True)
    dma("sp", fgain_sb[:, :], fgain_d, "const", writes=["fgain"], bulk=True)
    copy("dve", wup_bf[0:17, :], wup_f[0:17, :], reads=["wup_f"], writes=["wup_bf"])

    sc = smcol("sc", 16)
    tmp16 = smcol("tmp16", 16)
    act(tmp16, cT_sb[:, :], AF.Exp, reads=["cT"], writes=["tmp16"], scale=-1.0)
    ts("dve", tmp16, tmp16, 1.0, None, ALU.add, None, reads=["tmp16"], writes=["tmp16"])
    pg.add("dve", lambda e: e.reciprocal(out=tmp16, in_=tmp16), reads=["tmp16"], writes=["tmp16"])
    tt("dve", sc, cT_sb[:, :], tmp16, ALU.mult, reads=["cT", "tmp16"], writes=["sc"])

    sc_bf = sb("sc_bf", [128, 16], BF16)
    copy("dve", sc_bf[:, :], sc, reads=["sc"], writes=["sc_bf"])
    slab = [mixbuf[:, 0:8192].bitcast(F32).rearrange("p (k n) -> p k n", k=8),
            mixbuf[:, 8192:16384].bitcast(F32).rearrange("p (k n) -> p k n", k=8)]
    slab_bf = [Rbuf[:, 0:4096].rearrange("p (k n) -> p k n", k=8), Rbuf[:, 4096:8192].rearrange("p (k n) -> p k n", k=8)]
    w_ada_v = w_ada_d.rearrange("(k p) n -> p k n", p=128)
    for sl in range(6):
        i2 = sl % 2
        dma("sp", slab[i2], w_ada_v[:, :, sl * 512:(sl + 1) * 512], ("slab", i2), writes=[("slab", i2)])
        copy("dve", slab_bf[i2], slab[i2], reads=[("slab", i2)], writes=[("slabbf", i2)])
        for jj in range(4):
            j = sl * 4 + jj
            pairs = [(slab_bf[i2][:, kc, jj * 128:(jj + 1) * 128], sc_bf[:, kc * 2:(kc + 1) * 2]) for kc in range(8)]
            mm_group(bank(0, 2, j * 2), pairs, reads=[("slabbf", i2), "sc_bf"], excl=[("ps", 0)])
    tt("dve", modT[:, :].rearrange("p (j b) -> p j b", b=2), bank(0, 48).rearrange("p (j b) -> p j b", b=2),
       b_adaT_sb[:, :].unsqueeze(2).to_broadcast([128, 24, 2]), ALU.add,
       reads=["b_adaT"], writes=["modT", ("slab", 0), ("slab", 1), ("slabbf", 0), ("slabbf", 1)], excl=[("ps", 0)])
    ts("dve", acol[:, :], modT[:, 16:32], 1.0, None, ALU.add, None, reads=["modT"], writes=["acol"])
    tt("dve", acol[:, :].rearrange("p (k b) -> p k b", b=2), acol[:, :].rearrange("p (k b) -> p k b", b=2),
       ngain_sb[:, :].unsqueeze(2).to_broadcast([128, 8, 2]), ALU.mult, reads=["acol", "ngain"], writes=["acol"])
    copy("dve", scol[:, :], modT[:, 0:16], reads=["modT"], writes=["scol"])
    copy("dve", gcol[:, :], modT[:, 32:48], reads=["modT"], writes=["gcol"])
    dump("acol", acol[:, :], ["acol"])
    dump("scol", scol[:, :], ["scol"])
    dump("gcol", gcol[:, :], ["gcol"])

    lam_t = smcol("lam_t", 2)
    neg_lam = smcol("neg_lam", 1)
    lamprod = smcol("lamprod", 128)
    tt("dve", lamprod.rearrange("p (a d) -> p a d", a=2), lam_sb[:, :].rearrange("p (a t d) -> p a t d", a=2, t=2)[:, :, 0, :],
       lam_sb[:, :].rearrange("p (a t d) -> p a t d", a=2, t=2)[:, :, 1, :], ALU.mult,
       reads=["lam_in"], writes=["lamprod"])
    pg.add("dve", lambda e: e.tensor_reduce(out=lam_t, in_=lamprod.rearrange("p (a d) -> p a d", a=2), axis=AX.X, op=ALU.add),
           reads=["lamprod"], writes=["lam_t"])
    act(lam_t, lam_t, AF.Exp, reads=["lam_t"], writes=["lam_t"])
    stt(neg_lam, lam_t[:, 1:2], -LAM_INIT, lam_t[:, 0:1], ALU.add, ALU.subtract, reads=["lam_t"], writes=["neg_lam"])
    dump("neg_lam", neg_lam, ["neg_lam"])

    wst = [arena[:, i * 7200:(i + 1) * 7200].bitcast(F32) for i in range(3)]
    for kc in range(8):
        sl = kc % 3
        dma("sp", wst[sl], w_in_d[kc * 128:(kc + 1) * 128, :], ("wst", sl), writes=[("A:wst", sl)])
        copy("dve", w_in_sb[:, kc, :], wst[sl], reads=[("A:wst", sl)], writes=[("win", kc)])

    xslot = [0]

    def load_x(tok0):
        s = xslot[0] % NXT
        xslot[0] += 1
        dma("sp", xt[s][:, :], x_d[tok0:tok0 + 128, :], ("xt", s), writes=[("xt", s)])
        return s

    ss_c = smcol("ss_c", 1)
    rstd_c = smcol("rstd_c", 1)
    ss_p = [smcol("ss_p%d" % i, 1) for i in range(3)]
    rstd_p = [smcol("rstd_p%d" % i, 1) for i in range(3)]
    ss4_p = [smcol("ss4_p%d" % i, 4) for i in range(2)]
    rstd4_p = [smcol("rstd4_p%d" % i, 4) for i in range(2)]


    NG = S // 512
    arena_f = arena[:, :].bitcast(F32)
    dummy = smcol("dummy", 1)
    ipb = [0]

    def phase_barrier():
        pg.add("dve", lambda e: e.memset(dummy, 0.0), reads=[], writes=["AR", "dummy"])

    def win_keys(col):
        return [("win", kc) for kc in range(8)]

    evq = [0]

    evpref = [None]

    def evac_copy(out, in_, reads, writes, excl):
        evq[0] += 1
        use_act = (evq[0] % 2 == 0) if evpref[0] is None else (evpref[0] == "act")
        if use_act:
            act(out, in_, AF.Copy, reads=reads, writes=writes, excl=excl)
        else:
            copy("dve", out, in_, reads=reads, writes=writes, excl=excl)

    ipbanks = [2]
    ipbl = [[0, 1]]

    def inproj_fm_thunks(col0, M, evac):
        def mk(g):
            def f():
                bk = ipbl[0][ipb[0] % min(ipbanks[0], len(ipbl[0]))]
                ipb[0] += 1
                pairs = [(w_in_sb[:, kc, col0:col0 + M], hT[:, kc, g * 512:(g + 1) * 512]) for kc in range(8)]
                mm_group(bank(bk)[0:M, :], pairs, reads=win_keys(col0) + [("hT", g, "d"), ("hT", g, "a")], excl=[("ps", bk)])
                evac(g, bank(bk)[0:M, :], bk)
            return f
        return [mk(g) for g in range(NG)]

    def inproj_fm(col0, M, evac):
        for f in inproj_fm_thunks(col0, M, evac):
            f()

    def inproj_fm_pairs(col0, M, evac):
        def mk(g):
            def mm(bk):
                pairs = [(w_in_sb[:, kc, col0:col0 + M], hT[:, kc, g * 512:(g + 1) * 512]) for kc in range(8)]
                mm_group(bank(bk)[0:M, :], pairs, reads=win_keys(col0) + [("hT", g, "d"), ("hT", g, "a")], excl=[("ps", bk)])
            def ev(bk):
                evac(g, bank(bk)[0:M, :], bk)
            return (mm, ev)
        return [mk(g) for g in range(NG)]

    def inproj_fm_pieces(col0, M, evac, npieces=4):
        out_list = []
        per = 8 // npieces
        for g in range(NG):
            for pc in range(npieces):
                def f(g=g, pc=pc):
                    bk = 0
                    pairs = [(w_in_sb[:, kc, col0:col0 + M], hT[:, kc, g * 512:(g + 1) * 512])
                             for kc in range(pc * per, (pc + 1) * per)]
                    mm_group(bank(bk)[0:M, :], pairs, reads=win_keys(col0) + [("hT", g, "d"), ("hT", g, "a")], excl=[("ps", bk)],
                             first_start=(pc == 0), last_stop=(pc == npieces - 1))
                    if pc == npieces - 1:
                        evac(g, bank(bk)[0:M, :], bk)
                out_list.append(f)
        return out_list

    def inproj_tm_pairs(col0, evac):
        def mk(t):
            def mm(bk):
                pairs = [(hT[:, kc, t * 128:(t + 1) * 128], w_in_sb[:, kc, col0:col0 + 512]) for kc in range(8)]
                mm_group(bank(bk), pairs, reads=win_keys(col0) + [("hT", t // 4, "d"), ("hT", t // 4, "a")], excl=[("ps", bk)])
            def ev(bk):
                evac(t, bank(bk), bk)
            return (mm, ev)
        return [mk(t) for t in range(NT)]

    def inproj_tm_thunks(col0, evac):
        def mk(t):
            def f():
                bk = ipbl[0][ipb[0] % min(ipbanks[0], len(ipbl[0]))]
                ipb[0] += 1
                pairs = [(hT[:, kc, t * 128:(t + 1) * 128], w_in_sb[:, kc, col0:col0 + 512]) for kc in range(8)]
                mm_group(bank(bk), pairs, reads=win_keys(col0) + [("hT", t // 4, "d"), ("hT", t // 4, "a")], excl=[("ps", bk)])
                evac(t, bank(bk), bk)
            return f
        return [mk(t) for t in range(NT)]

    def inproj_tm(col0, evac):
        for f in inproj_tm_thunks(col0, evac):
            f()

    gqT = arena[:, 0:2 * S].rearrange("p (c t) -> p c t", c=2)
    gkT = arena[:, 2 * S:4 * S].rearrange("p (c t) -> p c t", c=2)
    gv = arena[:, 4 * S:8 * S].rearrange("p (t n) -> p t n", n=512)
    grT = arena[0:32, 8 * S:9 * S]
    tb = 9 * S
    tf = tb // 2
    laA = [arena_f[:, tf + i * 512:tf + (i + 1) * 512] for i in range(2)]
    ebA = [arena_f[:, tf + 1024 + i * 512:tf + 1024 + (i + 1) * 512] for i in range(2)]
    enbA = [arena_f[:, tf + 2048 + i * 512:tf + 2048 + (i + 1) * 512] for i in range(2)]
    assert 2 * (tf + 3072) <= 24576
    kinT = [arena[:, tb + i * 256:tb + (i + 1) * 256] for i in range(2)]
    AT = [arena[:, tb + 512 + i * 512:tb + 512 + (i + 1) * 512].rearrange("p (h t) -> p h t", h=4) for i in range(2)]
    on = [arena[:, tb + 1536 + i * 512:tb + 1536 + (i + 1) * 512].rearrange("p (h t) -> p h t", h=4) for i in range(2)]
    S_bf = arena[:, tb + 2560:tb + 2816].rearrange("p (c t) -> p c t", c=2)
    fB = (tb + 2816) // 2
    S32 = arena_f[:, fB:fB + 256]
    Sd = arena_f[:, fB + 256:fB + 512]
    osq = arena_f[:, fB + 512:fB + 1024]
    assert 2 * (fB + 1024) <= 24576
    dec_all = smcol("dec_all", 2 * NT)
    ss4 = smcol("ss4", 4)
    rstd4 = smcol("rstd4", 4)
    QORD = [0, 2, 1, 3]

    def dec_col(t, c):
        i = (t // 2) * 4 + c * 2 + (t % 2)
        return dec_all[:, i:i + 1]

    def gla_begin(b):
        phase_barrier()
        pg.add("pool", lambda e: e.memset(grT, 1.0), reads=[], writes=[("A:gr",)])
        per_g = {g: [] for g in range(NG)}
        def add_fm(col0, M, evac):
            for g, f in enumerate(inproj_fm_pairs(col0, M, evac)):
                per_g[g].append(f)
        add_fm(C_GR, 16, lambda g, p, bk: copy("dve", grT[0:16, g * 512:(g + 1) * 512], p, reads=[], writes=[("A:gr",)], excl=[("ps", bk)]))
        for c in range(2):
            add_fm(C_GQ + c * 128, 128, lambda g, p, bk, c=c: evac_copy(
                gqT[:, c, g * 512:(g + 1) * 512], p, reads=[], writes=[("A:gq", c, g)], excl=[("ps", bk)]))
            add_fm(C_GK + c * 128, 128, lambda g, p, bk, c=c: evac_copy(
                gkT[:, c, g * 512:(g + 1) * 512], p, reads=[], writes=[("A:gk", c, g)], excl=[("ps", bk)]))
        for hc in range(4):
            add_fm(C_GZ + hc * 128, 128, lambda g, p, bk, hc=hc: evac_copy(
                mixT[:, hc, g * 512:(g + 1) * 512], p, reads=[], writes=[("mix", hc, g)], excl=[("ps", bk)]))
        for t, f in enumerate(inproj_tm_pairs(C_GV, lambda t, p, bk: evac_copy(gv[:, t, :], p, reads=[], writes=[("A:gv", t)], excl=[("ps", bk)]))):
            per_g[t // 4].append(f)
        return per_g

    def gla_gate():
        for hc in range(4):
            mk = [("mix", hc, g) for g in range(NG)]
            dst = mixT[:, hc, :]
            act(dst, dst, AF.Silu, reads=mk, writes=mk)
            ts("dve", dst, dst, ggain_sb[:, hc:hc + 1], None, ALU.mult, None, reads=mk + ["ggain"], writes=mk)

    def run_gla(b, iptail):
        ipbanks[0] = 2
        ipbl[0] = [0, 1]
        ip_step, need_group, ipq, pend_ev = iptail
        gv_thunks = []

        def zstage(p):
            pp = p % 2
            zb = 2 + 2 * pp
            def fz(e):
                ins = None
                for j in range(2):
                    tok = slice(p * 256 + j * 128, p * 256 + (j + 1) * 128)
                    ins = e.matmul(bank(zb, 256, j * 256), grT[0:17, tok], wup_bf[0:17, :], start=True, stop=True)
                return ins
            pg.add("pe", fz, reads=[("A:gr",), "wup_bf"], excl=[("ps", zb)])
            act(laA[pp], bank(zb), AF.Exp, reads=[], writes=[("A:la", pp)], excl=[("ps", zb)], scale=-1.0)
            act(laA[pp], laA[pp], AF.Ln, reads=[("A:la", pp)], writes=[("A:la", pp)], bias=1.0)

        def cstage(p):
            pp = p % 2
            g = p // 2
            bb = 3 + 2 * pp
            tk = slice(p * 256, (p + 1) * 256)
            def fcs(e):
                ins = None
                for c in range(2):
                    for j in range(2):
                        ins = e.matmul(bank(bb, 128, (c * 2 + j) * 128), laA[pp][:, j * 256 + c * 128:j * 256 + (c + 1) * 128],
                                       csb["triI_f"][:, :], start=True, stop=True)
                return ins
            pg.add("pe", fcs, reads=[("A:la", pp), "triI_f"], excl=[("ps", bb)])
            act(ebA[pp], bank(bb), AF.Exp, reads=[], writes=[("A:eb", pp)], excl=[("ps", bb)], bias=math.log(0.125))
            act(enbA[pp], bank(bb), AF.Exp, reads=[], writes=[("A:enb", pp)], excl=[("ps", bb)], scale=-1.0)
            act(dec_all[:, p * 4:(p + 1) * 4].unsqueeze(2), bank(bb).rearrange("p (i t) -> p i t", i=4)[:, :, 127:128], AF.Exp,
                reads=[], writes=[("dec", p)], excl=[("ps", bb)])
            tt("pool", gqT[:, :, tk], gqT[:, :, tk], ebA[pp].rearrange("p (c t) -> p c t", c=2), ALU.mult,
               reads=[("A:gq", 0, g), ("A:gq", 1, g), ("A:eb", pp)], writes=[("A:qin", p)])
            tt("pool", gkT[:, :, tk], gkT[:, :, tk], enbA[pp].rearrange("p (c t) -> p c t", c=2), ALU.mult,
               reads=[("A:gk", 0, g), ("A:gk", 1, g), ("A:enb", pp)], writes=[("A:kin", p)])

        NP = NT // 2
        need_group(0)
        zstage(0)
        for p in range(NP):
            if p + 1 < NP:
                need_group((p + 1) // 2)
                zstage(p + 1)
            if ipq or pend_ev:
                ip_step(2)
                ip_step(2)
            cstage(p)
        while ipq or pend_ev:
            ip_step(2)
        gla_gate()
        pg.add("dve", lambda e: e.memset(S32, 0.0),
               reads=[("A:la", 0), ("A:la", 1), ("A:eb", 0), ("A:eb", 1), ("A:enb", 0), ("A:enb", 1)],
               writes=[("A:S32",), ("A:la", 0), ("A:la", 1), ("A:eb", 0), ("A:eb", 1), ("A:enb", 0), ("A:enb", 1)])
        pg.add("pool", lambda e: e.memset(Sd, 0.0), reads=[("A:S32",)], writes=[("A:Sd",)])
        bkey = [("A:la", 0)]

        def pre(t):
            tok = slice(t * 128, (t + 1) * 128)
            p = t // 2
            par = t % 2
            ktp = bank_bf(1)[:, 0:256]
            def fkt(e):
                e.transpose(ktp[:, 0:128], gkT[:, 0, tok], csb["ident_bf"][:, :])
                return e.transpose(ktp[:, 128:256], gkT[:, 1, tok], csb["ident_bf"][:, :])
            pg.add("pe", fkt, reads=[("A:kin", p), "ident_bf"], excl=[("ps", 1)])
            act(kinT[par], ktp, AF.Copy, reads=bkey, writes=[("A:kinT", par)], excl=[("ps", 1)])
            def fsc(e):
                ins = None
                for h in range(4):
                    c = h // 2
                    rows = slice(0, 64) if h % 2 == 0 else slice(64, 128)
                    ins = e.matmul(bank(2 + h % 2, 128, c * 128), gkT[rows, c, tok], gqT[rows, c, tok], start=True, stop=True)
                return ins
            pg.add("pe", fsc, reads=[("A:kin", p), ("A:qin", p)], excl=[("ps", 2), ("ps", 3)])
            for q in range(2):
                tt("dve", AT[par][:, q::2, :], bank(2 + q, 256).rearrange("p (c t) -> p c t", c=2),
                   csb["maskT_bf"][:, :].unsqueeze(1).to_broadcast([128, 2, 128]), ALU.mult,
                   reads=["maskT_bf"] + bkey, writes=[("A:AT", par, q)], excl=[("ps", 2 + q)])

        def obanks(t):
            return (6, 7) if t % 2 == 0 else (4, 5)

        def main_o(t):
            tok = slice(t * 128, (t + 1) * 128)
            p = t // 2
            par = t % 2
            ob = obanks(t)
            def fo_(e):
                ins = None
                for h in range(4):
                    ins = e.matmul(bank(ob[h % 2], 128, (h // 2) * 128), AT[par][:, h, :], gv[:, t, h * 128:(h + 1) * 128],
                                   start=(h < 2), stop=(t == 0), skip_group_check=True)
                if t > 0:
                    for h in range(4):
                        c = h // 2
                        rows = slice(0, 64) if h % 2 == 0 else slice(64, 128)
                        ins = e.matmul(bank(ob[h % 2], 128, c * 128), gqT[rows, c, tok], S_bf[rows, c, :],
                                       start=False, stop=True, skip_group_check=True)
                return ins
            pg.add("pe", fo_, reads=[("A:AT", par, 0), ("A:AT", par, 1), ("A:gv", t), ("A:qin", p), ("A:Sbf",)],
                   excl=[("ps", ob[0]), ("ps", ob[1])])
            ssk = "ss4_p%d" % par
            for q in range(4):
                act(osq[:, q * 128:(q + 1) * 128], bank(ob[q // 2], 128, (q % 2) * 128), AF.Square, reads=bkey,
                    writes=[("A:osq", q), (ssk, q)], excl=[("ps", ob[q // 2])], accum=ss4_p[par][:, q:q + 1])

        def main_kv(t):
            p = t // 2
            par = t % 2
            if t < NT - 1:
                def fkv(e):
                    ins = None
                    for h in range(4):
                        c = h // 2
                        rows = slice(0, 64) if h % 2 == 0 else slice(64, 128)
                        ins = e.matmul(bank(0)[rows, c * 128:(c + 1) * 128], kinT[par][:, h * 64:(h + 1) * 64],
                                       gv[:, t, h * 128:(h + 1) * 128], start=True, stop=True)
                    return ins
                pg.add("pe", fkv, reads=[("A:kinT", par), ("A:gv", t)], excl=[("ps", 0)])
                for c in range(2):
                    cs = slice(c * 128, (c + 1) * 128)
                    stt(S32[:, cs], bank(0, 128, c * 128), dec_col(t, c), Sd[:, cs], ALU.mult, ALU.add,
                        reads=[("A:Sd",), ("dec", p)], writes=[("A:S32",)], excl=[("ps", 0)])
                act(S_bf, S32.rearrange("p (c t) -> p c t", c=2), AF.Copy, reads=[("A:S32",)], writes=[("A:Sbf",)])
                if t + 1 < NT - 1:
                    for c in range(2):
                        cs = slice(c * 128, (c + 1) * 128)
                        ts("dve", Sd[:, cs], S32[:, cs], dec_col(t + 1, c), None, ALU.mult, None,
                           reads=[("A:S32",), ("dec", (t + 1) // 2)], writes=[("A:Sd",)])

        def post(t):
            tok = slice(t * 128, (t + 1) * 128)
            g = t // 4
            par = t % 2
            ob = obanks(t)
            ssk, rsk = "ss4_p%d" % par, "rstd4_p%d" % par
            act(ss4_p[par], ss4_p[par], AF.Ln, reads=[(ssk, q) for q in range(4)] + ["eps_col"], writes=[ssk],
                scale=1.0 / 128, bias=eps_col)
            act(rstd4_p[par], ss4_p[par], AF.Exp, reads=[ssk], writes=[rsk], scale=-0.5)
            for q in range(2):
                tt("dve", on[par][:, q * 2:(q + 1) * 2, :], bank(ob[q], 256).rearrange("p (c t) -> p c t", c=2),
                   rstd4_p[par][:, q * 2:(q + 1) * 2].unsqueeze(2).to_broadcast([128, 2, 128]), ALU.mult,
                   reads=[rsk] + bkey, writes=[("A:on", par, q)], excl=[("ps", ob[q])])
            tp = bank_bf(1)[:, 256:768]
            def ftp(e):
                ins = None
                for q in range(4):
                    h = QORD[q]
                    ins = e.transpose(tp[:, h * 128:(h + 1) * 128], on[par][:, q, :], csb["ident_bf"][:, :])
                return ins
            pg.add("pe", ftp, reads=[("A:on", par, 0), ("A:on", par, 1), "ident_bf"], excl=[("ps", 1)])
            tt("dve", mixT[:, 0:4, tok], tp.rearrange("p (h t) -> p h t", h=4), mixT[:, 0:4, tok], ALU.mult,
               reads=[("mix", hc, g) for hc in range(4)], writes=[("mix", hc, g) for hc in range(4)], excl=[("ps", 1)])

        pre(0)
        for t in range(NT):
            main_o(t)
            main_kv(t)
            if t + 1 < NT:
                pre(t + 1)
            post(t)

    dv_aug = arena[:, 0:NT * 520].rearrange("p (t h n) -> p t h n", h=4, n=130)
    do = NT * 520
    dqT = [arena[:, do + i * S:do + (i + 1) * S] for i in range(2)]
    dkT = [arena[:, do + (2 + i) * S:do + (3 + i) * S] for i in range(2)]
    po = do + 4 * S
    PT = [[arena[:, po + (2 * i + m) * 512:po + (2 * i + m + 1) * 512] for m in range(2)] for i in range(2)]
    fo2 = (po + 2048) // 2
    Oc = arena_f[:, fo2:fo2 + 8 * 129].rearrange("p (i n) -> p i n", n=129)
    Dm = arena_f[:, fo2 + 1032:fo2 + 1544].rearrange("p (s n) -> p s n", s=4)
    t2 = arena_f[:, fo2 + 1544:fo2 + 2056].rearrange("p (s n) -> p s n", s=4)
    dno = 2 * (fo2 + 2056)
    Dn = arena[:, dno:dno + 512].rearrange("p (s n) -> p s n", s=4)
    assert dno + 512 <= 24576
    rz = smcol("rz", 8)
    SBANKS = [(2, 3), (7, 1)]
    astep = [0]

    def obank(i):
        return 4 + i // 3, (i % 3) * 129

    def attention(b, h, buf, hook=None, fin_q=None, flush=True):
        qsub = 256 if SLOPES[h] * 511 > 40 else 512
        steps = [(G, kb) for G in range(NG) for kb in range(4 * G + 4)]
        started = {}
        pend = []

        def do_qk_exp(G, kb):
            t = kb - 4 * G
            s0 = max(t, 0)
            c0 = 128 * s0
            ks = slice(kb * 128, (kb + 1) * 128)
            qs = slice(G * 512 + c0, (G + 1) * 512)
            pb = astep[0] % 2
            astep[0] += 1
            sb0, sb1 = SBANKS[pb]
            def fqk(e):
                e.matmul(bank(sb0)[:, c0:512], dkT[buf][0:64, ks], dqT[buf][0:64, qs], start=True, stop=True)
                return e.matmul(bank(sb1)[:, c0:512], dkT[buf][64:128, ks], dqT[buf][64:128, qs], start=True, stop=True)
            pg.add("pe", fqk, reads=[("A:dk", buf, kb // 4), ("A:dq", buf, G)], excl=[("ps", sb0), ("ps", sb1)])
            for m in range(2):
                sbm = (sb0, sb1)[m]
                if qsub == 512:
                    chunks = [(c0, 512)]
                else:
                    chunks = [(max(c0, lo), lo + qsub) for lo in range(0, 512, qsub) if c0 < lo + qsub]
                for (a, bnd) in chunks:
                    delta = 4 * G + (bnd - 1) // 128 - kb
                    act(PT[pb][m][:, a:bnd], bank(sbm)[:, a:bnd], AF.Exp, reads=["bias_tab"], writes=[("A:PT", pb, m)],
                        excl=[("ps", sbm)], bias=csb["bias_tab"][:, h * 16 + delta:h * 16 + delta + 1], scale=0.125)
                if t >= 0:
                    tt("dve", PT[pb][m][:, c0:c0 + 128], PT[pb][m][:, c0:c0 + 128], csb["maskT_bf"][:, :], ALU.mult,
                       reads=[("A:PT", pb, m), "maskT_bf"], writes=[("A:PT", pb, m)])
            return pb, s0

        def do_pv(G, kb, pb, s0):
            st_set = started.setdefault(G, set())
            plan = []
            for m in range(2):
                for s in range(s0, 4):
                    bk, off = obank(4 * m + s)
                    plan.append((bk, off, m, s, bk not in st_set, kb == 4 * G + s))
                    st_set.add(bk)
            def fpv(e):
                ins = None
                for (bk, off, m, s, st, sp) in plan:
                    ins = e.matmul(bank(bk)[:, off:off + 129], PT[pb][m][:, s * 128:(s + 1) * 128], dv_aug[:, kb, h, 0:129],
                                   start=st, stop=sp, skip_group_check=True)
                return ins
            pg.add("pe", fpv, reads=[("A:PT", pb, 0), ("A:PT", pb, 1), ("A:dv", kb)], excl=[("ps", 4), ("ps", 5), ("ps", 6)])
            if kb == 4 * G + 3:
                finalize(G)

        if fin_q is None:
            fin_q = []

        def finalize(G):
            while any(f_ is not None and getattr(f_, "is_f2", False) for f_ in fin_q):
                f_ = fin_q.pop(0)
                if f_ is not None:
                    f_()
            copy("dve", Oc[:, 0:3, :], bank(4, 387).rearrange("p (i n) -> p i n", n=129), reads=[], writes=[("A:Oc", 0)], excl=[("ps", 4)])
            copy("dve", Oc[:, 3:6, :], bank(5, 387).rearrange("p (i n) -> p i n", n=129), reads=[], writes=[("A:Oc", 1)], excl=[("ps", 5)])
            copy("dve", Oc[:, 6:8, :], bank(6, 258).rearrange("p (i n) -> p i n", n=129), reads=[], writes=[("A:Oc", 2)], excl=[("ps", 6)])
            ock = [("A:Oc", i) for i in range(3)]

            def F2():
                pg.add("dve", lambda e: e.reciprocal(out=rz.unsqueeze(2), in_=Oc[:, :, 128:129]), reads=ock, writes=["rz"])
                ts("dve", rz[:, 4:8], rz[:, 4:8], neg_lam, None, ALU.mult, None, reads=["rz", "neg_lam"], writes=["rz"])
                tt("dve", t2, Oc[:, 4:8, 0:128], rz[:, 4:8].unsqueeze(2).to_broadcast([128, 4, 128]), ALU.mult,
                   reads=ock + ["rz"], writes=[("A:t2",)])
                tt("dve", Dm, Oc[:, 0:4, 0:128], rz[:, 0:4].unsqueeze(2).to_broadcast([128, 4, 128]), ALU.mult,
                   reads=ock + ["rz"], writes=[("A:Dm",)])
                tt("pool", Dm, Dm, t2, ALU.add, reads=[("A:Dm",), ("A:t2",)], writes=[("A:Dm",)])

            def F3():
                tt("dve", t2, Dm, Dm, ALU.mult, reads=[("A:Dm",)], writes=[("A:t2",)])
                pg.add("dve", lambda e: e.tensor_reduce(out=ss4, in_=t2, axis=AX.X, op=ALU.add), reads=[("A:t2",)], writes=["ss4"])

            def F4():
                rstd_from_ss(ss4, rstd4, 128, "ss4", "rstd4")

            def F5():
                tt("dve", Dn, Dm, rstd4.unsqueeze(2).to_broadcast([128, 4, 128]), ALU.mult, reads=[("A:Dm",), "rstd4"], writes=[("A:Dn",)])
                tbk = SBANKS[astep[0] % 2][0]
                tp = bank_bf(tbk)[:, 0:512]
                def ftp(e):
                    ins = None
                    for s_ in range(4):
                        ins = e.transpose(tp[:, s_ * 128:(s_ + 1) * 128], Dn[:, s_, :], csb["ident_bf"][:, :])
                    return ins
                pg.add("pe", ftp, reads=[("A:Dn",), "ident_bf"], excl=[("ps", tbk)])

                dst = mixT[:, 4 + h, G * 512:(G + 1) * 512]
                tt("dve", dst, tp, dst, ALU.mult, reads=[("mix", 4 + h, G)], writes=[("mix", 4 + h, G)], excl=[("ps", tbk)])

            F2.is_f2 = True
            fin_q.extend([F2, None, F3, None, F4, None, None, F5])

        for (G, kb) in steps:
            pb, s0 = do_qk_exp(G, kb)
            if pend:
                do_pv(*pend.pop(0))
            pend.append((G, kb, pb, s0))
            if fin_q:
                f_ = fin_q.pop(0)
                if f_ is not None:
                    f_()
            if hook is not None:
                hook()
        while pend:
            do_pv(*pend.pop(0))
        while flush and fin_q:
            f_ = fin_q.pop(0)
            if f_ is not None:
                f_()

    def head_inproj_thunks(h, pieces=False):
        buf = h % 2
        mk = inproj_fm_pieces if pieces else inproj_fm_thunks
        th = []
        th += mk(C_DZ + h * 128, 128, lambda g, p, bk: copy(
            "dve", mixT[:, 4 + h, g * 512:(g + 1) * 512], p, reads=[], writes=[("mix", 4 + h, g)], excl=[("ps", bk)]))
        th += mk(C_DQ + h * 128, 128, lambda g, p, bk: copy(
            "dve", dqT[buf][:, g * 512:(g + 1) * 512], p, reads=[], writes=[("A:dq", buf, g)], excl=[("ps", bk)]))
        th += mk(C_DK + h * 128, 128, lambda g, p, bk: copy(
            "dve", dkT[buf][:, g * 512:(g + 1) * 512], p, reads=[], writes=[("A:dk", buf, g)], excl=[("ps", bk)]))
        return th

    def head_gate(h):
        mk = [("mix", 4 + h, g) for g in range(NG)]
        dst = mixT[:, 4 + h, :]
        act(dst, dst, AF.Silu, reads=mk, writes=mk)
        ts("dve", dst, dst, dgain_sb[:, h:h + 1], 1.0 - LAM_INIT, ALU.mult, ALU.mult, reads=mk + ["dgain"], writes=mk)

    def run_diff(b):
        evpref[0] = None
        phase_barrier()
        ipbanks[0] = 2
        pg.add("pool", lambda e: e.memset(dv_aug[:, :, :, 128:129], 1.0), reads=[], writes=[("A:dvones",)])
        dv_th = inproj_tm_thunks(C_DV, lambda t, p, bk: copy("dve", dv_aug[:, t, :, 0:128], p.rearrange("p (h n) -> p h n", h=4),
                                                             reads=[("A:dvones",)], writes=[("A:dv", t)], excl=[("ps", bk)]))
        for f in head_inproj_thunks(0):
            f()
        n_pre = min(4, NT)
        for f in dv_th[:n_pre]:
            f()
        dv_late = dv_th[n_pre:]
        ipbanks[0] = 1
        fq = []
        for h in range(4):
            buf = h % 2
            head_gate(h)
            nxt = head_inproj_thunks(h + 1, pieces=True) if h < 3 else []
            if h == 0 and dv_late:
                merged = []
                while nxt or dv_late:
                    if dv_late:
                        merged.append((1, dv_late.pop(0)))
                    for _ in range(4):
                        if nxt:
                            merged.append((0, nxt.pop(0)))
                nxt = merged
            elif h == 3:
                nxt = [(0, f) for f in gate_thunks(b)]
            else:
                nxt = [(0, f) for f in nxt]
            def hook():
                budget = 2
                while nxt and budget > 0:
                    kind, f = nxt.pop(0)
                    f()
                    if kind == 0:
                        budget -= 1
            attention(b, h, buf, hook, fin_q=fq, flush=(h == 3))
            while nxt:
                nxt.pop(0)[1]()
            if h == 2:
                load_w_out()

    yslot = [0]

    wo_stage = Rbuf[:, 8 * D:16 * D].bitcast(F32).rearrange("p (k n) -> p k n", k=4)

    def load_w_out():
        hkeys = [("hT", g, e_) for g in range(NG) for e_ in ("d", "a")]
        for half in range(2):
            dma("sp", wo_stage, w_out_d[half * 512:(half + 1) * 512, :].rearrange("(k p) n -> p k n", p=128), ("wost", 0),
                writes=hkeys + ["wost"])
            copy("dve", w_out_sb[:, half * 4:(half + 1) * 4, :], wo_stage, reads=["wost"], writes=hkeys + [("wout", half)])

    def gate_thunks(b):
        def mk(kc):
            def f():
                ts("dve", diagg[:, :], csb["ident_f"][:, :], gcol[:, kc * 2 + b:kc * 2 + b + 1], None, ALU.mult, None,
                   reads=["ident_f", "gcol"], writes=["diagg"])
                mm_group(bank(0, 128, (kc % 4) * 128), [(csb["ones_f"][:, :], diagg[:, :])], reads=["ones_f", "diagg"], excl=[("ps", 0)])
                if kc % 4 == 3:
                    copy("dve", gate_bc[:, (kc // 4) * 512:(kc // 4 + 1) * 512], bank(0), reads=[], writes=["gate_bc"], excl=[("ps", 0)])
            return f
        return [mk(kc) for kc in range(8)]

    def run_out(b):
        hkeys = [("hT", g, e_) for g in range(NG) for e_ in ("d", "a")]
        wkeys = [("wout", 0), ("wout", 1)] + hkeys

        ybl = [ybuf[0], ybuf[1], ybuf3]
        ykl = [[("y", 0, 0), ("y", 0, 1)], [("y", 1, 0), ("y", 1, 1)], [("xn", 0), ("xn", 1)]]

        def out_stage1(t, s):
            tok = slice(t * 128, (t + 1) * 128)
            par = t % 2
            yb = ybl[t % 3]
            yk = ykl[t % 3]
            for half in range(2):
                bk = 2 * par + half
                pairs = [(mixT[:, kc, tok], w_out_sb[:, kc, half * 512:(half + 1) * 512]) for kc in range(8)]
                mm_group(bank(bk), pairs, reads=[("mix", kc, t // 4) for kc in range(8)] + wkeys, excl=[("ps", bk)])
                tt("dve", yb[:, half * 512:(half + 1) * 512], bank(bk), gate_bc[:, half * 512:(half + 1) * 512], ALU.mult,
                   reads=["gate_bc"], writes=[yk[half]], excl=[("ps", bk)])
            tt("pool", yb[:, :], yb[:, :], xt[s][:, :], ALU.add, reads=yk + [("xt", s)], writes=yk)

        def out_stage1c(t):
            i3 = t % 3
            yb = ybl[i3]
            yk = ykl[i3]
            ssk, rsk = "ss_p%d" % i3, "rstd_p%d" % i3
            jb = 4
            act(ps[:, jb * 512:jb * 512 + 1024], yb[:, :], AF.Square, reads=yk, writes=[ssk],
                excl=[("ps", jb), ("ps", jb + 1)], accum=ss_p[i3])
            rstd_from_ss(ss_p[i3], rstd_p[i3], D, ssk, rsk)

        def out_stage2(t):
            tok0 = b * S + t * 128
            i3 = t % 3
            yb = ybl[i3]
            yk = ykl[i3]
            stt(yb[:, :], yb[:, :], rstd_p[i3], fgain_sb[:, :], ALU.mult, ALU.mult, reads=yk + ["rstd_p%d" % i3, "fgain"], writes=yk)
            dma("act", y_d[tok0:tok0 + 128, :], yb[:, :], ("yout", i3), reads=yk)

        slots = {0: load_x(b * S)}
        for t in range(NT + 1):
            if t < NT:
                if t + 1 < NT:
                    slots[t + 1] = load_x(b * S + (t + 1) * 128)
                out_stage1(t, slots.pop(t))
            if t >= 1:
                out_stage2(t - 1)
            if t < NT:
                out_stage1c(t)

    for b in range(NSEQ):
        if upto == "mod":
            break
        DVE_KC = [0, 2, 3, 4, 6, 7]
        ACT_KC = [1, 5]

        def ht_stage1a(t, s):
            par = t % 2
            ssk, rsk = "ss_p%d" % par, "rstd_p%d" % par
            act(junk[:, :], xt[s][:, :], AF.Square, reads=[("xt", s)], writes=[("y", 1, 0), ("y", 1, 1), ssk], accum=ss_p[par])
            rstd_from_ss(ss_p[par], rstd_p[par], D, ssk, rsk)

        def ht_stage1b(t, s):
            par = t % 2
            xn = xnb[par][:, :]
            xk = ("xn", par)
            rsk = "rstd_p%d" % par
            ts("dve", xn, xt[s][:, :], rstd_p[par], None, ALU.mult, None, reads=[("xt", s), rsk], writes=[xk])
            bD, bA = 1 + 2 * par, 2 + 2 * par
            def ftr(e):
                ins = None
                for i, kc in enumerate(DVE_KC):
                    ins = e.transpose(bank_bf(bD)[:, i * 128:(i + 1) * 128], xn[:, kc * 128:(kc + 1) * 128], csb["ident_bf"][:, :])
                for i, kc in enumerate(ACT_KC):
                    ins = e.transpose(bank_bf(bA)[:, i * 128:(i + 1) * 128], xn[:, kc * 128:(kc + 1) * 128], csb["ident_bf"][:, :])
                return ins
            pg.add("pe", ftr, reads=[xk, "ident_bf"], excl=[("ps", bD), ("ps", bA)])

        def ht_stage2(t):
            par = t % 2
            bD, bA = 1 + 2 * par, 2 + 2 * par
            for i, kc in enumerate(DVE_KC):
                ts("dve", hT[:, kc, t * 128:(t + 1) * 128], bank_bf(bD)[:, i * 128:(i + 1) * 128],
                   acol[:, kc * 2 + b:kc * 2 + b + 1], scol[:, kc * 2 + b:kc * 2 + b + 1], ALU.mult, ALU.add,
                   reads=["acol", "scol"], writes=[("hT", t // 4, "d")], excl=[("ps", bD)])
            for i, kc in enumerate(ACT_KC):
                act(hT[:, kc, t * 128:(t + 1) * 128], bank_bf(bA)[:, i * 128:(i + 1) * 128], AF.Identity,
                    reads=["acol", "scol"], writes=[("hT", t // 4, "a")], excl=[("ps", bA)],
                    bias=scol[:, kc * 2 + b:kc * 2 + b + 1], scale=acol[:, kc * 2 + b:kc * 2 + b + 1])

        per_g = gla_begin(b)
        ipq = []
        pend_ev = []
        banksets = [[6, 7], [0, 5]]
        it = [0]

        ev_done = [0]

        def ip_step(n):
            while pend_ev:
                ev, bk = pend_ev.pop(0)
                ev(bk)
                ev_done[0] += 1
            for j in range(n):
                if ipq:
                    mm, ev = ipq.pop(0)
                    bk = banksets[it[0] % 2][j]
                    mm(bk)
                    pend_ev.append((ev, bk))
            it[0] += 1

        evpref[0] = "act"
        slots = {0: load_x(b * S)}
        for t in range(NT + 1):
            if t < NT:
                if t + 1 < NT:
                    slots[t + 1] = load_x(b * S + (t + 1) * 128)
                ht_stage1a(t, slots[t])
            if t >= 1:
                ht_stage2(t - 1)
                if (t - 1) % 4 == 3:
                    ipq.extend(per_g[(t - 1) // 4])
            if t < NT:
                ht_stage1b(t, slots.pop(t))
            ip_step(2)
        evpref[0] = "dve"
        banksets[1] = [0, 1]
        n_per_group = len(per_g[0])

        def need_group(g):
            while ev_done[0] < n_per_group * (g + 1) and (ipq or pend_ev):
                ip_step(2)
        iptail = (ip_step, need_group, ipq, pend_ev)
        if b == 0:
            for kc in range(8):
                dump("hT%d" % kc, hT[:, kc, :], [("hT", g, e_) for g in range(4) for e_ in ("d", "a")])
        if upto == "hT":
            break
        run_gla(b, iptail)
        if b == 0:
            for hc in range(4):
                dump("mixg%d" % hc, mixT[:, hc, :], [("mix", hc, g) for g in range(NG)])
        if upto == "gla":
            break
        run_diff(b)
        if b == 0:
            for hc in range(4):
                dump("mixd%d" % hc, mixT[:, 4 + hc, :], [("mix", 4 + hc, g) for g in range(NG)])
        if upto == "diff":
            break
        run_out(b)

    pg.finalize(nc, stack)
    tail = [(pg.sems[(("dma", k), 0)], 16 * pg.dma_count[k]) for k in pg.dma_count if k[0] in ("dump", "yout")]
    last = Op("sp", lambda e: e.nop(), len(pg.ops), None)
    last.waits = tail
    last.signal = None
    pg.ops.append(last)
    pg.emit_all(nc)
    return nc, stack


def make_in_maps(x, c, w_ada, b_ada, norm_gain, w_in, w_gla_gate_up, b_gla_gate, gla_out_gain,
                 lambda_q1, lambda_k1, lambda_q2, lambda_k2, diff_out_gain, w_out, final_gain):
    f = lambda a: np.ascontiguousarray(np.asarray(a, dtype=np.float32))
    x = f(x); c = f(c)
    consts = _consts()
    shared = {
        "w_ada": f(w_ada[0]),
        "b_adaT": f(np.asarray(b_ada[0]).reshape(24, 128).T),
        "ngainT": f(np.asarray(norm_gain[0]).reshape(8, 128).T),
        "w_in": f(w_in[0]),
        "wup_aug": f(np.concatenate([np.asarray(w_gla_gate_up[0]), np.asarray(b_gla_gate[0])[None, :]], axis=0)),
        "ggainT": f(np.asarray(gla_out_gain[0]).reshape(4, 128).T),
        "dgainT": f(np.asarray(diff_out_gain[0]).reshape(4, 128).T),
        "lam_bc": f(np.broadcast_to(np.concatenate([np.asarray(lambda_q1[0]), np.asarray(lambda_k1[0]),
                                                    np.asarray(lambda_q2[0]), np.asarray(lambda_k2[0])])[None, :], (128, 256))),
        "w_out": f(w_out[0]),
        "fgain_bc": f(np.broadcast_to(np.asarray(final_gain)[None, :], (128, D))),
    }
    shared.update(consts)
    in_maps = []
    for i in range(NCORES):
        m = dict(shared)
        m["x"] = np.ascontiguousarray(x[2 * i:2 * i + 2].reshape(NSEQ * S, D))
        m["cT"] = np.ascontiguousarray(c[2 * i:2 * i + 2].reshape(2, 8, 128).transpose(2, 1, 0).reshape(128, 16))
        in_maps.append(m)
    return in_maps


def kernel(**inputs):
    in_maps = make_in_maps(**inputs)
    nc, stack = build_program()
    with stack:
        res = run_bass_kernel_spmd(nc, in_maps, core_ids=list(range(NCORES)))
    outs = [np.asarray(r["y"], dtype=np.float32).reshape(NSEQ, S, D) for r in res.results]
    return np.concatenate(outs, axis=0)
```

```python
import math
from contextlib import ExitStack

import numpy as np
import ml_dtypes

import concourse.bass as bass
import concourse.mybir as mybir
from concourse.bass_utils import run_bass_kernel_spmd

F32 = mybir.dt.float32
BF16 = mybir.dt.bfloat16
AF = mybir.ActivationFunctionType
ALU = mybir.AluOpType
AX = mybir.AxisListType

NCORES = 8
D = 1024
S = 2048
NSEQ = 2
NT = S // 128
D_IN = 3600
EPS = 1e-6
LAM_INIT = 0.8 - 0.6 * math.exp(-0.3 * 0)
SLOPES = [2.0 ** (-8.0 * (h + 1) / 4) for h in range(4)]
C_GQ, C_GK, C_GV, C_GZ, C_GR = 0, 256, 512, 1024, 1536
C_DQ, C_DK, C_DV, C_DZ = 1552, 2064, 2576, 3088

EPOCH = 12000


class Op:
    __slots__ = ("eng", "emit", "idx", "deps", "dma_key", "signal", "cnt", "waits")

    def __init__(self, eng, emit, idx, dma_key):
        self.eng = eng
        self.emit = emit
        self.idx = idx
        self.dma_key = dma_key
        self.deps = {}
        self.signal = False
        self.cnt = 0
        self.waits = []


class Prog:
    ENGS = ("pe", "act", "dve", "pool", "sp")

    def __init__(self):
        self.ops = []
        self.last_w = {}
        self.readers = {}
        self.xacc = {}
        self.dma_count = {}
        self.bulk = set()

    def _chan(self, op):
        return ("dma", op.dma_key) if op.dma_key is not None else op.eng

    def add(self, eng, emit, reads=(), writes=(), excl=(), dma_key=None, bulk=False):
        op = Op(eng, emit, len(self.ops), dma_key)
        self.ops.append(op)
        if dma_key is not None:
            self.dma_count[dma_key] = self.dma_count.get(dma_key, 0) + 1
            op.cnt = self.dma_count[dma_key]
            if bulk:
                self.bulk.add(dma_key)
        me = self._chan(op)
        deps = {}
        if any(isinstance(k, tuple) and isinstance(k[0], str) and k[0].startswith("A:") for k in list(reads) + list(writes)):
            reads = list(reads) + ["AR"]

        def dep(i):
            if i is None:
                return
            p = self.ops[i]
            c = self._chan(p)
            if c == "pe" and me == "pe":
                return
            if c == me and op.dma_key is not None and op.dma_key in self.bulk:
                return
            if c not in deps or deps[c] < i:
                deps[c] = i

        for k in reads:
            dep(self.last_w.get(k))
        for k in writes:
            dep(self.last_w.get(k))
            for c, i in self.readers.get(k, {}).items():
                dep(i)
        for k in excl:
            for c, i in self.xacc.get(k, {}).items():
                if c != me:
                    dep(i)
        for k in reads:
            self.readers.setdefault(k, {})[me] = op.idx
        for k in writes:
            self.last_w[k] = op.idx
            self.readers[k] = {}
        for k in excl:
            self.xacc[k] = {me: op.idx}
        op.deps = deps
        for c, i in deps.items():
            self.ops[i].signal = True
        return op

    def finalize(self, nc, stack):
        counters = {}
        for op in self.ops:
            if op.dma_key is None and op.signal:
                counters[op.eng] = counters.get(op.eng, 0) + 1
                op.cnt = counters[op.eng]
        self.sems = {}

        def sem_for(name):
            if name not in self.sems:
                self.sems[name] = stack.enter_context(nc.semaphore("s%d" % len(self.sems)))
            return self.sems[name]

        def target(p):
            if p.dma_key is not None:
                n = self.dma_count[p.dma_key] if p.dma_key in self.bulk else p.cnt
                return ("dma", p.dma_key), 0, 16 * n
            e = (p.cnt - 1) // EPOCH
            return p.eng, e, (p.cnt - 1) % EPOCH + 1

        waited = {e: {} for e in self.ENGS}
        for op in self.ops:
            w = waited[op.eng]
            for c, i in op.deps.items():
                chan, ep, val = target(self.ops[i])
                if w.get(chan, (-1, 0)) >= (ep, val):
                    continue
                w[chan] = (ep, val)
                op.waits.append((sem_for((chan, ep)), val))
        for op in self.ops:
            if op.dma_key is not None:
                op.signal = (sem_for((("dma", op.dma_key), 0)), 16)
            elif op.signal:
                _, ep, _ = target(op)
                op.signal = (sem_for((op.eng, ep)), 1)
            else:
                op.signal = None

    def emit_all(self, nc):
        by_eng = {e: [o for o in self.ops if o.eng == e] for e in self.ENGS}

        def run(engine, ops):
            for op in ops:
                for sem, val in op.waits:
                    engine.wait_ge(sem, val)
                ins = op.emit(engine)
                if op.signal is not None:
                    ins.then_inc(op.signal[0], op.signal[1])

        with nc.Block() as block:
            @block.sync
            def _(e):
                run(e, by_eng["sp"])

            @block.tensor
            def _(e):
                run(e, by_eng["pe"])

            @block.scalar
            def _(e):
                run(e, by_eng["act"])

            @block.vector
            def _(e):
                run(e, by_eng["dve"])

            @block.gpsimd
            def _(e):
                run(e, by_eng["pool"])


def _consts():
    j = np.arange(128)
    c = {}
    c["ident_bf"] = np.eye(128, dtype=np.float32).astype(ml_dtypes.bfloat16)
    c["ident_f"] = np.eye(128, dtype=np.float32)
    c["ones_f"] = np.ones((128, 128), np.float32)
    c["maskT_bf"] = (j[None, :] >= j[:, None]).astype(np.float32).astype(ml_dtypes.bfloat16)
    c["triI_f"] = np.where(j[:, None] <= j[None, :], -1.0 / 16.0, 0.0).astype(np.float32)
    c["triU_f"] = np.where(j[:, None] > j[None, :], -1.0 / 16.0, 0.0).astype(np.float32)
    bt = np.zeros((128, 4, 16), np.float32)
    for h in range(4):
        for d in range(16):
            bt[:, h, d] = SLOPES[h] * (j - 127 - 128 * d)
    c["bias_tab"] = bt.reshape(128, 64)
    return c


CONST_SPECS = [
    ("ident_bf", [128, 128], BF16), ("ident_f", [128, 128], F32), ("ones_f", [128, 128], F32),
    ("maskT_bf", [128, 128], BF16), ("triI_f", [128, 128], F32), ("triU_f", [128, 128], F32),
    ("bias_tab", [128, 64], F32),
]


def build_program(upto="all", dumps=()):
    nc = bass.Bass("TRN2", target_bir_lowering=False)
    pg = Prog()
    stack = ExitStack()
    dram = {}

    def din(name, shape, dt=F32):
        dram[name] = nc.dram_tensor(name, list(shape), dt, kind="ExternalInput").ap()
        return dram[name]

    x_d = din("x", [NSEQ * S, D])
    cT_d = din("cT", [128, 16])
    w_ada_d = din("w_ada", [D, 3 * D])
    b_adaT_d = din("b_adaT", [128, 24])
    ngain_d = din("ngainT", [128, 8])
    w_in_d = din("w_in", [D, D_IN])
    wup_d = din("wup_aug", [17, 256])
    ggain_d = din("ggainT", [128, 4])
    dgain_d = din("dgainT", [128, 4])
    lam_d = din("lam_bc", [128, 256])
    w_out_d = din("w_out", [D, D])
    fgain_d = din("fgain_bc", [128, D])
    cd = {n: din(n, shp, dt) for n, shp, dt in CONST_SPECS}
    y_d = nc.dram_tensor("y", [NSEQ * S, D], F32, kind="ExternalOutput").ap()
    dump_d = {}
    for name, shape in dumps:
        dump_d[name] = nc.dram_tensor("dbg_" + name, list(shape), F32, kind="ExternalOutput").ap()

    def sb(name, shape, dt):
        return stack.enter_context(nc.sbuf_tensor("sb_" + name, list(shape), dt))

    stack.enter_context(nc.allow_low_precision("bf16 matmul operands, fp32 accumulation"))

    w_in_sb = sb("w_in_sb", [128, 8, D_IN], BF16)
    Rbuf = sb("Rbuf", [128, max(8 * S, 16 * D)], BF16)
    hT = Rbuf[:, 0:8 * S].rearrange("p (k t) -> p k t", k=8)
    w_out_sb = Rbuf[:, 0:8 * D].rearrange("p (k n) -> p k n", k=8)
    mixbuf = sb("mixbuf", [128, max(8 * S, 16384)], BF16)
    mixT = mixbuf[:, 0:8 * S].rearrange("p (k t) -> p k t", k=8)
    NXT = 2
    xt = [sb("xt%d" % i, [128, D], F32) for i in range(NXT)]
    xnb_all = sb("xnb_all", [128, 2 * D], BF16)
    xnb = [xnb_all[:, 0:D], xnb_all[:, D:2 * D]]
    ybuf3 = xnb_all[:, :].bitcast(F32)
    ybuf = [sb("ybuf%d" % i, [128, D], F32) for i in range(2)]
    junk = ybuf[1]
    gate_bc = sb("gate_bc", [128, D], F32)
    fgain_sb = sb("fgain_sb", [128, D], F32)
    csb = {n: sb(n, shp, dt) for n, shp, dt in CONST_SPECS}
    cT_sb = sb("cT_sb", [128, 16], F32)
    b_adaT_sb = sb("b_adaT_sb", [128, 24], F32)
    ngain_sb = sb("ngain_sb", [128, 8], F32)
    ggain_sb = sb("ggain_sb", [128, 4], F32)
    dgain_sb = sb("dgain_sb", [128, 4], F32)
    lam_sb = sb("lam_sb", [128, 256], F32)
    wup_f = sb("wup_f", [32, 256], F32)
    wup_bf = sb("wup_bf", [32, 256], BF16)
    small = sb("small", [128, 256], F32)
    modT = sb("modT", [128, 48], F32)
    acol = sb("acol", [128, 16], F32)
    scol = sb("scol", [128, 16], F32)
    gcol = sb("gcol", [128, 16], F32)
    diagg = sb("diagg", [128, 128], F32)
    arena = sb("arena", [128, 24576], BF16)
    ps = stack.enter_context(nc.psum_tensor("ps", [128, 4096], F32))

    def bank(b, n=512, off=0):
        return ps[:, b * 512 + off: b * 512 + off + n]

    def bank_bf(b):
        return ps[:, b * 512:(b + 1) * 512].bitcast(BF16)

    SM = {}
    _sm_next = [0]

    def smcol(name, n=1):
        SM[name] = small[:, _sm_next[0]:_sm_next[0] + n]
        _sm_next[0] += n
        return SM[name]

    eps_col = smcol("eps_col", 1)

    def dma(queue, out, in_, key, reads=(), writes=(), bulk=False, **kw):
        pg.add(queue, lambda e: e.dma_start(out=out, in_=in_, **kw), reads=reads, writes=writes,
               dma_key=key, bulk=bulk)

    def dump(name, src_ap, reads):
        if name in dump_d:
            dma("pool", dump_d[name], src_ap, ("dump", name), reads=reads, max_dma_last_dim=4096)

    def act(out, in_, func, reads, writes, excl=(), bias=0.0, scale=1.0, accum=None):
        def f(e):
            kw = {}
            if accum is not None:
                kw["accum_out"] = accum
            return e.activation(out=out, in_=in_, func=func, bias=bias, scale=scale, **kw)
        pg.add("act", f, reads=reads, writes=writes, excl=excl)

    def tt(eng, out, in0, in1, op, reads, writes, excl=()):
        pg.add(eng, lambda e: e.tensor_tensor(out=out, in0=in0, in1=in1, op=op),
               reads=reads, writes=writes, excl=excl)

    def ts(eng, out, in0, s1, s2, op0, op1, reads, writes, excl=()):
        if s2 is None:
            pg.add(eng, lambda e: e.tensor_scalar(out=out, in0=in0, scalar1=s1, scalar2=None, op0=op0),
                   reads=reads, writes=writes, excl=excl)
        else:
            pg.add(eng, lambda e: e.tensor_scalar(out=out, in0=in0, scalar1=s1, scalar2=s2, op0=op0, op1=op1),
                   reads=reads, writes=writes, excl=excl)

    def stt(out, in0, scalar, in1, op0, op1, reads, writes, excl=()):
        pg.add("dve", lambda e: e.scalar_tensor_tensor(out=out, in0=in0, scalar=scalar, in1=in1, op0=op0, op1=op1),
               reads=reads, writes=writes, excl=excl)

    def copy(eng, out, in_, reads, writes, excl=()):
        pg.add(eng, lambda e: e.tensor_copy(out=out, in_=in_), reads=reads, writes=writes, excl=excl)

    def mm_group(out, pairs, reads, excl, writes=(), first_start=True, last_stop=True):
        def f(e):
            ins = None
            n = len(pairs)
            for i, (l, r) in enumerate(pairs):
                ins = e.matmul(out, l, r, start=(first_start and i == 0), stop=(last_stop and i == n - 1),
                               skip_group_check=not (first_start and last_stop))
            return ins
        pg.add("pe", f, reads=reads, writes=writes, excl=excl)

    def rstd_from_ss(ss, rstd, n, key_ss, key_rstd):
        act(ss, ss, AF.Ln, reads=[key_ss, "eps_col"], writes=[key_ss], scale=1.0 / n, bias=eps_col)
        act(rstd, ss, AF.Exp, reads=[key_ss], writes=[key_rstd], scale=-0.5)

    pg.add("pool", lambda e: e.memset(eps_col, EPS), reads=[], writes=["eps_col"])
    for n, shp, dt in CONST_SPECS:
        dma("sp", csb[n][:, :], cd[n], "const", writes=[n], bulk=True)
    dma("sp", cT_sb[:, :], cT_d, "const", writes=["cT"], bulk=True)
    dma("sp", b_adaT_sb[:, :], b_adaT_d, "const", writes=["b_adaT"], bulk=True)
    dma("sp", ngain_sb[:, :], ngain_d, "const", writes=["ngain"], bulk=True)
    dma("sp", ggain_sb[:, :], ggain_d, "const", writes=["ggain"], bulk=True)
    dma("sp", dgain_sb[:, :], dgain_d, "const", writes=["dgain"], bulk=True)
    dma("sp", lam_sb[:, :], lam_d, "const", writes=["lam_in"], bulk=True)
    dma("sp", wup_f[0:17, :], wup_d, "const", writes=["wup_f"], bulk=True)
    dma("sp", fgain_sb[:, :], fgain_d, "const", writes=["fgain"], bulk=True)
    copy("dve", wup_bf[0:17, :], wup_f[0:17, :], reads=["wup_f"], writes=["wup_bf"])

    sc = smcol("sc", 16)
    tmp16 = smcol("tmp16", 16)
    act(tmp16, cT_sb[:, :], AF.Exp, reads=["cT"], writes=["tmp16"], scale=-1.0)
    ts("dve", tmp16, tmp16, 1.0, None, ALU.add, None, reads=["tmp16"], writes=["tmp16"])
    pg.add("dve", lambda e: e.reciprocal(out=tmp16, in_=tmp16), reads=["tmp16"], writes=["tmp16"])
    tt("dve", sc, cT_sb[:, :], tmp16, ALU.mult, reads=["cT", "tmp16"], writes=["sc"])

    sc_bf = sb("sc_bf", [128, 16], BF16)
    copy("dve", sc_bf[:, :], sc, reads=["sc"], writes=["sc_bf"])
    slab = [mixbuf[:, 0:8192].bitcast(F32).rearrange("p (k n) -> p k n", k=8),
            mixbuf[:, 8192:16384].bitcast(F32).rearrange("p (k n) -> p k n", k=8)]
    slab_bf = [Rbuf[:, 0:4096].rearrange("p (k n) -> p k n", k=8), Rbuf[:, 4096:8192].rearrange("p (k n) -> p k n", k=8)]
    w_ada_v = w_ada_d.rearrange("(k p) n -> p k n", p=128)
    for sl in range(6):
        i2 = sl % 2
        dma("sp", slab[i2], w_ada_v[:, :, sl * 512:(sl + 1) * 512], ("slab", i2), writes=[("slab", i2)])
        copy("dve", slab_bf[i2], slab[i2], reads=[("slab", i2)], writes=[("slabbf", i2)])
        for jj in range(4):
            j = sl * 4 + jj
            pairs = [(slab_bf[i2][:, kc, jj * 128:(jj + 1) * 128], sc_bf[:, kc * 2:(kc + 1) * 2]) for kc in range(8)]
            mm_group(bank(0, 2, j * 2), pairs, reads=[("slabbf", i2), "sc_bf"], excl=[("ps", 0)])
    tt("dve", modT[:, :].rearrange("p (j b) -> p j b", b=2), bank(0, 48).rearrange("p (j b) -> p j b", b=2),
       b_adaT_sb[:, :].unsqueeze(2).to_broadcast([128, 24, 2]), ALU.add,
       reads=["b_adaT"], writes=["modT", ("slab", 0), ("slab", 1), ("slabbf", 0), ("slabbf", 1)], excl=[("ps", 0)])
    ts("dve", acol[:, :], modT[:, 16:32], 1.0, None, ALU.add, None, reads=["modT"], writes=["acol"])
    tt("dve", acol[:, :].rearrange("p (k b) -> p k b", b=2), acol[:, :].rearrange("p (k b) -> p k b", b=2),
       ngain_sb[:, :].unsqueeze(2).to_broadcast([128, 8, 2]), ALU.mult, reads=["acol", "ngain"], writes=["acol"])
    copy("dve", scol[:, :], modT[:, 0:16], reads=["modT"], writes=["scol"])
    copy("dve", gcol[:, :], modT[:, 32:48], reads=["modT"], writes=["gcol"])
    dump("acol", acol[:, :], ["acol"])
    dump("scol", scol[:, :], ["scol"])
    dump("gcol", gcol[:, :], ["gcol"])

    lam_t = smcol("lam_t", 2)
    neg_lam = smcol("neg_lam", 1)
    lamprod = smcol("lamprod", 128)
    tt("dve", lamprod.rearrange("p (a d) -> p a d", a=2), lam_sb[:, :].rearrange("p (a t d) -> p a t d", a=2, t=2)[:, :, 0, :],
       lam_sb[:, :].rearrange("p (a t d) -> p a t d", a=2, t=2)[:, :, 1, :], ALU.mult,
       reads=["lam_in"], writes=["lamprod"])
    pg.add("dve", lambda e: e.tensor_reduce(out=lam_t, in_=lamprod.rearrange("p (a d) -> p a d", a=2), axis=AX.X, op=ALU.add),
           reads=["lamprod"], writes=["lam_t"])
    act(lam_t, lam_t, AF.Exp, reads=["lam_t"], writes=["lam_t"])
    stt(neg_lam, lam_t[:, 1:2], -LAM_INIT, lam_t[:, 0:1], ALU.add, ALU.subtract, reads=["lam_t"], writes=["neg_lam"])
    dump("neg_lam", neg_lam, ["neg_lam"])

    wst = [arena[:, i * 7200:(i + 1) * 7200].bitcast(F32) for i in range(3)]
    for kc in range(8):
        sl = kc % 3
        dma("sp", wst[sl], w_in_d[kc * 128:(kc + 1) * 128, :], ("wst", sl), writes=[("A:wst", sl)])
        copy("dve", w_in_sb[:, kc, :], wst[sl], reads=[("A:wst", sl)], writes=[("win", kc)])

    xslot = [0]

    def load_x(tok0):
        s = xslot[0] % NXT
        xslot[0] += 1
        dma("sp", xt[s][:, :], x_d[tok0:tok0 + 128, :], ("xt", s), writes=[("xt", s)])
        return s

    ss_c = smcol("ss_c", 1)
    rstd_c = smcol("rstd_c", 1)
    ss_p = [smcol("ss_p%d" % i, 1) for i in range(3)]
    rstd_p = [smcol("rstd_p%d" % i, 1) for i in range(3)]
    ss4_p = [smcol("ss4_p%d" % i, 4) for i in range(2)]
    rstd4_p = [smcol("rstd4_p%d" % i, 4) for i in range(2)]


    NG = S // 512
    arena_f = arena[:, :].bitcast(F32)
    dummy = smcol("dummy", 1)
    ipb = [0]

    def phase_barrier():
        pg.add("dve", lambda e: e.memset(dummy, 0.0), reads=[], writes=["AR", "dummy"])

    def win_keys(col):
        return [("win", kc) for kc in range(8)]

    evq = [0]

    def evac_copy(out, in_, reads, writes, excl):
        evq[0] += 1
        if evq[0] % 2 == 0:
            act(out, in_, AF.Copy, reads=reads, writes=writes, excl=excl)
        else:
            copy("dve", out, in_, reads=reads, writes=writes, excl=excl)

    ipbanks = [2]
    ipbl = [[0, 1]]

    def inproj_fm_thunks(col0, M, evac):
        def mk(g):
            def f():
                bk = ipbl[0][ipb[0] % min(ipbanks[0], len(ipbl[0]))]
                ipb[0] += 1
                pairs = [(w_in_sb[:, kc, col0:col0 + M], hT[:, kc, g * 512:(g + 1) * 512]) for kc in range(8)]
                mm_group(bank(bk)[0:M, :], pairs, reads=win_keys(col0) + [("hT", g, "d"), ("hT", g, "a")], excl=[("ps", bk)])
                evac(g, bank(bk)[0:M, :], bk)
            return f
        return [mk(g) for g in range(NG)]

    def inproj_fm(col0, M, evac):
        for f in inproj_fm_thunks(col0, M, evac):
            f()

    def inproj_fm_pairs(col0, M, evac):
        def mk(g):
            def mm(bk):
                pairs = [(w_in_sb[:, kc, col0:col0 + M], hT[:, kc, g * 512:(g + 1) * 512]) for kc in range(8)]
                mm_group(bank(bk)[0:M, :], pairs, reads=win_keys(col0) + [("hT", g, "d"), ("hT", g, "a")], excl=[("ps", bk)])
            def ev(bk):
                evac(g, bank(bk)[0:M, :], bk)
            return (mm, ev)
        return [mk(g) for g in range(NG)]

    def inproj_fm_pieces(col0, M, evac, npieces=4):
        out_list = []
        per = 8 // npieces
        for g in range(NG):
            for pc in range(npieces):
                def f(g=g, pc=pc):
                    bk = 0
                    pairs = [(w_in_sb[:, kc, col0:col0 + M], hT[:, kc, g * 512:(g + 1) * 512])
                             for kc in range(pc * per, (pc + 1) * per)]
                    mm_group(bank(bk)[0:M, :], pairs, reads=win_keys(col0) + [("hT", g, "d"), ("hT", g, "a")], excl=[("ps", bk)],
                             first_start=(pc == 0), last_stop=(pc == npieces - 1))
                    if pc == npieces - 1:
                        evac(g, bank(bk)[0:M, :], bk)
                out_list.append(f)
        return out_list

    def inproj_tm_pairs(col0, evac):
        def mk(t):
            def mm(bk):
                pairs = [(hT[:, kc, t * 128:(t + 1) * 128], w_in_sb[:, kc, col0:col0 + 512]) for kc in range(8)]
                mm_group(bank(bk), pairs, reads=win_keys(col0) + [("hT", t // 4, "d"), ("hT", t // 4, "a")], excl=[("ps", bk)])
            def ev(bk):
                evac(t, bank(bk), bk)
            return (mm, ev)
        return [mk(t) for t in range(NT)]

    def inproj_tm_thunks(col0, evac):
        def mk(t):
            def f():
                bk = ipbl[0][ipb[0] % min(ipbanks[0], len(ipbl[0]))]
                ipb[0] += 1
                pairs = [(hT[:, kc, t * 128:(t + 1) * 128], w_in_sb[:, kc, col0:col0 + 512]) for kc in range(8)]
                mm_group(bank(bk), pairs, reads=win_keys(col0) + [("hT", t // 4, "d"), ("hT", t // 4, "a")], excl=[("ps", bk)])
                evac(t, bank(bk), bk)
            return f
        return [mk(t) for t in range(NT)]

    def inproj_tm(col0, evac):
        for f in inproj_tm_thunks(col0, evac):
            f()

    gqT = arena[:, 0:2 * S].rearrange("p (c t) -> p c t", c=2)
    gkT = arena[:, 2 * S:4 * S].rearrange("p (c t) -> p c t", c=2)
    gv = arena[:, 4 * S:8 * S].rearrange("p (t n) -> p t n", n=512)
    grT = arena[0:32, 8 * S:9 * S]
    tb = 9 * S
    tf = tb // 2
    laA = [arena_f[:, tf + i * 512:tf + (i + 1) * 512] for i in range(2)]
    ebA = [arena_f[:, tf + 1024 + i * 512:tf + 1024 + (i + 1) * 512] for i in range(2)]
    enbA = [arena_f[:, tf + 2048 + i * 512:tf + 2048 + (i + 1) * 512] for i in range(2)]
    assert 2 * (tf + 3072) <= 24576
    kinT = [arena[:, tb + i * 256:tb + (i + 1) * 256] for i in range(2)]
    AT = [arena[:, tb + 512 + i * 512:tb + 512 + (i + 1) * 512].rearrange("p (h t) -> p h t", h=4) for i in range(2)]
    on = [arena[:, tb + 1536 + i * 512:tb + 1536 + (i + 1) * 512].rearrange("p (h t) -> p h t", h=4) for i in range(2)]
    S_bf = arena[:, tb + 2560:tb + 2816].rearrange("p (c t) -> p c t", c=2)
    fB = (tb + 2816) // 2
    S32 = arena_f[:, fB:fB + 256]
    Sd = arena_f[:, fB + 256:fB + 512]
    osq = arena_f[:, fB + 512:fB + 1024]
    assert 2 * (fB + 1024) <= 24576
    dec_all = smcol("dec_all", 2 * NT)
    ss4 = smcol("ss4", 4)
    rstd4 = smcol("rstd4", 4)
    QORD = [0, 2, 1, 3]

    def dec_col(t, c):
        i = (t // 2) * 4 + c * 2 + (t % 2)
        return dec_all[:, i:i + 1]

    def gla_begin(b):
        phase_barrier()
        pg.add("pool", lambda e: e.memset(grT, 1.0), reads=[], writes=[("A:gr",)])
        per_g = {g: [] for g in range(NG)}
        def add_fm(col0, M, evac):
            for g, f in enumerate(inproj_fm_pairs(col0, M, evac)):
                per_g[g].append(f)
        add_fm(C_GR, 16, lambda g, p, bk: copy("dve", grT[0:16, g * 512:(g + 1) * 512], p, reads=[], writes=[("A:gr",)], excl=[("ps", bk)]))
        for c in range(2):
            add_fm(C_GQ + c * 128, 128, lambda g, p, bk, c=c: evac_copy(
                gqT[:, c, g * 512:(g + 1) * 512], p, reads=[], writes=[("A:gq", c, g)], excl=[("ps", bk)]))
            add_fm(C_GK + c * 128, 128, lambda g, p, bk, c=c: evac_copy(
                gkT[:, c, g * 512:(g + 1) * 512], p, reads=[], writes=[("A:gk", c, g)], excl=[("ps", bk)]))
        for hc in range(4):
            add_fm(C_GZ + hc * 128, 128, lambda g, p, bk, hc=hc: evac_copy(
                mixT[:, hc, g * 512:(g + 1) * 512], p, reads=[], writes=[("mix", hc, g)], excl=[("ps", bk)]))
        for t, f in enumerate(inproj_tm_pairs(C_GV, lambda t, p, bk: evac_copy(gv[:, t, :], p, reads=[], writes=[("A:gv", t)], excl=[("ps", bk)]))):
            per_g[t // 4].append(f)
        return per_g

    def gla_gate():
        for hc in range(4):
            mk = [("mix", hc, g) for g in range(NG)]
            dst = mixT[:, hc, :]
            act(dst, dst, AF.Silu, reads=mk, writes=mk)
            ts("dve", dst, dst, ggain_sb[:, hc:hc + 1], None, ALU.mult, None, reads=mk + ["ggain"], writes=mk)

    def run_gla(b, iptail):
        ipbanks[0] = 2
        ipbl[0] = [0, 1]
        ip_step, need_group, ipq, pend_ev = iptail
        gv_thunks = []

        def zstage(p):
            pp = p % 2
            zb = 2 + 2 * pp
            def fz(e):
                ins = None
                for j in range(2):
                    tok = slice(p * 256 + j * 128, p * 256 + (j + 1) * 128)
                    ins = e.matmul(bank(zb, 256, j * 256), grT[0:17, tok], wup_bf[0:17, :], start=True, stop=True)
                return ins
            pg.add("pe", fz, reads=[("A:gr",), "wup_bf"], excl=[("ps", zb)])
            act(laA[pp], bank(zb), AF.Exp, reads=[], writes=[("A:la", pp)], excl=[("ps", zb)], scale=-1.0)
            act(laA[pp], laA[pp], AF.Ln, reads=[("A:la", pp)], writes=[("A:la", pp)], bias=1.0)

        def cstage(p):
            pp = p % 2
            g = p // 2
            bb = 3 + 2 * pp
            tk = slice(p * 256, (p + 1) * 256)
            def fcs(e):
                ins = None
                for c in range(2):
                    for j in range(2):
                        ins = e.matmul(bank(bb, 128, (c * 2 + j) * 128), laA[pp][:, j * 256 + c * 128:j * 256 + (c + 1) * 128],
                                       csb["triI_f"][:, :], start=True, stop=True)
                return ins
            pg.add("pe", fcs, reads=[("A:la", pp), "triI_f"], excl=[("ps", bb)])
            act(ebA[pp], bank(bb), AF.Exp, reads=[], writes=[("A:eb", pp)], excl=[("ps", bb)], bias=math.log(0.125))
            act(enbA[pp], bank(bb), AF.Exp, reads=[], writes=[("A:enb", pp)], excl=[("ps", bb)], scale=-1.0)
            act(dec_all[:, p * 4:(p + 1) * 4].unsqueeze(2), bank(bb).rearrange("p (i t) -> p i t", i=4)[:, :, 127:128], AF.Exp,
                reads=[], writes=[("dec", p)], excl=[("ps", bb)])
            tt("pool", gqT[:, :, tk], gqT[:, :, tk], ebA[pp].rearrange("p (c t) -> p c t", c=2), ALU.mult,
               reads=[("A:gq", 0, g), ("A:gq", 1, g), ("A:eb", pp)], writes=[("A:qin", p)])
            tt("pool", gkT[:, :, tk], gkT[:, :, tk], enbA[pp].rearrange("p (c t) -> p c t", c=2), ALU.mult,
               reads=[("A:gk", 0, g), ("A:gk", 1, g), ("A:enb", pp)], writes=[("A:kin", p)])

        NP = NT // 2
        need_group(0)
        zstage(0)
        for p in range(NP):
            if p + 1 < NP:
                need_group((p + 1) // 2)
                zstage(p + 1)
            if ipq or pend_ev:
                ip_step(2)
                ip_step(2)
            cstage(p)
        while ipq or pend_ev:
            ip_step(2)
        gla_gate()
        pg.add("dve", lambda e: e.memset(S32, 0.0),
               reads=[("A:la", 0), ("A:la", 1), ("A:eb", 0), ("A:eb", 1), ("A:enb", 0), ("A:enb", 1)],
               writes=[("A:S32",), ("A:la", 0), ("A:la", 1), ("A:eb", 0), ("A:eb", 1), ("A:enb", 0), ("A:enb", 1)])
        pg.add("pool", lambda e: e.memset(Sd, 0.0), reads=[("A:S32",)], writes=[("A:Sd",)])
        bkey = [("A:la", 0)]

        def pre(t):
            tok = slice(t * 128, (t + 1) * 128)
            p = t // 2
            par = t % 2
            ktp = bank_bf(1)[:, 0:256]
            def fkt(e):
                e.transpose(ktp[:, 0:128], gkT[:, 0, tok], csb["ident_bf"][:, :])
                return e.transpose(ktp[:, 128:256], gkT[:, 1, tok], csb["ident_bf"][:, :])
            pg.add("pe", fkt, reads=[("A:kin", p), "ident_bf"], excl=[("ps", 1)])
            act(kinT[par], ktp, AF.Copy, reads=bkey, writes=[("A:kinT", par)], excl=[("ps", 1)])
            def fsc(e):
                ins = None
                for h in range(4):
                    c = h // 2
                    rows = slice(0, 64) if h % 2 == 0 else slice(64, 128)
                    ins = e.matmul(bank(2 + h % 2, 128, c * 128), gkT[rows, c, tok], gqT[rows, c, tok], start=True, stop=True)
                return ins
            pg.add("pe", fsc, reads=[("A:kin", p), ("A:qin", p)], excl=[("ps", 2), ("ps", 3)])
            for q in range(2):
                tt("dve", AT[par][:, q::2, :], bank(2 + q, 256).rearrange("p (c t) -> p c t", c=2),
                   csb["maskT_bf"][:, :].unsqueeze(1).to_broadcast([128, 2, 128]), ALU.mult,
                   reads=["maskT_bf"] + bkey, writes=[("A:AT", par, q)], excl=[("ps", 2 + q)])

        def obanks(t):
            return (6, 7) if t % 2 == 0 else (4, 5)

        def main_o(t):
            tok = slice(t * 128, (t + 1) * 128)
            p = t // 2
            par = t % 2
            ob = obanks(t)
            def fo_(e):
                ins = None
                for h in range(4):
                    ins = e.matmul(bank(ob[h % 2], 128, (h // 2) * 128), AT[par][:, h, :], gv[:, t, h * 128:(h + 1) * 128],
                                   start=(h < 2), stop=(t == 0), skip_group_check=True)
                if t > 0:
                    for h in range(4):
                        c = h // 2
                        rows = slice(0, 64) if h % 2 == 0 else slice(64, 128)
                        ins = e.matmul(bank(ob[h % 2], 128, c * 128), gqT[rows, c, tok], S_bf[rows, c, :],
                                       start=False, stop=True, skip_group_check=True)
                return ins
            pg.add("pe", fo_, reads=[("A:AT", par, 0), ("A:AT", par, 1), ("A:gv", t), ("A:qin", p), ("A:Sbf",)],
                   excl=[("ps", ob[0]), ("ps", ob[1])])
            ssk = "ss4_p%d" % par
            for q in range(4):
                act(osq[:, q * 128:(q + 1) * 128], bank(ob[q // 2], 128, (q % 2) * 128), AF.Square, reads=bkey,
                    writes=[("A:osq", q), (ssk, q)], excl=[("ps", ob[q // 2])], accum=ss4_p[par][:, q:q + 1])

        def main_kv(t):
            p = t // 2
            par = t % 2
            if t < NT - 1:
                def fkv(e):
                    ins = None
                    for h in range(4):
                        c = h // 2
                        rows = slice(0, 64) if h % 2 == 0 else slice(64, 128)
                        ins = e.matmul(bank(0)[rows, c * 128:(c + 1) * 128], kinT[par][:, h * 64:(h + 1) * 64],
                                       gv[:, t, h * 128:(h + 1) * 128], start=True, stop=True)
                    return ins
                pg.add("pe", fkv, reads=[("A:kinT", par), ("A:gv", t)], excl=[("ps", 0)])
                for c in range(2):
                    cs = slice(c * 128, (c + 1) * 128)
                    stt(S32[:, cs], bank(0, 128, c * 128), dec_col(t, c), Sd[:, cs], ALU.mult, ALU.add,
                        reads=[("A:Sd",), ("dec", p)], writes=[("A:S32",)], excl=[("ps", 0)])
                copy("dve", S_bf, S32.rearrange("p (c t) -> p c t", c=2), reads=[("A:S32",)], writes=[("A:Sbf",)])
                if t + 1 < NT - 1:
                    for c in range(2):
                        cs = slice(c * 128, (c + 1) * 128)
                        ts("dve", Sd[:, cs], S32[:, cs], dec_col(t + 1, c), None, ALU.mult, None,
                           reads=[("A:S32",), ("dec", (t + 1) // 2)], writes=[("A:Sd",)])

        def post(t):
            tok = slice(t * 128, (t + 1) * 128)
            g = t // 4
            par = t % 2
            ob = obanks(t)
            ssk, rsk = "ss4_p%d" % par, "rstd4_p%d" % par
            act(ss4_p[par], ss4_p[par], AF.Ln, reads=[(ssk, q) for q in range(4)] + ["eps_col"], writes=[ssk],
                scale=1.0 / 128, bias=eps_col)
            act(rstd4_p[par], ss4_p[par], AF.Exp, reads=[ssk], writes=[rsk], scale=-0.5)
            for q in range(2):
                tt("dve", on[par][:, q * 2:(q + 1) * 2, :], bank(ob[q], 256).rearrange("p (c t) -> p c t", c=2),
                   rstd4_p[par][:, q * 2:(q + 1) * 2].unsqueeze(2).to_broadcast([128, 2, 128]), ALU.mult,
                   reads=[rsk] + bkey, writes=[("A:on", par, q)], excl=[("ps", ob[q])])
            tp = bank_bf(1)[:, 256:768]
            def ftp(e):
                ins = None
                for q in range(4):
                    h = QORD[q]
                    ins = e.transpose(tp[:, h * 128:(h + 1) * 128], on[par][:, q, :], csb["ident_bf"][:, :])
                return ins
            pg.add("pe", ftp, reads=[("A:on", par, 0), ("A:on", par, 1), "ident_bf"], excl=[("ps", 1)])
            tt("dve", mixT[:, 0:4, tok], tp.rearrange("p (h t) -> p h t", h=4), mixT[:, 0:4, tok], ALU.mult,
               reads=[("mix", hc, g) for hc in range(4)], writes=[("mix", hc, g) for hc in range(4)], excl=[("ps", 1)])

        pre(0)
        for t in range(NT):
            main_o(t)
            main_kv(t)
            if t + 1 < NT:
                pre(t + 1)
            post(t)

    dv_aug = arena[:, 0:NT * 520].rearrange("p (t h n) -> p t h n", h=4, n=130)
    do = NT * 520
    dqT = [arena[:, do + i * S:do + (i + 1) * S] for i in range(2)]
    dkT = [arena[:, do + (2 + i) * S:do + (3 + i) * S] for i in range(2)]
    po = do + 4 * S
    PT = [[arena[:, po + (2 * i + m) * 512:po + (2 * i + m + 1) * 512] for m in range(2)] for i in range(2)]
    fo2 = (po + 2048) // 2
    Oc = arena_f[:, fo2:fo2 + 8 * 129].rearrange("p (i n) -> p i n", n=129)
    Dm = arena_f[:, fo2 + 1032:fo2 + 1544].rearrange("p (s n) -> p s n", s=4)
    t2 = arena_f[:, fo2 + 1544:fo2 + 2056].rearrange("p (s n) -> p s n", s=4)
    dno = 2 * (fo2 + 2056)
    Dn = arena[:, dno:dno + 512].rearrange("p (s n) -> p s n", s=4)
    assert dno + 512 <= 24576
    rz = smcol("rz", 8)
    SBANKS = [(2, 3), (7, 1)]
    astep = [0]

    def obank(i):
        return 4 + i // 3, (i % 3) * 129

    def attention(b, h, buf, hook=None, fin_q=None, flush=True):
        qsub = 256 if SLOPES[h] * 511 > 40 else 512
        steps = [(G, kb) for G in range(NG) for kb in range(4 * G + 4)]
        started = {}
        pend = []

        def do_qk_exp(G, kb):
            t = kb - 4 * G
            s0 = max(t, 0)
            c0 = 128 * s0
            ks = slice(kb * 128, (kb + 1) * 128)
            qs = slice(G * 512 + c0, (G + 1) * 512)
            pb = astep[0] % 2
            astep[0] += 1
            sb0, sb1 = SBANKS[pb]
            def fqk(e):
                e.matmul(bank(sb0)[:, c0:512], dkT[buf][0:64, ks], dqT[buf][0:64, qs], start=True, stop=True)
                return e.matmul(bank(sb1)[:, c0:512], dkT[buf][64:128, ks], dqT[buf][64:128, qs], start=True, stop=True)
            pg.add("pe", fqk, reads=[("A:dk", buf, kb // 4), ("A:dq", buf, G)], excl=[("ps", sb0), ("ps", sb1)])
            for m in range(2):
                sbm = (sb0, sb1)[m]
                if qsub == 512:
                    chunks = [(c0, 512)]
                else:
                    chunks = [(max(c0, lo), lo + qsub) for lo in range(0, 512, qsub) if c0 < lo + qsub]
                for (a, bnd) in chunks:
                    delta = 4 * G + (bnd - 1) // 128 - kb
                    act(PT[pb][m][:, a:bnd], bank(sbm)[:, a:bnd], AF.Exp, reads=["bias_tab"], writes=[("A:PT", pb, m)],
                        excl=[("ps", sbm)], bias=csb["bias_tab"][:, h * 16 + delta:h * 16 + delta + 1], scale=0.125)
                if t >= 0:
                    tt("dve", PT[pb][m][:, c0:c0 + 128], PT[pb][m][:, c0:c0 + 128], csb["maskT_bf"][:, :], ALU.mult,
                       reads=[("A:PT", pb, m), "maskT_bf"], writes=[("A:PT", pb, m)])
            return pb, s0

        def do_pv(G, kb, pb, s0):
            st_set = started.setdefault(G, set())
            plan = []
            for m in range(2):
                for s in range(s0, 4):
                    bk, off = obank(4 * m + s)
                    plan.append((bk, off, m, s, bk not in st_set, kb == 4 * G + s))
                    st_set.add(bk)
            def fpv(e):
                ins = None
                for (bk, off, m, s, st, sp) in plan:
                    ins = e.matmul(bank(bk)[:, off:off + 129], PT[pb][m][:, s * 128:(s + 1) * 128], dv_aug[:, kb, h, 0:129],
                                   start=st, stop=sp, skip_group_check=True)
                return ins
            pg.add("pe", fpv, reads=[("A:PT", pb, 0), ("A:PT", pb, 1), ("A:dv", kb)], excl=[("ps", 4), ("ps", 5), ("ps", 6)])
            if kb == 4 * G + 3:
                finalize(G)

        if fin_q is None:
            fin_q = []

        def finalize(G):
            while any(f_ is not None and getattr(f_, "is_f2", False) for f_ in fin_q):
                f_ = fin_q.pop(0)
                if f_ is not None:
                    f_()
            copy("dve", Oc[:, 0:3, :], bank(4, 387).rearrange("p (i n) -> p i n", n=129), reads=[], writes=[("A:Oc", 0)], excl=[("ps", 4)])
            copy("dve", Oc[:, 3:6, :], bank(5, 387).rearrange("p (i n) -> p i n", n=129), reads=[], writes=[("A:Oc", 1)], excl=[("ps", 5)])
            copy("dve", Oc[:, 6:8, :], bank(6, 258).rearrange("p (i n) -> p i n", n=129), reads=[], writes=[("A:Oc", 2)], excl=[("ps", 6)])
            ock = [("A:Oc", i) for i in range(3)]

            def F2():
                pg.add("dve", lambda e: e.reciprocal(out=rz.unsqueeze(2), in_=Oc[:, :, 128:129]), reads=ock, writes=["rz"])
                ts("dve", rz[:, 4:8], rz[:, 4:8], neg_lam, None, ALU.mult, None, reads=["rz", "neg_lam"], writes=["rz"])
                tt("dve", t2, Oc[:, 4:8, 0:128], rz[:, 4:8].unsqueeze(2).to_broadcast([128, 4, 128]), ALU.mult,
                   reads=ock + ["rz"], writes=[("A:t2",)])
                tt("dve", Dm, Oc[:, 0:4, 0:128], rz[:, 0:4].unsqueeze(2).to_broadcast([128, 4, 128]), ALU.mult,
                   reads=ock + ["rz"], writes=[("A:Dm",)])
                tt("pool", Dm, Dm, t2, ALU.add, reads=[("A:Dm",), ("A:t2",)], writes=[("A:Dm",)])

            def F3():
                tt("dve", t2, Dm, Dm, ALU.mult, reads=[("A:Dm",)], writes=[("A:t2",)])
                pg.add("dve", lambda e: e.tensor_reduce(out=ss4, in_=t2, axis=AX.X, op=ALU.add), reads=[("A:t2",)], writes=["ss4"])

            def F4():
                rstd_from_ss(ss4, rstd4, 128, "ss4", "rstd4")

            def F5():
                tt("dve", Dn, Dm, rstd4.unsqueeze(2).to_broadcast([128, 4, 128]), ALU.mult, reads=[("A:Dm",), "rstd4"], writes=[("A:Dn",)])
                tbk = SBANKS[astep[0] % 2][0]
                tp = bank_bf(tbk)[:, 0:512]
                def ftp(e):
                    ins = None
                    for s_ in range(4):
                        ins = e.transpose(tp[:, s_ * 128:(s_ + 1) * 128], Dn[:, s_, :], csb["ident_bf"][:, :])
                    return ins
                pg.add("pe", ftp, reads=[("A:Dn",), "ident_bf"], excl=[("ps", tbk)])

                dst = mixT[:, 4 + h, G * 512:(G + 1) * 512]
                tt("dve", dst, tp, dst, ALU.mult, reads=[("mix", 4 + h, G)], writes=[("mix", 4 + h, G)], excl=[("ps", tbk)])

            F2.is_f2 = True
            fin_q.extend([F2, None, F3, None, F4, None, None, F5])

        for (G, kb) in steps:
            pb, s0 = do_qk_exp(G, kb)
            if pend:
                do_pv(*pend.pop(0))
            pend.append((G, kb, pb, s0))
            if fin_q:
                f_ = fin_q.pop(0)
                if f_ is not None:
                    f_()
            if hook is not None:
                hook()
        while pend:
            do_pv(*pend.pop(0))
        while flush and fin_q:
            f_ = fin_q.pop(0)
            if f_ is not None:
                f_()

    def head_inproj_thunks(h, pieces=False):
        buf = h % 2
        mk = inproj_fm_pieces if pieces else inproj_fm_thunks
        th = []
        th += mk(C_DZ + h * 128, 128, lambda g, p, bk: copy(
            "dve", mixT[:, 4 + h, g * 512:(g + 1) * 512], p, reads=[], writes=[("mix", 4 + h, g)], excl=[("ps", bk)]))
        th += mk(C_DQ + h * 128, 128, lambda g, p, bk: copy(
            "dve", dqT[buf][:, g * 512:(g + 1) * 512], p, reads=[], writes=[("A:dq", buf, g)], excl=[("ps", bk)]))
        th += mk(C_DK + h * 128, 128, lambda g, p, bk: copy(
            "dve", dkT[buf][:, g * 512:(g + 1) * 512], p, reads=[], writes=[("A:dk", buf, g)], excl=[("ps", bk)]))
        return th

    def head_gate(h):
        mk = [("mix", 4 + h, g) for g in range(NG)]
        dst = mixT[:, 4 + h, :]
        act(dst, dst, AF.Silu, reads=mk, writes=mk)
        ts("dve", dst, dst, dgain_sb[:, h:h + 1], 1.0 - LAM_INIT, ALU.mult, ALU.mult, reads=mk + ["dgain"], writes=mk)

    def run_diff(b):
        phase_barrier()
        ipbanks[0] = 2
        pg.add("pool", lambda e: e.memset(dv_aug[:, :, :, 128:129], 1.0), reads=[], writes=[("A:dvones",)])
        dv_th = inproj_tm_thunks(C_DV, lambda t, p, bk: copy("dve", dv_aug[:, t, :, 0:128], p.rearrange("p (h n) -> p h n", h=4),
                                                             reads=[("A:dvones",)], writes=[("A:dv", t)], excl=[("ps", bk)]))
        for f in head_inproj_thunks(0):
            f()
        n_pre = min(4, NT)
        for f in dv_th[:n_pre]:
            f()
        dv_late = dv_th[n_pre:]
        ipbanks[0] = 1
        fq = []
        for h in range(4):
            buf = h % 2
            head_gate(h)
            nxt = head_inproj_thunks(h + 1, pieces=True) if h < 3 else []
            if h == 0 and dv_late:
                merged = []
                while nxt or dv_late:
                    if dv_late:
                        merged.append((1, dv_late.pop(0)))
                    for _ in range(4):
                        if nxt:
                            merged.append((0, nxt.pop(0)))
                nxt = merged
            elif h == 3:
                nxt = [(0, f) for f in gate_thunks(b)]
            else:
                nxt = [(0, f) for f in nxt]
            def hook():
                budget = 2
                while nxt and budget > 0:
                    kind, f = nxt.pop(0)
                    f()
                    if kind == 0:
                        budget -= 1
            attention(b, h, buf, hook, fin_q=fq, flush=(h == 3))
            while nxt:
                nxt.pop(0)[1]()
            if h == 2:
                load_w_out()

    yslot = [0]

    wo_stage = Rbuf[:, 8 * D:16 * D].bitcast(F32).rearrange("p (k n) -> p k n", k=4)

    def load_w_out():
        hkeys = [("hT", g, e_) for g in range(NG) for e_ in ("d", "a")]
        for half in range(2):
            dma("sp", wo_stage, w_out_d[half * 512:(half + 1) * 512, :].rearrange("(k p) n -> p k n", p=128), ("wost", 0),
                writes=hkeys + ["wost"])
            copy("dve", w_out_sb[:, half * 4:(half + 1) * 4, :], wo_stage, reads=["wost"], writes=hkeys + [("wout", half)])

    def gate_thunks(b):
        def mk(kc):
            def f():
                ts("dve", diagg[:, :], csb["ident_f"][:, :], gcol[:, kc * 2 + b:kc * 2 + b + 1], None, ALU.mult, None,
                   reads=["ident_f", "gcol"], writes=["diagg"])
                mm_group(bank(0, 128, (kc % 4) * 128), [(csb["ones_f"][:, :], diagg[:, :])], reads=["ones_f", "diagg"], excl=[("ps", 0)])
                if kc % 4 == 3:
                    copy("dve", gate_bc[:, (kc // 4) * 512:(kc // 4 + 1) * 512], bank(0), reads=[], writes=["gate_bc"], excl=[("ps", 0)])
            return f
        return [mk(kc) for kc in range(8)]

    def run_out(b):
        hkeys = [("hT", g, e_) for g in range(NG) for e_ in ("d", "a")]
        wkeys = [("wout", 0), ("wout", 1)] + hkeys

        ybl = [ybuf[0], ybuf[1], ybuf3]
        ykl = [[("y", 0, 0), ("y", 0, 1)], [("y", 1, 0), ("y", 1, 1)], [("xn", 0), ("xn", 1)]]

        def out_stage1(t, s):
            tok = slice(t * 128, (t + 1) * 128)
            par = t % 2
            yb = ybl[t % 3]
            yk = ykl[t % 3]
            for half in range(2):
                bk = 2 * par + half
                pairs = [(mixT[:, kc, tok], w_out_sb[:, kc, half * 512:(half + 1) * 512]) for kc in range(8)]
                mm_group(bank(bk), pairs, reads=[("mix", kc, t // 4) for kc in range(8)] + wkeys, excl=[("ps", bk)])
                tt("dve", yb[:, half * 512:(half + 1) * 512], bank(bk), gate_bc[:, half * 512:(half + 1) * 512], ALU.mult,
                   reads=["gate_bc"], writes=[yk[half]], excl=[("ps", bk)])
            tt("pool", yb[:, :], yb[:, :], xt[s][:, :], ALU.add, reads=yk + [("xt", s)], writes=yk)

        def out_stage1c(t):
            i3 = t % 3
            yb = ybl[i3]
            yk = ykl[i3]
            ssk, rsk = "ss_p%d" % i3, "rstd_p%d" % i3
            jb = 4
            act(ps[:, jb * 512:jb * 512 + 1024], yb[:, :], AF.Square, reads=yk, writes=[ssk],
                excl=[("ps", jb), ("ps", jb + 1)], accum=ss_p[i3])
            rstd_from_ss(ss_p[i3], rstd_p[i3], D, ssk, rsk)

        def out_stage2(t):
            tok0 = b * S + t * 128
            i3 = t % 3
            yb = ybl[i3]
            yk = ykl[i3]
            stt(yb[:, :], yb[:, :], rstd_p[i3], fgain_sb[:, :], ALU.mult, ALU.mult, reads=yk + ["rstd_p%d" % i3, "fgain"], writes=yk)
            dma("act", y_d[tok0:tok0 + 128, :], yb[:, :], ("yout", i3), reads=yk)

        slots = {0: load_x(b * S)}
        for t in range(NT + 1):
            if t < NT:
                if t + 1 < NT:
                    slots[t + 1] = load_x(b * S + (t + 1) * 128)
                out_stage1(t, slots.pop(t))
            if t >= 1:
                out_stage2(t - 1)
            if t < NT:
                out_stage1c(t)

    for b in range(NSEQ):
        if upto == "mod":
            break
        DVE_KC = [0, 2, 3, 4, 6, 7]
        ACT_KC = [1, 5]

        def ht_stage1a(t, s):
            par = t % 2
            ssk, rsk = "ss_p%d" % par, "rstd_p%d" % par
            act(junk[:, :], xt[s][:, :], AF.Square, reads=[("xt", s)], writes=[("y", 1, 0), ("y", 1, 1), ssk], accum=ss_p[par])
            rstd_from_ss(ss_p[par], rstd_p[par], D, ssk, rsk)

        def ht_stage1b(t, s):
            par = t % 2
            xn = xnb[par][:, :]
            xk = ("xn", par)
            rsk = "rstd_p%d" % par
            ts("dve", xn, xt[s][:, :], rstd_p[par], None, ALU.mult, None, reads=[("xt", s), rsk], writes=[xk])
            bD, bA = 1 + 2 * par, 2 + 2 * par
            def ftr(e):
                ins = None
                for i, kc in enumerate(DVE_KC):
                    ins = e.transpose(bank_bf(bD)[:, i * 128:(i + 1) * 128], xn[:, kc * 128:(kc + 1) * 128], csb["ident_bf"][:, :])
                for i, kc in enumerate(ACT_KC):
                    ins = e.transpose(bank_bf(bA)[:, i * 128:(i + 1) * 128], xn[:, kc * 128:(kc + 1) * 128], csb["ident_bf"][:, :])
                return ins
            pg.add("pe", ftr, reads=[xk, "ident_bf"], excl=[("ps", bD), ("ps", bA)])

        def ht_stage2(t):
            par = t % 2
            bD, bA = 1 + 2 * par, 2 + 2 * par
            for i, kc in enumerate(DVE_KC):
                ts("dve", hT[:, kc, t * 128:(t + 1) * 128], bank_bf(bD)[:, i * 128:(i + 1) * 128],
                   acol[:, kc * 2 + b:kc * 2 + b + 1], scol[:, kc * 2 + b:kc * 2 + b + 1], ALU.mult, ALU.add,
                   reads=["acol", "scol"], writes=[("hT", t // 4, "d")], excl=[("ps", bD)])
            for i, kc in enumerate(ACT_KC):
                act(hT[:, kc, t * 128:(t + 1) * 128], bank_bf(bA)[:, i * 128:(i + 1) * 128], AF.Identity,
                    reads=["acol", "scol"], writes=[("hT", t // 4, "a")], excl=[("ps", bA)],
                    bias=scol[:, kc * 2 + b:kc * 2 + b + 1], scale=acol[:, kc * 2 + b:kc * 2 + b + 1])

        per_g = gla_begin(b)
        ipq = []
        pend_ev = []
        banksets = [[6, 7], [0, 5]]
        it = [0]

        ev_done = [0]

        def ip_step(n):
            while pend_ev:
                ev, bk = pend_ev.pop(0)
                ev(bk)
                ev_done[0] += 1
            for j in range(n):
                if ipq:
                    mm, ev = ipq.pop(0)
                    bk = banksets[it[0] % 2][j]
                    mm(bk)
                    pend_ev.append((ev, bk))
            it[0] += 1

        slots = {0: load_x(b * S)}
        for t in range(NT + 1):
            if t < NT:
                if t + 1 < NT:
                    slots[t + 1] = load_x(b * S + (t + 1) * 128)
                ht_stage1a(t, slots[t])
            if t >= 1:
                ht_stage2(t - 1)
                if (t - 1) % 4 == 3:
                    ipq.extend(per_g[(t - 1) // 4])
            if t < NT:
                ht_stage1b(t, slots.pop(t))
            ip_step(2)
        banksets[1] = [0, 1]
        n_per_group = len(per_g[0])

        def need_group(g):
            while ev_done[0] < n_per_group * (g + 1) and (ipq or pend_ev):
                ip_step(2)
        iptail = (ip_step, need_group, ipq, pend_ev)
        if b == 0:
            for kc in range(8):
                dump("hT%d" % kc, hT[:, kc, :], [("hT", g, e_) for g in range(4) for e_ in ("d", "a")])
        if upto == "hT":
            break
        run_gla(b, iptail)
        if b == 0:
            for hc in range(4):
                dump("mixg%d" % hc, mixT[:, hc, :], [("mix", hc, g) for g in range(NG)])
        if upto == "gla":
            break
        run_diff(b)
        if b == 0:
            for hc in range(4):
                dump("mixd%d" % hc, mixT[:, 4 + hc, :], [("mix", 4 + hc, g) for g in range(NG)])
        if upto == "diff":
            break
        run_out(b)

    pg.finalize(nc, stack)
    tail = [(pg.sems[(("dma", k), 0)], 16 * pg.dma_count[k]) for k in pg.dma_count if k[0] in ("dump", "yout")]
    last = Op("sp", lambda e: e.nop(), len(pg.ops), None)
    last.waits = tail
    last.signal = None
    pg.ops.append(last)
    pg.emit_all(nc)
    return nc, stack


def make_in_maps(x, c, w_ada, b_ada, norm_gain, w_in, w_gla_gate_up, b_gla_gate, gla_out_gain,
                 lambda_q1, lambda_k1, lambda_q2, lambda_k2, diff_out_gain, w_out, final_gain):
    f = lambda a: np.ascontiguousarray(np.asarray(a, dtype=np.float32))
    x = f(x); c = f(c)
    consts = _consts()
    shared = {
        "w_ada": f(w_ada[0]),
        "b_adaT": f(np.asarray(b_ada[0]).reshape(24, 128).T),
        "ngainT": f(np.asarray(norm_gain[0]).reshape(8, 128).T),
        "w_in": f(w_in[0]),
        "wup_aug": f(np.concatenate([np.asarray(w_gla_gate_up[0]), np.asarray(b_gla_gate[0])[None, :]], axis=0)),
        "ggainT": f(np.asarray(gla_out_gain[0]).reshape(4, 128).T),
        "dgainT": f(np.asarray(diff_out_gain[0]).reshape(4, 128).T),
        "lam_bc": f(np.broadcast_to(np.concatenate([np.asarray(lambda_q1[0]), np.asarray(lambda_k1[0]),
                                                    np.asarray(lambda_q2[0]), np.asarray(lambda_k2[0])])[None, :], (128, 256))),
        "w_out": f(w_out[0]),
        "fgain_bc": f(np.broadcast_to(np.asarray(final_gain)[None, :], (128, D))),
    }
    shared.update(consts)
    in_maps = []
    for i in range(NCORES):
        m = dict(shared)
        m["x"] = np.ascontiguousarray(x[2 * i:2 * i + 2].reshape(NSEQ * S, D))
        m["cT"] = np.ascontiguousarray(c[2 * i:2 * i + 2].reshape(2, 8, 128).transpose(2, 1, 0).reshape(128, 16))
        in_maps.append(m)
    return in_maps


def kernel(**inputs):
    in_maps = make_in_maps(**inputs)
    nc, stack = build_program()
    with stack:
        res = run_bass_kernel_spmd(nc, in_maps, core_ids=list(range(NCORES)))
    outs = [np.asarray(r["y"], dtype=np.float32).reshape(NSEQ, S, D) for r in res.results]
    return np.concatenate(outs, axis=0)
```
